# Optimizing a Trainium2 kernel written in Bass

```python
import math
import jax
import jax.numpy as jnp
from jax import lax
import numpy as np

D_MODEL = 2048
BATCH = 4
SEQ = 2048
DEPTH = 2

CTX_LEN = 256
GRID_W = 64

N_BRANCH = 4
BRANCH_WIDTH = D_MODEL // 2

ROPE_DIM = 64
ROPE_BASE = 10000.0

SSD_WIDTH = BRANCH_WIDTH
SSD_HEAD_DIM = 64
SSD_HEADS = SSD_WIDTH // SSD_HEAD_DIM
SSD_GROUPS = 2
SSD_STATE = 128
SSD_CHUNK = 128
SSD_CONV = 3
SSD_CONV_CH = SSD_WIDTH + 2 * SSD_GROUPS * SSD_STATE

MLA_HEADS = 8
MLA_NOPE = 128
MLA_ROPE = ROPE_DIM
MLA_V = 128
MLA_Q_LORA = 512
MLA_KV_LORA = 256
MLA_WIDTH = MLA_HEADS * MLA_V
Q_BLOCK = 128

GQA_HEADS = 16
GQA_KV_HEADS = 2
GQA_HEAD_DIM = ROPE_DIM
GQA_WIDTH = GQA_HEADS * GQA_HEAD_DIM
GQA_KV_WIDTH = GQA_KV_HEADS * GQA_HEAD_DIM
WINDOW = 128
WIN_BLOCK = 128

HY_WIDTH = BRANCH_WIDTH
HY_CONV = 3
HY_POS_EMB = 33
HY_BANDS = (HY_POS_EMB - 1) // 2
HY_FILTER_HIDDEN = 64
HY_FAST_DECAY = 0.3
HY_SLOW_DECAY = 1.5
HY_DECAY_TARGET = 0.01

LN_EPS = 1e-6
NEG_INF = -1e30
DEEPNORM_ALPHA = (2 * DEPTH) ** 0.25
DEEPNORM_BETA = (8 * DEPTH) ** -0.25

IN_SPLITS = (
    SSD_WIDTH,
    SSD_CONV_CH,
    2 * SSD_HEADS,
    MLA_Q_LORA,
    MLA_KV_LORA,
    MLA_ROPE,
    MLA_WIDTH,
    GQA_WIDTH,
    GQA_KV_WIDTH,
    GQA_KV_WIDTH,
    GQA_WIDTH,
    3 * HY_WIDTH,
    HY_WIDTH,
    N_BRANCH * D_MODEL,
)
IN_WIDTH = sum(IN_SPLITS)

kernel_name = 'hybrid_ssd_mla_swa_hyena_prefix_dit'


def _split_cols(t):
    out, start = [], 0
    for size in IN_SPLITS:
        out.append(t[..., start:start + size])
        start += size
    return out


def _layernorm(t):
    tf = t.astype(jnp.float32)
    mu = jnp.mean(tf, axis=-1, keepdims=True)
    var = jnp.mean(jnp.square(tf - mu), axis=-1, keepdims=True)
    return ((tf - mu) * lax.rsqrt(var + LN_EPS)).astype(t.dtype)


def _rmsnorm(t, w):
    tf = t.astype(jnp.float32)
    out = tf * lax.rsqrt(jnp.mean(jnp.square(tf), axis=-1, keepdims=True) + LN_EPS)
    return (out * w.astype(jnp.float32)).astype(t.dtype)


def _dwconv_centred(u, w, b):
    k = w.shape[0]
    y = lax.conv_general_dilated(u, w[:, None, :].astype(u.dtype), window_strides=(1,),
                                 padding=[(k // 2, k // 2)], dimension_numbers=('NWC', 'WIO', 'NWC'),
                                 feature_group_count=u.shape[-1])
    return y + b


def _axial_rope_tables(n_tok, rot_dim):
    rows = n_tok // GRID_W
    row = jnp.repeat(jnp.arange(rows, dtype=jnp.float32), GRID_W)
    col = jnp.tile(jnp.arange(GRID_W, dtype=jnp.float32), rows)
    n_freq = rot_dim // 4
    inv_freq = ROPE_BASE ** (-jnp.arange(n_freq, dtype=jnp.float32) / n_freq)
    ang = jnp.concatenate([row[:, None] * inv_freq[None, :], col[:, None] * inv_freq[None, :]], axis=-1)
    return jnp.cos(ang), jnp.sin(ang)


def _apply_rope(t, cos, sin):
    half = t.shape[-1] // 2
    shape = (t.shape[1],) + (1,) * (t.ndim - 3) + (half,)
    cos = cos.reshape(shape).astype(t.dtype)
    sin = sin.reshape(shape).astype(t.dtype)
    t1, t2 = t[..., :half], t[..., half:]
    return jnp.concatenate([t1 * cos - t2 * sin, t1 * sin + t2 * cos], axis=-1)


def _segsum(a):
    t = a.shape[-1]
    rep = jnp.broadcast_to(a[..., :, None], a.shape + (t,))
    rep = jnp.where(jnp.tril(jnp.ones((t, t), dtype=bool), -1), rep, 0.0)
    cs = jnp.cumsum(rep, axis=-2)
    return jnp.where(jnp.tril(jnp.ones((t, t), dtype=bool), 0), cs, -jnp.inf)


def _ssd_chunked(xs, dt, a, bm, cm, init_state, with_y):
    bsz, n, nh, hp = xs.shape
    nc = n // SSD_CHUNK
    rep = nh // bm.shape[2]
    bh = jnp.repeat(bm, rep, axis=2).reshape(bsz, nc, SSD_CHUNK, nh, -1)
    xd = (xs * dt[..., None]).reshape(bsz, nc, SSD_CHUNK, nh, hp)
    la = (dt * a).reshape(bsz, nc, SSD_CHUNK, nh).transpose(0, 3, 1, 2)
    la_cum = jnp.cumsum(la, axis=-1)
    decay_to_end = jnp.exp(la_cum[..., -1:] - la_cum)
    local = jnp.einsum('bclhn,bhcl,bclhp->bchpn', bh, decay_to_end, xd)
    states = jnp.concatenate([init_state[:, None], local], axis=1)
    chunk_decay = jnp.exp(_segsum(jnp.pad(la_cum[..., -1], ((0, 0), (0, 0), (1, 0)))))
    states = jnp.einsum('bhzc,bchpn->bzhpn', chunk_decay, states)
    final_state = states[:, -1]
    if not with_y:
        return None, final_state
    ch = jnp.repeat(cm, rep, axis=2).reshape(bsz, nc, SSD_CHUNK, nh, -1)
    scores = jnp.einsum('bclhn,bcshn->bhcls', ch, bh) * jnp.exp(_segsum(la))
    y_diag = jnp.einsum('bhcls,bcshp->bclhp', scores, xd)
    y_off = jnp.einsum('bclhn,bchpn,bhcl->bclhp', ch, states[:, :-1], jnp.exp(la_cum))
    return (y_diag + y_off).reshape(bsz, n, nh, hp), final_state


def _ssd_prep(xbc, dt_raw, conv_w, conv_b, dt_bias):
    bsz, n, _ = xbc.shape
    u = jax.nn.silu(_dwconv_centred(xbc, conv_w, conv_b)).astype(jnp.float32)
    gn = SSD_GROUPS * SSD_STATE
    xs = u[..., :SSD_WIDTH].reshape(bsz, n, SSD_HEADS, SSD_HEAD_DIM)
    bm = u[..., SSD_WIDTH:SSD_WIDTH + gn].reshape(bsz, n, SSD_GROUPS, SSD_STATE)
    cm = u[..., SSD_WIDTH + gn:].reshape(bsz, n, SSD_GROUPS, SSD_STATE)
    dt = jax.nn.softplus(dt_raw.astype(jnp.float32).reshape(bsz, n, 2, SSD_HEADS) + dt_bias.astype(jnp.float32))
    return xs, bm, cm, dt


def _ssd_bidir(xs, bm, cm, dt, a, init_f, init_b, with_y):
    y_f, s_f = _ssd_chunked(xs, dt[:, :, 0], a[0], bm, cm, init_f, with_y)
    flip = lambda t: jnp.flip(t, axis=1)
    y_b, s_b = _ssd_chunked(flip(xs), flip(dt[:, :, 1]), a[1], flip(bm), flip(cm), init_b, with_y)
    y = y_f + flip(y_b) if with_y else None
    return y, s_f, s_b


def _ssd_branch(xbc, dt_raw, xbc_c, dt_raw_c, conv_w, conv_b, dt_bias, a_log, d_skip, with_ctx_out):
    bsz = xbc.shape[0]
    a = -jnp.exp(a_log.astype(jnp.float32))
    d_skip = d_skip.astype(jnp.float32)[:, None]
    zero = jnp.zeros((bsz, SSD_HEADS, SSD_HEAD_DIM, SSD_STATE), jnp.float32)
    xs_c, bm_c, cm_c, dt_c = _ssd_prep(xbc_c, dt_raw_c, conv_w, conv_b, dt_bias)
    y_c, s_f, s_b = _ssd_bidir(xs_c, bm_c, cm_c, dt_c, a, zero, zero, with_ctx_out)
    xs, bm, cm, dt = _ssd_prep(xbc, dt_raw, conv_w, conv_b, dt_bias)
    y, _, _ = _ssd_bidir(xs, bm, cm, dt, a, s_f, s_b, True)
    y = (y + xs * d_skip).reshape(bsz, xbc.shape[1], SSD_WIDTH)
    if with_ctx_out:
        y_c = (y_c + xs_c * d_skip).reshape(bsz, xbc_c.shape[1], SSD_WIDTH)
    return y, y_c


def _mla_attend(qn, qp, kn, kp, v, scale):
    s = (jnp.einsum('bqhd,bkhd->bhqk', qn, kn) + jnp.einsum('bqhr,bkr->bhqk', qp, kp)).astype(jnp.float32) * scale
    p = jax.nn.softmax(s, axis=-1).astype(v.dtype)
    return jnp.einsum('bhqk,bkhd->bqhd', p, v)


def _sweep_query_blocks(fn, *qs):
    bsz, n = qs[0].shape[:2]
    nb = n // Q_BLOCK
    blocks = tuple(jnp.moveaxis(q.reshape((bsz, nb, Q_BLOCK) + q.shape[2:]), 1, 0) for q in qs)
    out = lax.map(lambda blk: fn(*blk), blocks)
    out = jnp.moveaxis(out, 0, 1)
    return out.reshape((bsz, n) + out.shape[3:])


def _mla_branch(cq, ckv, kpe, cq_c, ckv_c, kpe_c, q_norm, w_uq, kv_norm, w_ukv, cos, sin, with_ctx_out):
    bsz, n, _ = cq.shape
    n_c = ckv_c.shape[1]
    scale = (MLA_NOPE + MLA_ROPE) ** -0.5

    def up_q(t):
        q = (_rmsnorm(t, q_norm) @ w_uq).reshape(t.shape[0], t.shape[1], MLA_HEADS, MLA_NOPE + MLA_ROPE)
        return q[..., :MLA_NOPE], q[..., MLA_NOPE:]

    def up_kv(t):
        kv = (_rmsnorm(t, kv_norm) @ w_ukv).reshape(t.shape[0], t.shape[1], MLA_HEADS, MLA_NOPE + MLA_V)
        return kv[..., :MLA_NOPE], kv[..., MLA_NOPE:]

    kn_c, v_c = up_kv(ckv_c)
    qn, qp = up_q(cq)
    qp = _apply_rope(qp, cos, sin)
    kn, v = up_kv(ckv)
    kp = _apply_rope(kpe, cos, sin)
    kn_all = jnp.concatenate([kn_c, kn], axis=1)
    kp_all = jnp.concatenate([kpe_c, kp], axis=1)
    v_all = jnp.concatenate([v_c, v], axis=1)
    y = _sweep_query_blocks(lambda a, b: _mla_attend(a, b, kn_all, kp_all, v_all, scale), qn, qp)
    y = y.reshape(bsz, n, MLA_WIDTH)
    y_c = None
    if with_ctx_out:
        qn_c, qp_c = up_q(cq_c)
        y_c = _mla_attend(qn_c, qp_c, kn_c, kpe_c, v_c, scale).reshape(bsz, n_c, MLA_WIDTH)
    return y, y_c


def _gqa_branch(q, k, v, q_c, k_c, v_c, sink, cos, sin, with_ctx_out):
    bsz, n, _ = q.shape
    n_c = k_c.shape[1]
    g = GQA_HEADS // GQA_KV_HEADS
    scale = GQA_HEAD_DIM ** -0.5
    q = _apply_rope(q.reshape(bsz, n, GQA_KV_HEADS, g, GQA_HEAD_DIM), cos, sin)
    k = _apply_rope(k.reshape(bsz, n, GQA_KV_HEADS, GQA_HEAD_DIM), cos, sin)
    v = v.reshape(bsz, n, GQA_KV_HEADS, GQA_HEAD_DIM)
    k_c = k_c.reshape(bsz, n_c, GQA_KV_HEADS, GQA_HEAD_DIM)
    v_c = v_c.reshape(bsz, n_c, GQA_KV_HEADS, GQA_HEAD_DIM)
    sink = sink.astype(jnp.float32).reshape(GQA_KV_HEADS, g)

    nb = n // WIN_BLOCK
    nw = 3 * WIN_BLOCK
    qb = q.reshape(bsz, nb, WIN_BLOCK, GQA_KV_HEADS, g, GQA_HEAD_DIM)

    def neighbours(t):
        tp = jnp.pad(t.reshape(bsz, nb, WIN_BLOCK, GQA_KV_HEADS, GQA_HEAD_DIM),
                     ((0, 0), (1, 1), (0, 0), (0, 0), (0, 0)))
        return jnp.concatenate([tp[:, :-2], tp[:, 1:-1], tp[:, 2:]], axis=2)

    kw, vw = neighbours(k), neighbours(v)
    s_loc = jnp.einsum('bnqkgd,bnjkd->bnkgqj', qb, kw).astype(jnp.float32) * scale
    blk = jnp.arange(nb)[:, None]
    qpos = blk * WIN_BLOCK + jnp.arange(WIN_BLOCK)[None, :]
    kpos = (blk - 1) * WIN_BLOCK + jnp.arange(nw)[None, :]
    valid = ((jnp.abs(qpos[:, :, None] - kpos[:, None, :]) <= WINDOW)
             & (kpos[:, None, :] >= 0) & (kpos[:, None, :] < n))
    s_loc = jnp.where(valid[None, :, None, None], s_loc, NEG_INF)
    s_ctx = jnp.einsum('bnqkgd,bjkd->bnkgqj', qb, k_c).astype(jnp.float32) * scale
    s_sink = jnp.broadcast_to(sink[None, None, :, :, None, None], s_loc.shape[:-1] + (1,))
    p = jax.nn.softmax(jnp.concatenate([s_loc, s_ctx, s_sink], axis=-1), axis=-1).astype(v.dtype)
    y = (jnp.einsum('bnkgqj,bnjkd->bnqkgd', p[..., :nw], vw)
         + jnp.einsum('bnkgqj,bjkd->bnqkgd', p[..., nw:nw + n_c], v_c))
    y = y.reshape(bsz, n, GQA_WIDTH)

    y_c = None
    if with_ctx_out:
        qc = q_c.reshape(bsz, n_c, GQA_KV_HEADS, g, GQA_HEAD_DIM)
        s = jnp.einsum('bqkgd,bjkd->bkgqj', qc, k_c).astype(jnp.float32) * scale
        s_sink_c = jnp.broadcast_to(sink[None, :, :, None, None], s.shape[:-1] + (1,))
        pc = jax.nn.softmax(jnp.concatenate([s, s_sink_c], axis=-1), axis=-1).astype(v_c.dtype)
        y_c = jnp.einsum('bkgqj,bjkd->bqkgd', pc[..., :n_c], v_c).reshape(bsz, n_c, GQA_WIDTH)
    return y, y_c


def _hyena_filter(n, w1, b1, w2, b2, w3, freq):
    f32 = jnp.float32
    t = jnp.linspace(0.0, 1.0, n, dtype=f32)[:, None]
    w_ang = (2.0 * math.pi / n) * jnp.arange(n, dtype=f32)[:, None]
    bands = jnp.linspace(1e-4, HY_BANDS - 1, HY_BANDS, dtype=f32)[None, :]
    feats = jnp.concatenate([t, jnp.cos(bands * w_ang), -jnp.sin(bands * w_ang)], axis=-1)
    freq = freq.astype(f32)
    hdn = jnp.sin(freq[0] * (feats @ w1.astype(f32) + b1.astype(f32)))
    hdn = jnp.sin(freq[1] * (hdn @ w2.astype(f32) + b2.astype(f32)))
    h = hdn @ w3.astype(f32)
    deltas = jnp.abs(jnp.linspace(math.log(HY_DECAY_TARGET) / HY_FAST_DECAY,
                                  math.log(HY_DECAY_TARGET) / HY_SLOW_DECAY, HY_WIDTH, dtype=f32))
    h = h * jnp.exp(-t * jnp.tile(deltas, 2)[None, :])
    h_fwd, h_bwd = h[:, :HY_WIDTH], h[:, HY_WIDTH:]
    return jnp.concatenate([h_fwd, jnp.zeros((1, HY_WIDTH), f32), h_bwd[:0:-1]], axis=0)


def _bidir_fftconv(u, filt, d_skip):
    n = u.shape[1]
    uf = jnp.fft.rfft(u.astype(jnp.float32), n=2 * n, axis=1)
    kf = jnp.fft.rfft(filt, axis=0)
    y = jnp.fft.irfft(uf * kf[None], n=2 * n, axis=1)[:, :n]
    return (y + u.astype(jnp.float32) * d_skip.astype(jnp.float32)).astype(u.dtype)


def _hyena_seq(xv, conv_w, conv_b, w1, b1, w2, b2, w3, freq, d_skip):
    n = xv.shape[1]
    u = _dwconv_centred(xv, conv_w, conv_b)
    x0, x1, v = u[..., :HY_WIDTH], u[..., HY_WIDTH:2 * HY_WIDTH], u[..., 2 * HY_WIDTH:]
    filt = _hyena_filter(n, w1, b1, w2, b2, w3, freq)
    return x0 * _bidir_fftconv(x1 * v, filt, d_skip)


def _merge(res, gate, merge_logits, ys, zs, ssd_norm_w, w_branch, w_out, ln_g, ln_b):
    y_ssd, y_mla, y_gqa, y_hy = ys
    z_ssd, z_mla, z_gqa, z_hy = zs
    dt = res.dtype
    lead = y_ssd.shape[:2]
    ssd_g = (y_ssd * jax.nn.silu(z_ssd.astype(jnp.float32))).reshape(lead + (SSD_GROUPS, SSD_WIDTH // SSD_GROUPS))
    ssd_g = _rmsnorm(ssd_g, ssd_norm_w.reshape(SSD_GROUPS, -1)).reshape(lead + (SSD_WIDTH,)).astype(dt)
    gated = jnp.stack([ssd_g,
                       y_mla * jax.nn.silu(z_mla),
                       y_gqa * jax.nn.silu(z_gqa),
                       y_hy * jax.nn.silu(z_hy)], axis=2)
    branch = jnp.einsum('blkw,kwd->blkd', gated, w_branch)
    gates = jax.nn.sigmoid(merge_logits.reshape(merge_logits.shape[:-1] + (N_BRANCH, D_MODEL)))
    mixed = jnp.sum(gates * branch, axis=2) @ w_out
    return _layernorm(DEEPNORM_ALPHA * res + gate * mixed) * ln_g + ln_b


def _layer(x, ctx, c, c_ctx, w_ada, b_ada, w_in, ssd_conv_w, ssd_conv_b, ssd_dt_bias, ssd_a_log, ssd_d,
           ssd_norm_w, mla_q_norm, mla_w_uq, mla_kv_norm, mla_w_ukv, gqa_sink, hy_conv_w, hy_conv_b,
           hy_w1, hy_b1, hy_w2, hy_b2, hy_w3, hy_freq, hy_d, w_branch, w_out, ln_g, ln_b, cos, sin, with_ctx_out):
    mod = jax.nn.silu(c) @ w_ada + b_ada
    shift, scale, gate = jnp.split(mod[:, None, :], 3, axis=-1)
    mod_c = jax.nn.silu(c_ctx) @ w_ada + b_ada
    shift_c, scale_c, gate_c = jnp.split(mod_c, 3, axis=-1)
    h = _layernorm(x) * (1.0 + scale) + shift
    h_c = _layernorm(ctx) * (1.0 + scale_c) + shift_c
    (s_z, s_xbc, s_dt, m_cq, m_ckv, m_kpe, m_z, g_q, g_k, g_v, g_z, hy_xv, hy_z, merge) = _split_cols(h @ w_in)
    (s_z_c, s_xbc_c, s_dt_c, m_cq_c, m_ckv_c, m_kpe_c, m_z_c, g_q_c, g_k_c, g_v_c, g_z_c, hy_xv_c, hy_z_c,
     merge_c) = _split_cols(h_c @ w_in)

    y_ssd, y_ssd_c = _ssd_branch(s_xbc, s_dt, s_xbc_c, s_dt_c, ssd_conv_w, ssd_conv_b, ssd_dt_bias, ssd_a_log,
                                 ssd_d, with_ctx_out)
    y_mla, y_mla_c = _mla_branch(m_cq, m_ckv, m_kpe, m_cq_c, m_ckv_c, m_kpe_c, mla_q_norm, mla_w_uq,
                                 mla_kv_norm, mla_w_ukv, cos, sin, with_ctx_out)
    y_gqa, y_gqa_c = _gqa_branch(g_q, g_k, g_v, g_q_c, g_k_c, g_v_c, gqa_sink, cos, sin, with_ctx_out)
    y_hy = _hyena_seq(hy_xv, hy_conv_w, hy_conv_b, hy_w1, hy_b1, hy_w2, hy_b2, hy_w3, hy_freq, hy_d)
    x_new = _merge(x, gate, merge, (y_ssd, y_mla, y_gqa, y_hy), (s_z, m_z, g_z, hy_z),
                   ssd_norm_w, w_branch, w_out, ln_g, ln_b)
    if not with_ctx_out:
        return x_new, None
    y_hy_c = _hyena_seq(hy_xv_c, hy_conv_w, hy_conv_b, hy_w1, hy_b1, hy_w2, hy_b2, hy_w3, hy_freq, hy_d)
    ctx_new = _merge(ctx, gate_c, merge_c, (y_ssd_c, y_mla_c, y_gqa_c, y_hy_c), (s_z_c, m_z_c, g_z_c, hy_z_c),
                     ssd_norm_w, w_branch, w_out, ln_g, ln_b)
    return x_new, ctx_new


def setup_inputs(seed: int = 0) -> dict:
    key = jax.random.key(seed)
    ks = iter(jax.random.split(key, 64))
    f32 = jnp.float32
    d = D_MODEL
    L = DEPTH

    def nrm(shape, scale):
        return jax.random.normal(next(ks), shape, f32) * scale

    u_dt = jax.random.uniform(next(ks), (L, 2, SSD_HEADS), f32)
    dt0 = jnp.exp(u_dt * (math.log(0.1) - math.log(0.001)) + math.log(0.001))
    ssd_dt_bias = dt0 + jnp.log(-jnp.expm1(-dt0))
    ssd_a_log = jnp.log(jax.random.uniform(next(ks), (L, 2, SSD_HEADS), f32, 1.0, 16.0))
    return {
        'x': nrm((BATCH, SEQ, d), 1.0),
        'c': nrm((BATCH, d), 1.0),
        'ctx': nrm((BATCH, CTX_LEN, d), 1.0),
        'c_ctx': nrm((d,), 1.0),
        'w_ada': nrm((L, d, 3 * d), d ** -0.5),
        'b_ada': nrm((L, 3 * d), 0.02),
        'w_in': nrm((L, d, IN_WIDTH), d ** -0.5),
        'ssd_conv_w': nrm((L, SSD_CONV, SSD_CONV_CH), SSD_CONV ** -0.5),
        'ssd_conv_b': nrm((L, SSD_CONV_CH), 0.02),
        'ssd_dt_bias': ssd_dt_bias,
        'ssd_a_log': ssd_a_log,
        'ssd_d': 1.0 + nrm((L, SSD_HEADS), 0.1),
        'ssd_norm_w': 1.0 + nrm((L, SSD_WIDTH), 0.02),
        'mla_q_norm': 1.0 + nrm((L, MLA_Q_LORA), 0.02),
        'mla_w_uq': nrm((L, MLA_Q_LORA, MLA_HEADS * (MLA_NOPE + MLA_ROPE)), MLA_Q_LORA ** -0.5),
        'mla_kv_norm': 1.0 + nrm((L, MLA_KV_LORA), 0.02),
        'mla_w_ukv': nrm((L, MLA_KV_LORA, MLA_HEADS * (MLA_NOPE + MLA_V)), MLA_KV_LORA ** -0.5),
        'gqa_sink': nrm((L, GQA_HEADS), 0.5),
        'hy_conv_w': nrm((L, HY_CONV, 3 * HY_WIDTH), HY_CONV ** -0.5),
        'hy_conv_b': nrm((L, 3 * HY_WIDTH), 0.02),
        'hy_w1': nrm((L, HY_POS_EMB, HY_FILTER_HIDDEN), HY_POS_EMB ** -0.5),
        'hy_b1': nrm((L, HY_FILTER_HIDDEN), 0.1),
        'hy_w2': nrm((L, HY_FILTER_HIDDEN, HY_FILTER_HIDDEN), HY_FILTER_HIDDEN ** -0.5),
        'hy_b2': nrm((L, HY_FILTER_HIDDEN), 0.1),
        'hy_w3': nrm((L, HY_FILTER_HIDDEN, 2 * HY_WIDTH), 0.1 * HY_FILTER_HIDDEN ** -0.5),
        'hy_freq': 1.0 + nrm((L, 2, HY_FILTER_HIDDEN), 0.1),
        'hy_d': nrm((L, HY_WIDTH), 0.5),
        'w_branch': nrm((L, N_BRANCH, BRANCH_WIDTH, d), DEEPNORM_BETA * BRANCH_WIDTH ** -0.5),
        'w_out': nrm((L, d, d), DEEPNORM_BETA * d ** -0.5),
        'ln_g': 1.0 + nrm((L, d), 0.02),
        'ln_b': nrm((L, d), 0.02),
    }


def reference(x, c, ctx, c_ctx, w_ada, b_ada, w_in, ssd_conv_w, ssd_conv_b, ssd_dt_bias, ssd_a_log, ssd_d,
              ssd_norm_w, mla_q_norm, mla_w_uq, mla_kv_norm, mla_w_ukv, gqa_sink, hy_conv_w, hy_conv_b,
              hy_w1, hy_b1, hy_w2, hy_b2, hy_w3, hy_freq, hy_d, w_branch, w_out, ln_g, ln_b):
    n_lat = x.shape[1]
    cos, sin = _axial_rope_tables(n_lat, ROPE_DIM)
    for i in range(DEPTH):
        x, ctx = _layer(x, ctx, c, c_ctx, w_ada[i], b_ada[i], w_in[i], ssd_conv_w[i], ssd_conv_b[i],
                        ssd_dt_bias[i], ssd_a_log[i], ssd_d[i], ssd_norm_w[i], mla_q_norm[i], mla_w_uq[i],
                        mla_kv_norm[i], mla_w_ukv[i], gqa_sink[i], hy_conv_w[i], hy_conv_b[i], hy_w1[i],
                        hy_b1[i], hy_w2[i], hy_b2[i], hy_w3[i], hy_freq[i], hy_d[i], w_branch[i], w_out[i],
                        ln_g[i], ln_b[i], cos, sin, i < DEPTH - 1)
    return x
```

```python
import contextlib
import numpy as np
import concourse.bass as bass
import concourse.mybir as mybir
from concourse.bass_utils import run_bass_kernel_spmd

F32 = mybir.dt.float32
BF16 = mybir.dt.bfloat16
AF = mybir.ActivationFunctionType
ALU = mybir.AluOpType
AX = mybir.AxisListType

SAME_ENGINE_SYNC = True
NDMASEM = 8


class V:
    def __init__(self, buf, ap):
        self.buf = buf
        self.ap = ap

    def __getitem__(self, idx):
        return V(self.buf, self.ap[idx])

    def __getattr__(self, name):
        attr = getattr(self.ap, name)
        if callable(attr):
            buf = self.buf
            apt = type(self.ap)

            def f(*a, **k):
                r = attr(*a, **k)
                return V(buf, r) if isinstance(r, apt) else r
            return f
        return attr


class Buf:
    def __init__(self, t, name):
        self.t = t
        self.name = name
        self.w = None
        self.r = []
        self.full = t.ap() if hasattr(t, "ap") else t[:]
        self.is_psum = False

    def __getitem__(self, idx):
        return V(self, self.full[idx])

    @property
    def v(self):
        return V(self, self.full)


class Prog:
    ENG = ["pe", "act", "dve", "pool", "sp"]

    def __init__(self, nc):
        self.nc = nc
        self.ops = {e: [] for e in self.ENG}
        self.sems = {}
        for e in self.ENG:
            self.sems[("c", e)] = nc.alloc_semaphore("c_" + e)
        self.cnt = {e: 0 for e in self.ENG}
        self.waited = {e: {} for e in self.ENG}
        self.dma_i = {e: 0 for e in self.ENG}
        self.dma_use = {}
        self.nbuf = 0
        self.out_tokens = []
        self.scopes = []

    def sb(self, shape, dt, name=None):
        self.nbuf += 1
        name = f"s{self.nbuf}_" + (name or "t")
        if self.scopes:
            t = self.scopes[-1].enter_context(self.nc.sbuf_tensor(name, list(shape), dt))
        else:
            t = self.nc.alloc_sbuf_tensor(name, list(shape), dt)
        return Buf(t, name)

    @contextlib.contextmanager
    def scope(self):
        es = contextlib.ExitStack()
        self.scopes.append(es)
        try:
            yield
        finally:
            self.barrier()
            self.scopes.pop()
            es.close()

    def barrier(self):
        snap = [(("c", e), self.cnt[e]) for e in self.ENG if self.cnt[e] > 0]
        snap += [(key, 16 * n) for key, n in self.dma_use.items() if n > 0]
        for e in self.ENG:
            waits = []
            for key, val in snap:
                if key == ("c", e):
                    continue
                if self.waited[e].get(key, -1) >= val:
                    continue
                self.waited[e][key] = val
                waits.append((key, val))
            if waits:
                self.ops[e].append((waits, None, None, 0))

    def ps(self, shape, dt, name=None):
        self.nbuf += 1
        name = name or f"ps{self.nbuf}"
        b = Buf(self.nc.alloc_psum_tensor(name, list(shape), dt), name)
        b.is_psum = True
        return b

    def dram(self, name, shape, dt, kind="Internal"):
        return Buf(self.nc.dram_tensor(name, list(shape), dt, kind=kind), name)

    def _deps(self, eng, reads, writes, is_dma):
        toks = []
        for b in reads:
            if b.w is not None:
                toks.append(b.w)
            if b.is_psum:
                toks.extend(t for t in b.r if t[0] != eng)
        for b in writes:
            if b.w is not None:
                toks.append(b.w)
            toks.extend(b.r)
        best = {}
        for tok in toks:
            teng, key, val, tdma = tok
            if (not tdma) and teng == eng and not is_dma:
                if eng == "pe" or not SAME_ENGINE_SYNC:
                    continue
            if key not in best or best[key] < val:
                best[key] = val
        waits = []
        for key, val in best.items():
            if self.waited[eng].get(key, -1) >= val:
                continue
            self.waited[eng][key] = val
            waits.append((key, val))
        return waits

    def _commit(self, tok, reads, writes):
        for b in reads:
            b.r.append(tok)
            if len(b.r) > 64:
                d = {}
                for t in b.r:
                    k = (t[0], t[1], t[3])
                    if k not in d or d[k][2] < t[2]:
                        d[k] = t
                b.r = list(d.values())
        for b in writes:
            b.w = tok
            b.r = []

    @staticmethod
    def _scan(args, kwargs, reads, writes):
        reads = list(reads)
        writes = list(writes)
        a2 = []
        for i, a in enumerate(args):
            if isinstance(a, V):
                (writes if i == 0 else reads).append(a.buf)
                a = a.ap
            a2.append(a)
        k2 = {}
        for k, a in kwargs.items():
            if isinstance(a, V):
                (writes if k in ("out", "accum_out") else reads).append(a.buf)
                a = a.ap
            k2[k] = a
        return a2, k2, reads, writes

    def op(self, eng, meth, *args, reads=(), writes=(), **kwargs):
        args, kwargs, reads, writes = self._scan(args, kwargs, reads, writes)
        fn = (meth, args, kwargs)
        waits = self._deps(eng, reads, writes, False)
        self.cnt[eng] += 1
        key = ("c", eng)
        tok = (eng, key, self.cnt[eng], False)
        self.ops[eng].append((waits, fn, key, 1))
        self._commit(tok, reads, writes)
        return tok

    def dma(self, out, in_, q="sp", is_out=False, reads=(), writes=(), **kw):
        args, kwargs, reads, writes = self._scan((), dict(out=out, in_=in_, **kw), reads, writes)
        fn = ("dma_start", args, kwargs)
        waits = self._deps(q, reads, writes, True)
        i = self.dma_i[q]
        self.dma_i[q] += 1
        slot = i % NDMASEM
        key = ("d", q, slot)
        if key not in self.sems:
            self.sems[key] = self.nc.alloc_semaphore(f"d_{q}_{slot}")
            self.dma_use[key] = 0
        prev = self.dma_use[key]
        if prev > 0 and self.waited[q].get(key, -1) < 16 * prev:
            self.waited[q][key] = 16 * prev
            waits.append((key, 16 * prev))
        self.dma_use[key] = prev + 1
        tok = (q, key, 16 * (prev + 1), True)
        self.ops[q].append((waits, fn, key, 16))
        self._commit(tok, reads, writes)
        if is_out:
            self.out_tokens.append(tok)
        return tok

    def cc(self, kind, alu, groups, src, dst):
        q = "pool"
        reads, writes = [src], [dst]
        waits = self._deps(q, reads, writes, True)
        key = ("d", q, "cc%d" % self.dma_i[q])
        self.dma_i[q] += 1
        self.sems[key] = self.nc.alloc_semaphore("cc_%d" % len(self.sems))
        self.dma_use[key] = 1
        tok = (q, key, 16, True)
        fn = ("collective_compute", [kind, alu], dict(replica_groups=groups, ins=[src.full], outs=[dst.full]))
        self.ops[q].append((waits, fn, key, 16))
        self._commit(tok, reads, writes)
        return tok

    def emit(self):
        nc = self.nc
        finals = [(t[1], t[2]) for t in self.out_tokens]
        for e in self.ENG:
            if self.cnt[e] > 0:
                finals.append((("c", e), self.cnt[e]))
        for key, n in self.dma_use.items():
            finals.append((key, 16 * n))
        sems = self.sems
        ops = self.ops

        def run(engname):
            def body(eng):
                for waits, fn, key, inc in ops[engname]:
                    for k, v in waits:
                        eng.wait_ge(sems[k], v)
                    if fn is None:
                        continue
                    ins = getattr(eng, fn[0])(*fn[1], **fn[2])
                    ins.then_inc(sems[key], inc)
                if engname == "sp":
                    for k, v in finals:
                        eng.wait_ge(sems[k], v)
            return body

        with nc.Block() as blk:
            blk.tensor(run("pe"))
            blk.scalar(run("act"))
            blk.vector(run("dve"))
            blk.gpsimd(run("pool"))
            blk.sync(run("sp"))


D = 2048
NTOK = 2304
NTILE = 18
NCTX = 256
NSEQ = 2048
TG = [(0, 256), (256, 512), (768, 512), (1280, 512), (1792, 512)]
PI = float(np.pi)


class Cx:
    def __init__(self, P, wcols=256):
        self.P = P
        self.psf = [P.ps([128, 512], F32, f"psf{i}") for i in range(6)]
        self.pst = [P.ps([128, 1024], BF16, f"pst{i}") for i in range(2)]
        self.ipf = 0
        self.ipt = 0
        self.wcols = wcols
        self.wi = 0
        identf = P.sb([128, 128], F32, "identf")
        self.ident = P.sb([128, 128], BF16, "ident")
        P.op("pool", "memset", identf[:, :], 0.0)
        P.op("pool", "affine_select", identf[:, :], identf[:, :], [[1, 128]], ALU.not_equal, 1.0,
             base=0, channel_multiplier=-1)
        P.op("dve", "tensor_copy", self.ident[:, :], identf[:, :])
        self.identf = identf
        self.cst = P.sb([128, 8], F32, "cst")
        for i, v in enumerate([1e-6, -PI, 1.0, 0.0]):
            P.op("dve", "memset", self.cst[:, i:i + 1], float(v))
        self.onesb = P.sb([128, 128], BF16, "onesb")
        P.op("dve", "memset", self.onesb[:, :], 1.0)

    def alloc_w(self):
        P = self.P
        nf = 2 if self.wcols > 128 else 3
        self.wf = [P.sb([128, 16, self.wcols], F32, f"wf{i}") for i in range(nf)]
        self.wb = [P.sb([128, 16, self.wcols], BF16, f"wb{i}") for i in range(2)]

    def eps(self, n=128):
        return self.cst[:n, 0:1]

    def negpi(self, n=128):
        return self.cst[:n, 1:2]

    def bank(self):
        b = self.psf[self.ipf % len(self.psf)]
        self.ipf += 1
        return b

    def tbank(self):
        b = self.pst[self.ipt % len(self.pst)]
        self.ipt += 1
        return b


def tgroups(t_lo, t_hi):
    a, e = t_lo * 128, t_hi * 128
    out = []
    if a < NCTX:
        out.append((a, min(e, NCTX) - a))
        a = min(e, NCTX)
    while a < e:
        l = min(512, e - a)
        out.append((a, l))
        a += l
    return out


class HT:
    def __init__(self, cx, hT_d, t_lo, t_hi):
        P = cx.P
        self.base = t_lo * 128
        self.parts = []
        for gi, (t0, tl) in enumerate(tgroups(t_lo, t_hi)):
            b = P.sb([128, 16, tl], BF16, f"hTg{gi}")
            P.dma(b[:, :, :], hT_d[:, :, t0:t0 + tl], q=("sp" if gi % 2 == 0 else "pool"))
            self.parts.append((t0 - self.base, tl, b))

    def __getitem__(self, idx):
        p, kk, sl = idx
        a, e = sl.start, sl.stop
        for (r0, tl, b) in self.parts:
            if r0 <= a and e <= r0 + tl:
                return b[p, kk, a - r0:e - r0]
        raise IndexError((a, e))


def emit_h(cx, xin, modT, t_lo=0, t_hi=NTILE, ctx_tiles=2):
    P = cx.P
    hT_d = getattr(cx, "hT_d", None)
    if hT_d is not None:
        return HT(cx, hT_d, t_lo, t_hi)
    hT = P.sb([128, 16, (t_hi - t_lo) * 128], BF16, "hT")
    with P.scope():
        _emit_h_body(cx, xin, modT, hT, t_lo, t_hi, ctx_tiles)
    return hT


def _emit_h_body(cx, xin, modT, hT, t_lo, t_hi, ctx_tiles):
    P = cx.P
    xts = [P.sb([128, D], F32, f"xt{i}") for i in range(2)]
    xns = [P.sb([128, D], BF16, f"xn{i}") for i in range(2)]
    sts = [P.sb([128, 4, 6], F32, f"lnst{i}") for i in range(2)]
    mvs = [P.sb([128, 2], F32, f"lnmv{i}") for i in range(2)]
    rss = [P.sb([128, 1], F32, f"lnrs{i}") for i in range(2)]
    for t in range(t_lo, t_hi):
        tt = t - t_lo
        xt, xn, st, mv, rs = xts[t % 2], xns[t % 2], sts[t % 2], mvs[t % 2], rss[t % 2]
        P.dma(xt[:, :], xin[t * 128:(t + 1) * 128, :], q=("sp" if t % 2 == 0 else "pool"))
        for c in range(4):
            P.op("dve", "bn_stats", st[:, c, :], xt[:, c * 512:(c + 1) * 512])
        P.op("dve", "bn_aggr", mv[:, :], st[:, :, :])
        P.op("act", "activation", rs[:, :], mv[:, 1:2], AF.Sqrt, bias=cx.eps(), scale=1.0)
        P.op("dve", "reciprocal", rs[:, :], rs[:, :])
        P.op("dve", "tensor_scalar", xn[:, :], xt[:, :], mv[:, 0:1], rs[:, 0:1], ALU.subtract, ALU.mult)
        mo = 32 if t < ctx_tiles else 0
        for g in range(2):
            pt = cx.tbank()
            for j in range(8):
                k = g * 8 + j
                P.op("pe", "transpose", pt[:, j * 128:(j + 1) * 128], xn[:, k * 128:(k + 1) * 128], cx.ident[:, :])
            for j in range(8):
                k = g * 8 + j
                P.op("act", "activation", hT[:, k, tt * 128:(tt + 1) * 128], pt[:, j * 128:(j + 1) * 128],
                     AF.Identity, bias=modT[:, mo + k:mo + k + 1], scale=modT[:, mo + 16 + k:mo + 17 + k])


def load_modT(cx, modT_d):
    P = cx.P
    m = P.sb([128, 64], F32, "modT_sb")
    P.dma(m[:, :], modT_d.v)
    P.op("dve", "tensor_scalar", m[:, 16:32], m[:, 16:32], 1.0, None, ALU.add)
    P.op("dve", "tensor_scalar", m[:, 48:64], m[:, 48:64], 1.0, None, ALU.add)
    return m


def project(cx, hT, wd, col0, n, mode, evac, cast_eng="pool", t_lo=0, t_hi=NTILE):
    P = cx.P
    W = cx.wcols
    groups = [(c0, min(W, n - c0)) for c0 in range(0, n, W)]
    base = cx.wi
    cx.wi += len(groups)

    nf = len(cx.wf)

    def issue(gi):
        c0, cw = groups[gi]
        wf = cx.wf[(base + gi) % nf]
        P.dma(wf[:, :, :cw], wd[:, col0 + c0:col0 + c0 + cw].rearrange("(k p) n -> p k n", p=128), q="sp")

    for gi in range(min(nf - 1, len(groups))):
        issue(gi)
    for gi, (c0, cw) in enumerate(groups):
        if gi + nf - 1 < len(groups):
            issue(gi + nf - 1)
        wf, wb = cx.wf[(base + gi) % nf], cx.wb[(base + gi) % 2]
        if (base + gi) % 2 == 0:
            P.op("dve", "tensor_copy", wb[:, :, :cw], wf[:, :, :cw])
        else:
            P.op("act", "activation", wb[:, :, :cw], wf[:, :, :cw], AF.Identity)
        if mode == "fm":
            for j in range(0, cw, 128):
                jw = min(128, cw - j)
                for (t0, tl) in tgroups(t_lo, t_hi):
                    ps = cx.bank()
                    h0 = t0 - t_lo * 128
                    for k in range(16):
                        P.op("pe", "matmul", ps[:jw, :tl], wb[:, k, j:j + jw], hT[:, k, h0:h0 + tl],
                             start=(k == 0), stop=(k == 15))
                    evac(ps, c0 + j, jw, t0, tl)
        else:
            for t in range(t_lo, t_hi):
                ps = cx.bank()
                tt = t - t_lo
                for k in range(16):
                    P.op("pe", "matmul", ps[:, :cw], hT[:, k, tt * 128:(tt + 1) * 128], wb[:, k, :cw],
                         start=(k == 0), stop=(k == 15))
                evac(ps, c0, cw, t * 128, 128)


def conv3(P, eng, out, pre, w, b, a, e):
    P.op(eng, "tensor_scalar", out[:, a:e], pre[:, a:e], w[:, 1:2], b, ALU.mult, ALU.add)
    P.op("dve", "scalar_tensor_tensor", out[:, a + 1:e], pre[:, a:e - 1], w[:, 0:1], out[:, a + 1:e], ALU.mult, ALU.add)
    P.op("dve", "scalar_tensor_tensor", out[:, a:e - 1], pre[:, a + 1:e], w[:, 2:3], out[:, a:e - 1], ALU.mult, ALU.add)


def build_mod():
    nc = bass.Bass("TRN2", target_bir_lowering=False)
    P = Prog(nc)
    cT = P.dram("cT", [128, 16, 5], F32, kind="ExternalInput")
    wa = P.dram("wa", [2048, 768], F32, kind="ExternalInput")
    ba = P.dram("ba", [128, 6], F32, kind="ExternalInput")
    out = P.dram("modT", [128, 6, 5], F32, kind="ExternalOutput")
    cs = P.sb([128, 16, 5], F32, "cs")
    bs = P.sb([128, 6], F32, "bs")
    ws = P.sb([128, 16, 768], F32, "ws")
    P.dma(cs[:, :, :], cT.v)
    P.dma(bs[:, :], ba.v)
    P.dma(ws[:, :, :], wa.v.rearrange("(k p) n -> p k n", p=128))
    P.op("act", "activation", cs[:, :, :], cs[:, :, :], AF.Silu)
    ps = P.ps([128, 512], F32, "ps")
    o = P.sb([128, 6, 5], F32, "o")
    for j in range(6):
        for k in range(16):
            P.op("pe", "matmul", ps[:, j * 8:j * 8 + 5], ws[:, k, j * 128:(j + 1) * 128], cs[:, k, :],
                 start=(k == 0), stop=(k == 15))
    for j in range(6):
        P.op("dve", "tensor_scalar", o[:, j, :], ps[:, j * 8:j * 8 + 5], bs[:, j:j + 1], None, ALU.add)
    P.dma(out.v, o[:, :, :], is_out=True)
    P.emit()
    return nc


def sin_mlp_layer(cx, outT, ps, n, bcol, fcol, tmp, tmpi):
    P = cx.P
    I2PI = 1.0 / (2 * PI)
    P.op("dve", "tensor_scalar", tmp[:64, :n], ps[:64, :n], bcol, fcol, ALU.add, ALU.mult)
    P.op("dve", "tensor_scalar", tmp[:64, :n], tmp[:64, :n], I2PI, 16.0, ALU.mult, ALU.add)
    P.op("dve", "tensor_copy", tmpi[:64, :n], tmp[:64, :n])
    P.op("dve", "tensor_copy", outT[:64, :n], tmpi[:64, :n])
    P.op("dve", "tensor_tensor", tmp[:64, :n], tmp[:64, :n], outT[:64, :n], ALU.subtract)
    P.op("act", "activation", outT[:64, :n], tmp[:64, :n], AF.Sin, scale=6.28318)


def hy_filter(cx, n, featsT_d, dec_d, mlp, gp, gm):
    P = cx.P
    w1, w2, w3, pb = mlp
    nch = n // 128
    fT = P.sb([33, n], F32, f"featsT{n}")
    P.dma(fT[:, :], featsT_d.v)
    decs = [P.sb([128, 512], F32, f"dec{n}_{i}") for i in range(2)]
    h1 = P.sb([64, n], F32, f"h1T{n}")
    h2 = P.sb([64, n], F32, f"h2T{n}")
    tmp = P.sb([64, 512], F32, f"mlptmp{n}")
    tmpi = P.sb([64, 512], mybir.dt.int32, f"mlptmpi{n}")
    for m0 in range(0, n, 512):
        ml = min(512, n - m0)
        ps = cx.bank()
        P.op("pe", "matmul", ps[:64, :ml], w1[:, :], fT[:, m0:m0 + ml], start=True, stop=True)
        sin_mlp_layer(cx, h1[:, m0:m0 + ml], ps, ml, pb[:, 0:1], pb[:, 1:2], tmp, tmpi)
    for m0 in range(0, n, 512):
        ml = min(512, n - m0)
        ps = cx.bank()
        P.op("pe", "matmul", ps[:64, :ml], w2[:, :], h1[:, m0:m0 + ml], start=True, stop=True)
        sin_mlp_layer(cx, h2[:, m0:m0 + ml], ps, ml, pb[:, 2:3], pb[:, 3:4], tmp, tmpi)
    hb = P.sb([128, 512], F32, f"hb{n}")
    sm = P.sb([128, 512], F32, f"hsum{n}")
    df = P.sb([128, 512], F32, f"hdif{n}")
    for mc in range(nch):
        dec = decs[mc % 2]
        P.dma(dec[:, :], dec_d[:, mc, :])
        pf = cx.bank()
        pbk = cx.bank()
        P.op("pe", "matmul", pf[:, :], h2[:, mc * 128:(mc + 1) * 128], w3[:, 0:512], start=True, stop=True)
        P.op("pe", "matmul", pbk[:, :], h2[:, mc * 128:(mc + 1) * 128], w3[:, 512:1024], start=True, stop=True)
        P.op("act", "activation", hb[:, :], pbk[:, :], AF.Identity)
        if mc == 0:
            P.op("dve", "memset", hb[0:1, :], 0.0)
        P.op("dve", "tensor_tensor", sm[:, :], pf[:, :], hb[:, :], ALU.add)
        P.op("dve", "tensor_tensor", df[:, :], pf[:, :], hb[:, :], ALU.subtract)
        P.op("pool", "tensor_tensor", gp[:, mc, :], sm[:, :], dec[:, :], ALU.mult)
        P.op("pool", "tensor_tensor", gm[:, mc, :], df[:, :], dec[:, :], ALU.mult)


def hy_seq(cx, n, t0, tabs, gp, gm, ut, ufm, x0z, dcol, outT):
    P = cx.P
    fwc, fws, ivc, ivs = tabs
    nch = n // 128
    TW = 256
    K = [P.sb([128, nch, 512], BF16, f"Kr{n}"), P.sb([128, nch, 512], BF16, f"Ki{n}")]
    sc = [P.sb([128, nch * TW], BF16, f"slc{n}_{i}") for i in range(2)]
    ss = [P.sb([128, nch * TW], BF16, f"sls{n}_{i}") for i in range(2)]
    ur = P.sb([128, 512], F32, f"ur{n}")
    ui = P.sb([128, 512], F32, f"ui{n}")
    t1 = P.sb([128, 512], F32, f"t1{n}")
    t2 = P.sb([128, 512], F32, f"t2{n}")
    t3 = P.sb([128, 512], F32, f"t3{n}")
    t4 = P.sb([128, 512], F32, f"t4{n}")
    tile0 = t0 // 128
    Kr, Ki = K
    for fc in range(nch):
        c_, s_ = sc[fc % 2], ss[fc % 2]
        P.dma(c_[:, :nch * 128], fwc[fc], q="sp")
        P.dma(s_[:, :nch * 128], fws[fc], q="pool")
        cv = c_[:, :nch * 128].rearrange("p (k f) -> p k f", k=nch)
        sv = s_[:, :nch * 128].rearrange("p (k f) -> p k f", k=nch)
        pr, pi = cx.bank(), cx.bank()
        for k in range(nch):
            P.op("pe", "matmul", pr[:, :], cv[:, k, :], gp[:, k, :], start=(k == 0), stop=(k == nch - 1))
        for k in range(nch):
            P.op("pe", "matmul", pi[:, :], sv[:, k, :], gm[:, k, :], start=(k == 0), stop=(k == nch - 1))
        P.op("act", "activation", Kr[:, fc, :], pr[:, :], AF.Identity)
        P.op("act", "activation", Ki[:, fc, :], pi[:, :], AF.Identity)
        pr, pi = cx.bank(), cx.bank()
        for k in range(nch):
            P.op("pe", "matmul", pr[:, :], cv[:, k, :], ut[:, tile0 + k, :], start=(k == 0), stop=(k == nch - 1))
        for k in range(nch):
            P.op("pe", "matmul", pi[:, :], sv[:, k, :], ut[:, tile0 + k, :], start=(k == 0), stop=(k == nch - 1))
        P.op("act", "activation", ur[:, :], pr[:, :], AF.Identity)
        P.op("act", "activation", ui[:, :], pi[:, :], AF.Identity)
        P.op("dve", "tensor_tensor", t1[:, :], ur[:, :], Kr[:, fc, :], ALU.mult)
        P.op("pool", "tensor_tensor", t2[:, :], ui[:, :], Ki[:, fc, :], ALU.mult)
        P.op("dve", "tensor_tensor", t3[:, :], ur[:, :], Ki[:, fc, :], ALU.mult)
        P.op("pool", "tensor_tensor", t4[:, :], ui[:, :], Kr[:, fc, :], ALU.mult)
        P.op("dve", "tensor_tensor", Kr[:, fc, :], t1[:, :], t2[:, :], ALU.subtract)
        P.op("dve", "tensor_tensor", Ki[:, fc, :], t3[:, :], t4[:, :], ALU.add)
    Yr, Yi = Kr, Ki
    ores = [P.sb([128, TW], F32, f"hyo{n}_{i}") for i in range(2)]
    ob = [P.sb([128, TW], BF16, f"hyob{n}_{i}") for i in range(2)]
    it = 0
    for tg in range(n // TW):
        c_, s_ = sc[tg % 2], ss[tg % 2]
        P.dma(c_[:, :], ivc[tg], q="sp")
        P.dma(s_[:, :], ivs[tg], q="pool")
        cv = c_[:, :].rearrange("p (k t) -> p k t", k=nch)
        sv = s_[:, :].rearrange("p (k t) -> p k t", k=nch)
        a = t0 + tg * TW
        for cc in range(4):
            ps = cx.bank()
            for k in range(nch):
                P.op("pe", "matmul", ps[:, :TW], Yr[:, k, cc * 128:(cc + 1) * 128], cv[:, k, :], start=(k == 0), stop=False)
            for k in range(nch):
                P.op("pe", "matmul", ps[:, :TW], Yi[:, k, cc * 128:(cc + 1) * 128], sv[:, k, :], start=False, stop=(k == nch - 1))
            o, o2 = ores[it % 2], ob[it % 2]
            it += 1
            P.op("dve", "scalar_tensor_tensor", o[:, :], ufm[:, cc, a:a + TW], dcol[:, cc:cc + 1], ps[:, :TW], ALU.mult, ALU.add)
            P.op("pool", "tensor_tensor", o2[:, :], o[:, :], x0z[:, cc, a:a + TW], ALU.mult)
            P.dma(outT[cc, :, a:a + TW], o2[:, :], q="sp", is_out=True)


def decl_hy(P, pfx, io=True):
    xin = (P.dram(pfx + "xin", [NTOK, D], F32, kind="ExternalInput") if io else None)
    modT_d = (P.dram(pfx + "modT", [128, 64], F32, kind="ExternalInput") if io else None)
    w = P.dram(pfx + "w", [D, 2048], F32, kind="ExternalInput")
    cw_d = P.dram(pfx + "cw", [128, 12, 4], F32, kind="ExternalInput")
    dsk_d = P.dram(pfx + "dsk", [128, 4], F32, kind="ExternalInput")
    w1_d = P.dram(pfx + "w1", [33, 64], F32, kind="ExternalInput")
    w2_d = P.dram(pfx + "w2", [64, 64], F32, kind="ExternalInput")
    w3_d = P.dram(pfx + "w3", [64, 1024], F32, kind="ExternalInput")
    pb_d = P.dram(pfx + "pb", [64, 4], F32, kind="ExternalInput")
    fL = P.dram(pfx + "featsL", [33, 2048], F32, kind="ExternalInput")
    fC = P.dram(pfx + "featsC", [33, 256], F32, kind="ExternalInput")
    dL = P.dram(pfx + "decL", [128, 16, 512], F32, kind="ExternalInput")
    dC = P.dram(pfx + "decC", [128, 2, 512], F32, kind="ExternalInput")
    tabL = [P.dram(pfx + nm, [16, 128, 2048], BF16, kind="ExternalInput") for nm in ("fwcL", "fwsL")] + \
           [P.dram(pfx + nm, [8, 128, 16 * 256], BF16, kind="ExternalInput") for nm in ("ivcL", "ivsL")]
    tabC = [P.dram(pfx + nm, [2, 128, 256], BF16, kind="ExternalInput") for nm in ("fwcC", "fwsC")] + \
           [P.dram(pfx + nm, [1, 128, 2 * 256], BF16, kind="ExternalInput") for nm in ("ivcC", "ivsC")]
    outT = (P.dram(pfx + "gT", [4, 128, NTOK], BF16, kind="ExternalOutput") if io else None)
    return dict(xin=xin, modT_d=modT_d, w=w, cw_d=cw_d, dsk_d=dsk_d, w1_d=w1_d, w2_d=w2_d, w3_d=w3_d, pb_d=pb_d, fL=fL, fC=fC, dL=dL, dC=dC, tabL=tabL, tabC=tabC, outT=outT)


def body_hy(cx, xin, modT, T, outT):
    P = cx.P
    w = T["w"]
    cw_d = T["cw_d"]
    dsk_d = T["dsk_d"]
    w1_d = T["w1_d"]
    w2_d = T["w2_d"]
    w3_d = T["w3_d"]
    pb_d = T["pb_d"]
    fL = T["fL"]
    fC = T["fC"]
    dL = T["dL"]
    dC = T["dC"]
    tabL = T["tabL"]
    tabC = T["tabC"]
    _psf_saved = list(cx.psf)
    cw = P.sb([128, 12, 4], F32, "cw")
    dsk = P.sb([128, 4], F32, "dsk")
    P.dma(cw[:, :, :], cw_d.v)
    P.dma(dsk[:, :], dsk_d.v)
    w1 = P.sb([33, 64], F32, "w1")
    w2 = P.sb([64, 64], F32, "w2")
    w3 = P.sb([64, 1024], F32, "w3")
    pb = P.sb([64, 4], F32, "pb")
    for s_, d_ in ((w1, w1_d), (w2, w2_d), (w3, w3_d), (pb, pb_d)):
        P.dma(s_.v, d_.v)
    x0z = P.sb([128, 4, NTOK], BF16, "x0z")
    ufm = P.sb([128, 4, NTOK], BF16, "ufm")
    ut = P.sb([128, NTILE, 512], BF16, "ut")
    with P.scope():
        hT = emit_h(cx, xin, modT)
        cx.alloc_w()
        pre = P.sb([128, NTOK], F32, "pre")
        cv = P.sb([128, NTOK], F32, "cvtmp")
        x1c = P.sb([128, 4, NTOK], BF16, "x1c")

        def evac(ps, c, cwid, t0, tl):
            j = c // 128
            if j >= 12:
                P.op("act", "activation", pre[:, t0:t0 + tl], ps[:, :tl], AF.Silu)
                P.op("dve", "tensor_tensor", x0z[:, j - 12, t0:t0 + tl], x0z[:, j - 12, t0:t0 + tl], pre[:, t0:t0 + tl], ALU.mult)
                return
            P.op("act", "activation", pre[:, t0:t0 + tl], ps[:, :tl], AF.Identity)
            if t0 + tl == NTOK:
                dst = x0z[:, j, :] if j < 4 else (x1c[:, j - 4, :] if j < 8 else cv[:, :])
                for (a, e) in ((0, NCTX), (NCTX, NTOK)):
                    conv3(P, "pool", dst, pre, cw[:, j, 0:3], cw[:, j, 3:4], a, e)
                if j >= 8:
                    cc = j - 8
                    P.op("dve", "tensor_tensor", ufm[:, cc, :], x1c[:, cc, :], cv[:, :], ALU.mult)
                    for t in range(NTILE):
                        if t % 8 == 0:
                            pt = cx.tbank()
                        P.op("pe", "transpose", pt[:, (t % 8) * 128:(t % 8 + 1) * 128], ufm[:, cc, t * 128:(t + 1) * 128], cx.ident[:, :])
                        if t % 8 == 7 or t == NTILE - 1:
                            tb = (t // 8) * 8
                            nt = t - tb + 1
                            P.op("act", "activation", ut[:, tb:tb + nt, cc * 128:(cc + 1) * 128],
                                 pt[:, :nt * 128].rearrange("p (t c) -> p t c", t=nt), AF.Identity)

        project(cx, hT, w, 0, 2048, "fm", evac)
    for (n, t0, fd, dd, tabs) in ((256, 0, fC, dC, tabC), (2048, NCTX, fL, dL, tabL)):
        with P.scope():
            gp = P.sb([128, n // 128, 512], BF16, f"gp{n}")
            gm = P.sb([128, n // 128, 512], BF16, f"gm{n}")
            with P.scope():
                hy_filter(cx, n, fd, dd, (w1, w2, w3, pb), gp, gm)
            hy_seq(cx, n, t0, tabs, gp, gm, ut, ufm, x0z, dsk, outT)
    cx.psf = _psf_saved


def build_hy():
    nc = bass.Bass("TRN2", target_bir_lowering=False)
    P = Prog(nc)
    T = decl_hy(P, "")
    cx = Cx(P, wcols=128)
    modT = load_modT(cx, T['modT_d'])
    body_hy(cx, T['xin'], modT, T, T['outT'])
    P.emit()
    return nc


def rope_blocks(P, out_bf, src, CS, SN, tA, tB, nblk, tl, c0):
    n = 64 * nblk
    P.op("dve", "tensor_tensor", tA[:n, :tl], src[:n, :tl], CS[:n, c0:c0 + tl], ALU.mult)
    for b in range(nblk):
        lo, hi = slice(64 * b, 64 * b + 32), slice(64 * b + 32, 64 * b + 64)
        P.op("pool", "tensor_tensor", tB[lo, :tl], src[hi, :tl], SN[hi, c0:c0 + tl], ALU.mult)
        P.op("pool", "tensor_tensor", tB[hi, :tl], src[lo, :tl], SN[lo, c0:c0 + tl], ALU.mult)
        P.op("dve", "tensor_tensor", out_bf[lo, :tl], tA[lo, :tl], tB[lo, :tl], ALU.subtract)
        P.op("dve", "tensor_tensor", out_bf[hi, :tl], tA[hi, :tl], tB[hi, :tl], ALU.add)


def decl_mla(P, pfx, io=True):
    xin = (P.dram(pfx + "xin", [NTOK, D], F32, kind="ExternalInput") if io else None)
    modT_d = (P.dram(pfx + "modT", [128, 64], F32, kind="ExternalInput") if io else None)
    w = P.dram(pfx + "w", [D, 1408], F32, kind="ExternalInput")
    wuq_d = P.dram(pfx + "wuq", [128, 4, 768], F32, kind="ExternalInput")
    wukv_d = P.dram(pfx + "wukv", [128, 2, 1024], F32, kind="ExternalInput")
    nrm_d = P.dram(pfx + "nrm", [128, 6], F32, kind="ExternalInput")
    cs_d = P.dram(pfx + "ropeC", [128, NSEQ], F32, kind="ExternalInput")
    sn_d = P.dram(pfx + "ropeS", [128, NSEQ], F32, kind="ExternalInput")
    outT = (P.dram(pfx + "gT", [4, 128, NTOK], BF16, kind="ExternalOutput") if io else None)
    return dict(xin=xin, modT_d=modT_d, w=w, wuq_d=wuq_d, wukv_d=wukv_d, nrm_d=nrm_d, cs_d=cs_d, sn_d=sn_d, outT=outT)


def body_mla(cx, xin, modT, T, outT):
    P = cx.P
    w = T["w"]
    wuq_d = T["wuq_d"]
    wukv_d = T["wukv_d"]
    nrm_d = T["nrm_d"]
    cs_d = T["cs_d"]
    sn_d = T["sn_d"]
    _psf_saved = list(cx.psf)
    zs = P.sb([128, 4, NTOK], BF16, "zs")
    kpT = P.sb([128, NTOK], BF16, "kpT")
    qnT = P.sb([128, 4, NTOK], BF16, "qnT")
    qpT = P.sb([128, 2, NTOK], BF16, "qpT")
    knT = P.sb([128, 4, NTOK], BF16, "knT")
    vtm = P.sb([128, NTILE, 512], BF16, "vtm")
    with P.scope():
        cqT = P.sb([128, 4, NTOK], BF16, "cqT")
        ckvT = P.sb([128, 2, NTOK], BF16, "ckvT")
        kpf = P.sb([128, NTOK], F32, "kpf")
        for (ta, tb) in ((0, 9), (9, 18)):
            with P.scope():
                hT = emit_h(cx, xin, modT, ta, tb)
                cx.alloc_w()

                def evac(ps, c, cwid, t0, tl):
                    j = c // 128
                    if j < 4:
                        P.op("act", "activation", cqT[:, j, t0:t0 + tl], ps[:, :tl], AF.Identity)
                    elif j < 6:
                        P.op("act", "activation", ckvT[:, j - 4, t0:t0 + tl], ps[:, :tl], AF.Identity)
                    elif j == 6:
                        P.op("act", "activation", kpf[:, t0:t0 + tl], ps[:, :tl], AF.Identity)
                    else:
                        P.op("act", "activation", zs[:, j - 7, t0:t0 + tl], ps[:, :tl], AF.Silu)

                project(cx, hT, w, 0, 1408, "fm", evac, t_lo=ta, t_hi=tb)
        CS = P.sb([128, NSEQ], F32, "CS")
        SN = P.sb([128, NSEQ], F32, "SN")
        P.dma(CS[:, :], cs_d.v)
        P.dma(SN[:, :], sn_d.v, q="pool")
        nrm = P.sb([128, 6], F32, "nrm")
        P.dma(nrm[:, :], nrm_d.v)
        wq = P.sb([128, 4, 768], BF16, "wq")
        wkv = P.sb([128, 2, 1024], BF16, "wkv")
        with P.scope():
            wq_f = P.sb([128, 4, 768], F32, "wq_f")
            wkv_f = P.sb([128, 2, 1024], F32, "wkv_f")
            P.dma(wq_f[:, :, :], wuq_d.v)
            P.dma(wkv_f[:, :, :], wukv_d.v, q="pool")
            for r in range(4):
                P.op("dve", "tensor_scalar", wq[:, r, :], wq_f[:, r, :], nrm[:, r:r + 1], None, ALU.mult)
            for r in range(2):
                P.op("dve", "tensor_scalar", wkv[:, r, :], wkv_f[:, r, :], nrm[:, 4 + r:5 + r], None, ALU.mult)
        rq = P.sb([128, NTOK], F32, "rq")
        rk_ = P.sb([128, NTOK], F32, "rk")
        rktm = P.sb([128, NTILE], F32, "rktm")
        tgs = tgroups(0, NTILE)
        sq = P.sb([128, 4, 512], BF16, "sq")
        skv = P.sb([128, 2, 512], BF16, "skv")
        allps = list(cx.psf)
        pstm = allps[5]
        cx.psf = allps[:5]
        for (t0, tl) in tgs:
            P.op("act", "activation", sq[:, :, :tl], cqT[:, :, t0:t0 + tl], AF.Square)
            P.op("act", "activation", skv[:, :, :tl], ckvT[:, :, t0:t0 + tl], AF.Square)
            ps = cx.bank()
            for r in range(4):
                P.op("pe", "matmul", ps[:, :tl], cx.onesb[:, :], sq[:, r, :tl], start=(r == 0), stop=(r == 3))
            P.op("act", "activation", rq[:, t0:t0 + tl], ps[:, :tl], AF.Sqrt, bias=cx.eps(), scale=1.0 / 512)
            ps = cx.bank()
            for r in range(2):
                P.op("pe", "matmul", ps[:, :tl], cx.onesb[:, :], skv[:, r, :tl], start=(r == 0), stop=(r == 1))
            P.op("act", "activation", rk_[:, t0:t0 + tl], ps[:, :tl], AF.Sqrt, bias=cx.eps(), scale=1.0 / 256)
            for ti in range(tl // 128):
                t = t0 // 128 + ti
                for r in range(2):
                    P.op("pe", "matmul", pstm[:, t:t + 1], skv[:, r, ti * 128:(ti + 1) * 128], cx.onesb[:, 0:1], start=(r == 0), stop=(r == 1))
        P.op("act", "activation", rktm[:, :], pstm[:, :NTILE], AF.Sqrt, bias=cx.eps(), scale=1.0 / 256)
        P.op("dve", "reciprocal", rq[:, :], rq[:, :])
        P.op("dve", "reciprocal", rk_[:, :], rk_[:, :])
        P.op("dve", "reciprocal", rktm[:, :], rktm[:, :])
        P.op("pool", "tensor_scalar", rq[:, :], rq[:, :], float(192 ** -0.5), None, ALU.mult)
        tA = P.sb([128, 512], F32, "ropeA")
        tB = P.sb([128, 512], F32, "ropeB")
        tS = P.sb([128, 512], F32, "ropeSrc")
        for (t0, tl) in tgs:
            lat = t0 >= NCTX
            for hh in range(4):
                ps = cx.bank()
                for r in range(4):
                    P.op("pe", "matmul", ps[:, :tl], wq[:, r, hh * 128:(hh + 1) * 128], cqT[:, r, t0:t0 + tl], start=(r == 0), stop=(r == 3))
                P.op("dve", "tensor_tensor", qnT[:, hh, t0:t0 + tl], ps[:, :tl], rq[:, t0:t0 + tl], ALU.mult)
                ps = cx.bank()
                for r in range(2):
                    P.op("pe", "matmul", ps[:, :tl], wkv[:, r, hh * 128:(hh + 1) * 128], ckvT[:, r, t0:t0 + tl], start=(r == 0), stop=(r == 1))
                P.op("dve", "tensor_tensor", knT[:, hh, t0:t0 + tl], ps[:, :tl], rk_[:, t0:t0 + tl], ALU.mult)
            for pr in range(2):
                ps = cx.bank()
                for r in range(4):
                    P.op("pe", "matmul", ps[:, :tl], wq[:, r, 512 + pr * 128:512 + (pr + 1) * 128], cqT[:, r, t0:t0 + tl], start=(r == 0), stop=(r == 3))
                if lat:
                    P.op("dve", "tensor_tensor", tS[:, :tl], ps[:, :tl], rq[:, t0:t0 + tl], ALU.mult)
                    rope_blocks(P, qpT[:, pr, t0:t0 + tl], tS, CS, SN, tA, tB, 2, tl, t0 - NCTX)
                else:
                    P.op("dve", "tensor_tensor", qpT[:, pr, t0:t0 + tl], ps[:, :tl], rq[:, t0:t0 + tl], ALU.mult)
            if lat:
                rope_blocks(P, kpT[:, t0:t0 + tl], kpf[:, t0:t0 + tl], CS, SN, tA, tB, 2, tl, t0 - NCTX)
            else:
                P.op("dve", "tensor_copy", kpT[:, t0:t0 + tl], kpf[:, t0:t0 + tl])
        for t in range(NTILE):
            ps = cx.bank()
            for r in range(2):
                P.op("pe", "matmul", ps[:, :], ckvT[:, r, t * 128:(t + 1) * 128], wkv[:, r, 512:1024], start=(r == 0), stop=(r == 1))
            P.op("act", "activation", vtm[:, t, :], ps[:, :], AF.Identity, scale=rktm[:, t:t + 1])
    pts = [P.sb([128, 512], BF16, f"pT{i}") for i in range(3)]
    rec = P.sb([128, 512], F32, "rec")
    yo = [P.sb([128, 512], F32, f"yo{i}") for i in range(2)]
    yb = [P.sb([128, 512], BF16, f"yb{i}") for i in range(2)]
    acc_y = allps[4]
    acc_s = allps[5]
    cx.psf = allps[:4]
    ip = 0
    io = 0
    for hh in range(4):
        pb0 = 64 * (hh % 2)
        for (q0, ql) in tgs:
            nkt = 2 if q0 < NCTX else NTILE
            for kt in range(nkt):
                ps = cx.bank()
                ks = slice(kt * 128, (kt + 1) * 128)
                P.op("pe", "matmul", ps[:, :ql], knT[:, hh, ks], qnT[:, hh, q0:q0 + ql], start=True, stop=False)
                P.op("pe", "matmul", ps[:, :ql], kpT[pb0:pb0 + 64, ks], qpT[pb0:pb0 + 64, hh // 2, q0:q0 + ql], start=False, stop=True)
                pT = pts[ip % 3]
                ip += 1
                P.op("act", "activation", pT[:, :ql], ps[:, :ql], AF.Exp)
                P.op("pe", "matmul", acc_y[:, :ql], vtm[:, kt, hh * 128:(hh + 1) * 128], pT[:, :ql], start=(kt == 0), stop=(kt == nkt - 1))
                P.op("pe", "matmul", acc_s[:, :ql], cx.onesb[:, :], pT[:, :ql], start=(kt == 0), stop=(kt == nkt - 1))
            o, o2 = yo[io % 2], yb[io % 2]
            io += 1
            P.op("dve", "reciprocal", rec[:, :ql], acc_s[:, :ql])
            P.op("dve", "tensor_tensor", o[:, :ql], acc_y[:, :ql], rec[:, :ql], ALU.mult)
            P.op("pool", "tensor_tensor", o2[:, :ql], o[:, :ql], zs[:, hh, q0:q0 + ql], ALU.mult)
            P.dma(outT[hh, :, q0:q0 + ql], o2[:, :ql], is_out=True)
    cx.psf = _psf_saved


def build_mla():
    nc = bass.Bass("TRN2", target_bir_lowering=False)
    P = Prog(nc)
    T = decl_mla(P, "")
    cx = Cx(P, wcols=128)
    modT = load_modT(cx, T['modT_d'])
    body_mla(cx, T['xin'], modT, T, T['outT'])
    P.emit()
    return nc


def decl_gqa(P, pfx, io=True):
    xin = (P.dram(pfx + "xin", [NTOK, D], F32, kind="ExternalInput") if io else None)
    modT_d = (P.dram(pfx + "modT", [128, 64], F32, kind="ExternalInput") if io else None)
    w = P.dram(pfx + "w", [D, 1280], F32, kind="ExternalInput")
    cs_d = P.dram(pfx + "ropeC", [128, NSEQ], F32, kind="ExternalInput")
    sn_d = P.dram(pfx + "ropeS", [128, NSEQ], F32, kind="ExternalInput")
    sink_d = P.dram(pfx + "sink", [128, 4], F32, kind="ExternalInput")
    msk_d = P.dram(pfx + "msk", [128, 2, 128], BF16, kind="ExternalInput")
    outT = (P.dram(pfx + "gT", [4, 128, NTOK], BF16, kind="ExternalOutput") if io else None)
    return dict(xin=xin, modT_d=modT_d, w=w, cs_d=cs_d, sn_d=sn_d, sink_d=sink_d, msk_d=msk_d, outT=outT)


def body_gqa(cx, xin, modT, T, outT):
    P = cx.P
    w = T["w"]
    cs_d = T["cs_d"]
    sn_d = T["sn_d"]
    sink_d = T["sink_d"]
    msk_d = T["msk_d"]
    _psf_saved = list(cx.psf)
    zs = P.sb([128, 4, NTOK], BF16, "zs")
    qT = P.sb([128, 4, NTOK], BF16, "qT")
    kT = P.sb([128, NTOK], BF16, "kT")
    vT = P.sb([128, NTOK], BF16, "vT")
    vA = P.sb([128, NTILE, 128], BF16, "vA")
    vB = P.sb([128, NTILE, 128], BF16, "vB")
    P.op("pool", "memset", vA[:, :, :], 0.0)
    P.op("pool", "memset", vB[:, :, :], 0.0)
    onesA = P.sb([128, 128], BF16, "onesA")
    onesB = P.sb([128, 128], BF16, "onesB")
    P.op("pool", "memset", onesA[:, :], 0.0)
    P.op("pool", "memset", onesB[:, :], 0.0)
    P.op("pool", "memset", onesA[:, 0:64], 1.0)
    P.op("pool", "memset", onesB[:, 64:128], 1.0)
    msk = P.sb([128, 2, 128], BF16, "msk")
    P.dma(msk[:, :, :], msk_d.v)
    esink = P.sb([128, 4], F32, "esink")
    P.dma(esink[:, :], sink_d.v)
    P.op("act", "activation", esink[:, :], esink[:, :], AF.Exp)
    with P.scope():
        CS = P.sb([128, NSEQ], F32, "CS")
        SN = P.sb([128, NSEQ], F32, "SN")
        P.dma(CS[:, :], cs_d.v)
        P.dma(SN[:, :], sn_d.v, q="pool")
        tA = P.sb([128, 512], F32, "ropeA")
        tB = P.sb([128, 512], F32, "ropeB")
        tS = P.sb([128, 512], F32, "ropeSrc")
        for (ta, tb) in ((0, 9), (9, 18)):
            with P.scope():
                hT = emit_h(cx, xin, modT, ta, tb)
                cx.alloc_w()

                def evac(ps, c, cwid, t0, tl):
                    j = c // 128
                    if j < 5:
                        dst = qT[:, j, t0:t0 + tl] if j < 4 else kT[:, t0:t0 + tl]
                        sc = 0.125 if j < 4 else 1.0
                        if t0 >= NCTX:
                            P.op("act", "activation", tS[:, :tl], ps[:, :tl], AF.Identity, scale=sc)
                            rope_blocks(P, dst, tS, CS, SN, tA, tB, 2, tl, t0 - NCTX)
                        else:
                            P.op("act", "activation", dst, ps[:, :tl], AF.Identity, scale=sc)
                    elif j < 9:
                        P.op("act", "activation", zs[:, j - 5, t0:t0 + tl], ps[:, :tl], AF.Silu)
                    else:
                        P.op("act", "activation", vT[:, t0:t0 + tl], ps[:, :tl], AF.Identity)

                project(cx, hT, w, 0, 1280, "fm", evac, t_lo=ta, t_hi=tb)
        for t in range(NTILE):
            if t % 8 == 0:
                pt = cx.tbank()
            P.op("pe", "transpose", pt[:, (t % 8) * 128:(t % 8 + 1) * 128], vT[:, t * 128:(t + 1) * 128], cx.ident[:, :])
            if t % 8 == 7 or t == NTILE - 1:
                tb_ = (t // 8) * 8
                nt = t - tb_ + 1
                ptv = pt[:, :nt * 128].rearrange("p (t c) -> p t c", t=nt)
                P.op("act", "activation", vA[:, tb_:tb_ + nt, 0:64], ptv[:, :, 0:64], AF.Identity)
                P.op("act", "activation", vB[:, tb_:tb_ + nt, 64:128], ptv[:, :, 64:128], AF.Identity)
    qTo = P.sb([128, 4, NTOK], BF16, "qTo")
    P.op("pool", "tensor_copy", qTo[:, :, :], qT[:, :, :])
    P.op("pool", "memset", qTo[0:64, :, :], 0.0)
    P.op("dve", "memset", qT[64:128, :, :], 0.0)
    pts = [[P.sb([128, 512], BF16, f"pT{e}_{i}") for i in range(2)] for e in range(2)]
    den = P.sb([128, 512], F32, "den")
    yo = P.sb([128, 512], F32, "yo")
    stg = [P.sb([128, 4, 512], BF16, f"stg{i}") for i in range(2)]
    ogroups = [(0, 2)] + [(2 + 4 * i, 4) for i in range(4)]
    acc_y, acc_s = cx.psf[4], cx.psf[5]
    cx.psf = cx.psf[:4]
    ip = 0
    for qb in range(NTILE):
        q0 = qb * 128
        if qb < 2:
            chunks = [(0, None), (1, None)]
        else:
            chunks = []
            if qb > 2:
                chunks.append((qb - 1, 0))
            chunks.append((qb, None))
            if qb < NTILE - 1:
                chunks.append((qb + 1, 1))
            chunks += [(0, None), (1, None)]
        for ci, (kt, mk) in enumerate(chunks):
            ks = slice(kt * 128, (kt + 1) * 128)
            pTs = []
            for e in range(2):
                ps = cx.bank()
                src = qT if e == 0 else qTo
                P.op("pe", "matmul", ps[:, :].rearrange("p (j q) -> p j q", j=4), kT[:, ks], src[:, :, q0:q0 + 128], start=True, stop=True)
                pT = pts[e][ip % 2]
                P.op("act", "activation", pT[:, :], ps[:, :], AF.Exp)
                if mk is not None:
                    P.op(("dve" if e == 0 else "pool"), "tensor_tensor", pT[:, :].rearrange("p (j q) -> p j q", j=4),
                         pT[:, :].rearrange("p (j q) -> p j q", j=4), msk[:, mk:mk + 1, :].to_broadcast([128, 4, 128]), ALU.mult)
                pTs.append(pT)
            ip += 1
            first, last = (ci == 0), (ci == len(chunks) - 1)
            P.op("pe", "matmul", acc_y[:, :], vA[:, kt, :], pTs[0][:, :], start=first, stop=False)
            P.op("pe", "matmul", acc_y[:, :], vB[:, kt, :], pTs[1][:, :], start=False, stop=last)
            P.op("pe", "matmul", acc_s[:, :], onesA[:, :], pTs[0][:, :], start=first, stop=False)
            P.op("pe", "matmul", acc_s[:, :], onesB[:, :], pTs[1][:, :], start=False, stop=last)
        gi = [i for i, (a, n_) in enumerate(ogroups) if a <= qb < a + n_][0]
        ga, gn = ogroups[gi]
        sg = stg[gi % 2]
        v3 = lambda t: t[:, :].rearrange("p (j q) -> p j q", j=4)
        P.op("dve", "tensor_tensor", v3(den), v3(acc_s), esink[:, :].unsqueeze(2).to_broadcast([128, 4, 128]), ALU.add)
        P.op("dve", "reciprocal", den[:, :], den[:, :])
        P.op("dve", "tensor_tensor", yo[:, :], acc_y[:, :], den[:, :], ALU.mult)
        P.op("pool", "tensor_tensor", sg[:, :, (qb - ga) * 128:(qb - ga + 1) * 128], v3(yo), zs[:, :, q0:q0 + 128], ALU.mult)
        if qb == ga + gn - 1:
            for jj in range(4):
                P.dma(outT[jj, :, ga * 128:(ga + gn) * 128], sg[:, jj, :gn * 128], is_out=True)
    cx.psf = _psf_saved


def build_gqa():
    nc = bass.Bass("TRN2", target_bir_lowering=False)
    P = Prog(nc)
    T = decl_gqa(P, "")
    cx = Cx(P, wcols=128)
    modT = load_modT(cx, T['modT_d'])
    body_gqa(cx, T['xin'], modT, T, T['outT'])
    P.emit()
    return nc


def decl_ssd(P, pfx, io=True):
    xin = (P.dram(pfx + "xin", [NTOK, D], F32, kind="ExternalInput") if io else None)
    modT_d = (P.dram(pfx + "modT", [128, 64], F32, kind="ExternalInput") if io else None)
    w = P.dram(pfx + "w", [D, 1408], F32, kind="ExternalInput")
    cw_d = P.dram(pfx + "cw", [128, 6, 4], F32, kind="ExternalInput")
    rows_d = P.dram(pfx + "rows", [128, 16 + 16 + 8 + 512], F32, kind="ExternalInput")
    msk_d = P.dram(pfx + "msk", [128, 4, 128], F32, kind="ExternalInput")
    outT = (P.dram(pfx + "gT", [4, 128, NTOK], BF16, kind="ExternalOutput") if io else None)
    return dict(xin=xin, modT_d=modT_d, w=w, cw_d=cw_d, rows_d=rows_d, msk_d=msk_d, outT=outT)


def body_ssd(cx, xin, modT, T, outT):
    P = cx.P
    w = T["w"]
    cw_d = T["cw_d"]
    rows_d = T["rows_d"]
    msk_d = T["msk_d"]
    _psf_saved = list(cx.psf)
    cx.wcols = 256
    cw = P.sb([128, 6, 4], F32, "cw")
    rows = P.sb([128, 552], F32, "rows")
    msk = P.sb([128, 4, 128], F32, "msk")
    P.dma(cw[:, :, :], cw_d.v)
    P.dma(rows[:, :], rows_d.v)
    P.dma(msk[:, :, :], msk_d.v)
    onesf = P.sb([128, 128], F32, "onesf")
    P.op("dve", "memset", onesf[:, :], 1.0)
    abc = P.sb([128, 16], F32, "abc")
    P.op("act", "activation", abc[:, :], rows[:, 16:32], AF.Exp)
    P.op("dve", "tensor_scalar", abc[:, :], abc[:, :], -1.0, None, ALU.mult)
    xs = P.sb([128, NTILE, 512], BF16, "xs_tm")
    Btm = P.sb([128, NTILE, 128], BF16, "B_tm")
    BT = P.sb([128, NTOK], BF16, "BT")
    CT = P.sb([128, NTOK], BF16, "CT")
    zs = P.sb([128, NTILE, 512], BF16, "zs_tm")
    dt = P.sb([128, NTILE, 16], F32, "dt")
    la = P.sb([128, NTILE, 16], F32, "la")
    with P.scope():
        pre = P.sb([128, 6, NTOK], BF16, "pre")
        xT = P.sb([128, 4, NTOK], BF16, "xT")
        cvt = P.sb([128, NTOK], F32, "cvt")
        for (ta, tb) in ((0, 9), (9, 18)):
            with P.scope():
                hT = emit_h(cx, xin, modT, ta, tb)
                cx.alloc_w()

                def evac(ps, c, cwid, t0, tl):
                    P.op("act", "activation", pre[:, c // 128, t0:t0 + tl], ps[:, :tl], AF.Identity)

                def evac_tm(ps, c, cwid, t0, tl):
                    t = t0 // 128
                    if c < 512:
                        P.op("act", "activation", zs[:, t, c:c + cwid], ps[:, :cwid], AF.Silu)
                    else:
                        P.op("dve", "tensor_tensor", dt[:, t, :], ps[:, 0:16], rows[:, 0:16], ALU.add)

                project(cx, hT, w, 0, 768, "fm", evac, t_lo=ta, t_hi=tb)
                project(cx, hT, w, 768, 640, "tm", evac_tm, t_lo=ta, t_hi=tb)
        for j in range(6):
            for (a, e) in ((0, NCTX), (NCTX, NTOK)):
                conv3(P, "pool", cvt, pre[:, j, :], cw[:, j, 0:3], cw[:, j, 3:4], a, e)
            dst = xT[:, j, :] if j < 4 else (BT[:, :] if j == 4 else CT[:, :])
            P.op("act", "activation", dst, cvt[:, :], AF.Silu)
        for t in range(NTILE):
            pt = cx.tbank()
            for cc in range(4):
                P.op("pe", "transpose", pt[:, cc * 128:(cc + 1) * 128], xT[:, cc, t * 128:(t + 1) * 128], cx.ident[:, :])
            P.op("pe", "transpose", pt[:, 512:640], BT[:, t * 128:(t + 1) * 128], cx.ident[:, :])
            P.op("act", "activation", xs[:, t, :], pt[:, 0:512], AF.Identity)
            P.op("dve", "tensor_copy", Btm[:, t, :], pt[:, 512:640])
    P.op("act", "activation", dt[:, :, :], dt[:, :, :], AF.Exp)
    P.op("act", "activation", dt[:, :, :], dt[:, :, :], AF.Ln, bias=1.0, scale=1.0)
    P.op("dve", "tensor_tensor", la[:, :, :], dt[:, :, :], abc[:, :].unsqueeze(1).to_broadcast([128, NTILE, 16]), ALU.mult)
    with P.scope():
        yacc = P.sb([128, NTILE, 512], F32, "yacc")
        def bset(d):
            return dict(
                R=P.sb([128, 4, 128], F32, f"R{d}"), arg=P.sb([128, 8, 128], F32, f"arg{d}"), E=P.sb([128, 8, 128], F32, f"E{d}"),
                MT=P.sb([128, 8, 128], BF16, f"MT{d}"), Ec=P.sb([128, 8, 128], F32, f"Ecum{d}"), CdT=P.sb([128, 8, 128], BF16, f"CdT{d}"),
                sc=P.sb([128, 128], F32, f"scoresT{d}"), cum=P.sb([128, 8], F32, f"cum_tm{d}"), dte=P.sb([128, 8], F32, f"dte{d}"),
                xd=P.sb([128, 512], BF16, f"xd{d}"), xdw=P.sb([128, 512], BF16, f"xdw{d}"), xdf=P.sb([128, 512], F32, f"xdf{d}"),
                ST=P.sb([128, 8, 64], F32, f"ST{d}"), STt=P.sb([128, 8, 64], F32, f"STt{d}"), STb=P.sb([128, 8, 64], BF16, f"STb{d}"),
                pA=cx.psf[3 * d], pB=cx.psf[3 * d + 1], pC=cx.psf[3 * d + 2])
        BS = [bset(0), bset(1)]
        for t in range(NTILE):
            P.op("pool", "tensor_tensor", yacc[:, t, :].rearrange("p (j q) -> p j q", j=8),
                 xs[:, t, :].rearrange("p (j q) -> p j q", j=8),
                 rows[:, 32:40].unsqueeze(2).to_broadcast([128, 8, 64]), ALU.mult)
        orders = [[0, 1] + list(range(2, NTILE)), [1, 0] + list(range(NTILE - 1, 1, -1))]
        for d in range(2):
            P.op("dve", "memset", BS[d]["ST"][:, :, :], 0.0)
            P.op("dve", "memset", BS[d]["STb"][:, :, :], 0.0)

        def ssd_step(d, c):
            B = BS[d]
            R, arg, E, MT, Ec, CdT, sc, cum, dte = B["R"], B["arg"], B["E"], B["MT"], B["Ec"], B["CdT"], B["sc"], B["cum"], B["dte"]
            xd, xdw, xdf, ST, STt, STb = B["xd"], B["xdw"], B["xdf"], B["ST"], B["STt"], B["STb"]
            pA, pB, pC = B["pA"], B["pB"], B["pC"]
            tri = msk[:, d, :]
            neg = msk[:, 2 + d, :]
            edge = 127 if d == 0 else 0
            ts_ = slice(c * 128, (c + 1) * 128)
            lac = la[:, c, 8 * d:8 * d + 8]
            P.op("pe", "matmul", pA[:, 0:8], tri, lac, start=True, stop=True)
            P.op("pe", "matmul", pA[:, 128:256], BT[:, ts_], CT[:, ts_], start=True, stop=True)
            P.op("act", "activation", cum[:, :], pA[:, 0:8], AF.Identity)
            P.op("act", "activation", sc[:, :], pA[:, 128:256], AF.Identity)
            for hf in range(2):
                js = slice(4 * hf, 4 * hf + 4)
                P.op("dve", "tensor_tensor", R[:, :, :], lac[:, js].unsqueeze(2).to_broadcast([128, 4, 128]),
                     tri.unsqueeze(1).to_broadcast([128, 4, 128]), ALU.mult)
                P.op("pe", "matmul", pB[:, :], onesf[:, :], R[:, :, :].rearrange("p j l -> p (j l)"), start=True, stop=True)
                prv = pB[:, :].rearrange("p (j l) -> p j l", j=4)
                P.op("dve", "tensor_tensor", arg[:, js, :], prv, cum[:, js].unsqueeze(2).to_broadcast([128, 4, 128]), ALU.subtract)
                P.op("dve", "tensor_tensor", dte[:, js], prv[:, :, edge], cum[:, js], ALU.subtract)
                P.op("act", "activation", Ec[:, js, :], prv, AF.Exp)
            P.op("dve", "tensor_tensor", arg[:, :, :], arg[:, :, :], neg.unsqueeze(1).to_broadcast([128, 8, 128]), ALU.add)
            P.op("act", "activation", E[:, :, :], arg[:, :, :], AF.Exp)
            P.op("dve", "tensor_tensor", MT[:, :, :], E[:, :, :], sc[:, :].unsqueeze(1).to_broadcast([128, 8, 128]), ALU.mult)
            P.op("pool", "tensor_tensor", CdT[:, :, :], Ec[:, :, :], CT[:, ts_].unsqueeze(1).to_broadcast([128, 8, 128]), ALU.mult)
            P.op("dve", "tensor_tensor", xdf[:, :].rearrange("p (j q) -> p j q", j=8), xs[:, c, :].rearrange("p (j q) -> p j q", j=8),
                 dt[:, c, 8 * d:8 * d + 8].unsqueeze(2).to_broadcast([128, 8, 64]), ALU.mult)
            P.op("pool", "tensor_copy", xd[:, :], xdf[:, :])
            P.op("act", "activation", dte[:, :], dte[:, :], AF.Exp)
            P.op("dve", "tensor_tensor", xdw[:, :].rearrange("p (j q) -> p j q", j=8), xdf[:, :].rearrange("p (j q) -> p j q", j=8),
                 dte[:, :].unsqueeze(2).to_broadcast([128, 8, 64]), ALU.mult)
            for j in range(8):
                ys = pC[:, j * 64:(j + 1) * 64]
                P.op("pe", "matmul", ys, MT[:, j, :], xd[:, j * 64:(j + 1) * 64], start=True, stop=False)
                P.op("pe", "matmul", ys, CdT[:, j, :], STb[:, j, :], start=False, stop=True)
            P.op("dve", "tensor_tensor", yacc[:, c, :], yacc[:, c, :], pC[:, :], ALU.add)
            P.op("pe", "matmul", pC[:, :], Btm[:, c, :], xdw[:, :], start=True, stop=True)
            P.op("pool", "tensor_tensor", STt[:, :, :], ST[:, :, :], Ec[:, :, edge:edge + 1].to_broadcast([128, 8, 64]), ALU.mult)
            P.op("dve", "tensor_tensor", ST[:, :, :], STt[:, :, :], pC[:, :].rearrange("p (j q) -> p j q", j=8), ALU.add)
            P.op("act", "activation", STb[:, :, :], ST[:, :, :], AF.Identity)

        for step in range(NTILE):
            for d in range(2):
                ssd_step(d, orders[d][step])
        g = P.sb([128, 512], F32, "g")
        g2 = P.sb([128, 512], F32, "g2")
        gb = P.sb([128, 512], BF16, "gb")
        ssq = P.sb([128, 1], F32, "ssq")
        stg = [P.sb([128, 4, 512], BF16, f"stg{i}") for i in range(2)]
        ogroups = [(0, 2)] + [(2 + 4 * i, 4) for i in range(4)]
        for gi, (ga, gn) in enumerate(ogroups):
            sg = stg[gi % 2]
            for t in range(ga, ga + gn):
                P.op("dve", "tensor_tensor", g[:, :], yacc[:, t, :], zs[:, t, :], ALU.mult)
                P.op("pool", "tensor_tensor", g2[:, :], g[:, :], g[:, :], ALU.mult)
                P.op("dve", "reduce_sum", ssq[:, :], g2[:, :], AX.X)
                P.op("act", "activation", ssq[:, :], ssq[:, :], AF.Sqrt, bias=cx.eps(), scale=1.0 / 512)
                P.op("dve", "reciprocal", ssq[:, :], ssq[:, :])
                P.op("dve", "scalar_tensor_tensor", gb[:, :], g[:, :], ssq[:, 0:1], rows[:, 40:552], ALU.mult, ALU.mult)
                pt = cx.tbank()
                for cc in range(4):
                    P.op("pe", "transpose", pt[:, cc * 128:(cc + 1) * 128], gb[:, cc * 128:(cc + 1) * 128], cx.ident[:, :])
                P.op("act", "activation", sg[:, :, (t - ga) * 128:(t - ga + 1) * 128], pt[:, 0:512].rearrange("p (c q) -> p c q", c=4), AF.Identity)
            for cc in range(4):
                P.dma(outT[cc, :, ga * 128:(ga + gn) * 128], sg[:, cc, :gn * 128], is_out=True)
    cx.psf = _psf_saved
    cx.wcols = 128


def build_ssd():
    nc = bass.Bass("TRN2", target_bir_lowering=False)
    P = Prog(nc)
    T = decl_ssd(P, "")
    cx = Cx(P, wcols=128)
    modT = load_modT(cx, T['modT_d'])
    body_ssd(cx, T['xin'], modT, T, T['outT'])
    P.emit()
    return nc


MT_TOK = 1152
MTG = [(0, 128), (128, 512), (640, 512)]
ALPHA = float((2 * 2) ** 0.25)


def decl_merge(P, pfx, io=True):
    xin = (P.dram(pfx + "xin", [MT_TOK, D], F32, kind="ExternalInput") if io else None)
    modT_d = (P.dram(pfx + "modT", [128, 64], F32, kind="ExternalInput") if io else None)
    gT_d = P.dram(pfx + "gTin", [32, 128, MT_TOK], BF16, kind="ExternalInput")
    wm_d = P.dram(pfx + "wm", [D, 8192], F32, kind="ExternalInput")
    wb_d = P.dram(pfx + "wbr", [4, 1024, D], F32, kind="ExternalInput")
    wo_d = P.dram(pfx + "wo", [D, D], F32, kind="ExternalInput")
    rows_d = P.dram(pfx + "rows", [4, 128, D], F32, kind="ExternalInput")
    out_d = (P.dram(pfx + "xout", [MT_TOK, D], F32, kind="ExternalOutput") if io else None)
    return dict(xin=xin, modT_d=modT_d, gT_d=gT_d, wm_d=wm_d, wb_d=wb_d, wo_d=wo_d, rows_d=rows_d, out_d=out_d)


def body_merge(cx, xin, modT, T, out_d):
    P = cx.P
    gT_d = T["gT_d"]
    wm_d = T["wm_d"]
    wb_d = T["wb_d"]
    wo_d = T["wo_d"]
    rows_d = T["rows_d"]
    _psf_saved = list(cx.psf)
    mT = P.sb([128, 16, MT_TOK], BF16, "mergedT")
    with P.scope():
        hT = emit_h(cx, xin, modT, 0, 9, ctx_tiles=1)
        gT = P.sb([128, 32, MT_TOK], BF16, "gT")
        for i in range(32):
            P.dma(gT[:, i, :], gT_d[i], q=("sp" if i % 2 == 0 else "pool"))
        cx.alloc_w()
        bf = [P.sb([128, 8, 128], F32, f"bf{i}") for i in range(2)]
        bb = [P.sb([128, 8, 128], BF16, f"bb{i}") for i in range(2)]
        sig = P.sb([128, 512], F32, "sig")
        prod = P.sb([128, 512], F32, "prod")
        acc = [P.sb([128, 512], F32, f"acc{i}") for i in range(3)]
        it = 0
        for dc in range(16):
            for kb in range(4):
                wf, wb = cx.wf[it % 2], cx.wb[it % 2]
                b_f, b_b = bf[it % 2], bb[it % 2]
                it += 1
                c0 = kb * 2048 + dc * 128
                P.dma(wf[:, :, :], wm_d[:, c0:c0 + 128].rearrange("(k p) n -> p k n", p=128), q="sp")
                P.dma(b_f[:, :, :], wb_d[kb, :, dc * 128:(dc + 1) * 128].rearrange("(k p) n -> p k n", p=128), q="pool")
                P.op("pool", "tensor_copy", wb[:, :, :], wf[:, :, :])
                P.op("pool", "tensor_copy", b_b[:, :, :], b_f[:, :, :])
                for gi, (t0, tl) in enumerate(MTG):
                    pl, pb_ = cx.bank(), cx.bank()
                    for kk in range(16):
                        P.op("pe", "matmul", pl[:, :tl], wb[:, kk, :], hT[:, kk, t0:t0 + tl], start=(kk == 0), stop=(kk == 15))
                    for kk in range(8):
                        P.op("pe", "matmul", pb_[:, :tl], b_b[:, kk, :], gT[:, kb * 8 + kk, t0:t0 + tl], start=(kk == 0), stop=(kk == 7))
                    P.op("act", "activation", sig[:, :tl], pl[:, :tl], AF.Sigmoid)
                    if kb == 0:
                        P.op("dve", "tensor_tensor", acc[gi][:, :tl], pb_[:, :tl], sig[:, :tl], ALU.mult)
                    else:
                        P.op("dve", "tensor_tensor", prod[:, :tl], pb_[:, :tl], sig[:, :tl], ALU.mult)
                        if kb < 3:
                            P.op("pool", "tensor_tensor", acc[gi][:, :tl], acc[gi][:, :tl], prod[:, :tl], ALU.add)
                        else:
                            P.op("pool", "tensor_tensor", mT[:, dc, t0:t0 + tl], acc[gi][:, :tl], prod[:, :tl], ALU.add)
    with P.scope():
        wo = P.sb([128, 16, D], BF16, "wo")
        cx.alloc_w()
        for cg in range(16):
            wf = cx.wf[cg % 2]
            P.dma(wf[:, :, :], wo_d[:, cg * 128:(cg + 1) * 128].rearrange("(k p) n -> p k n", p=128), q=("sp" if cg % 2 == 0 else "pool"))
            P.op("pool", "tensor_copy", wo[:, :, cg * 128:(cg + 1) * 128], wf[:, :, :])
        rows = [P.sb([128, D], F32, f"row{i}") for i in range(4)]
        for i in range(4):
            P.dma(rows[i][:, :], rows_d[i], q=("sp" if i % 2 == 0 else "pool"))
        xts = [P.sb([128, D], F32, f"mx{i}") for i in range(2)]
        vs = [P.sb([128, D], F32, f"mv{i}") for i in range(2)]
        st = P.sb([128, 4, 6], F32, "mst")
        mv = P.sb([128, 2], F32, "mmv")
        rs = P.sb([128, 1], F32, "mrs")
        for t in range(9):
            xt, v = xts[t % 2], vs[t % 2]
            P.dma(xt[:, :], xin[t * 128:(t + 1) * 128, :], q="sp")
            grow = rows[1] if t == 0 else rows[0]
            for cg in range(4):
                ps = cx.bank()
                for dc in range(16):
                    P.op("pe", "matmul", ps[:, :], mT[:, dc, t * 128:(t + 1) * 128], wo[:, dc, cg * 512:(cg + 1) * 512], start=(dc == 0), stop=(dc == 15))
                P.op("dve", "tensor_tensor", v[:, cg * 512:(cg + 1) * 512], ps[:, :], grow[:, cg * 512:(cg + 1) * 512], ALU.mult)
            P.op("dve", "scalar_tensor_tensor", v[:, :], xt[:, :], ALPHA, v[:, :], ALU.mult, ALU.add)
            for c in range(4):
                P.op("dve", "bn_stats", st[:, c, :], v[:, c * 512:(c + 1) * 512])
            P.op("dve", "bn_aggr", mv[:, :], st[:, :, :])
            P.op("act", "activation", rs[:, :], mv[:, 1:2], AF.Sqrt, bias=cx.eps(), scale=1.0)
            P.op("dve", "reciprocal", rs[:, :], rs[:, :])
            P.op("dve", "tensor_scalar", v[:, :], v[:, :], mv[:, 0:1], rs[:, 0:1], ALU.subtract, ALU.mult)
            P.op("pool", "tensor_tensor", v[:, :], v[:, :], rows[2][:, :], ALU.mult)
            P.op("dve", "tensor_tensor", v[:, :], v[:, :], rows[3][:, :], ALU.add)
            P.dma(out_d[t * 128:(t + 1) * 128, :], v[:, :], q="sp", is_out=True)
    cx.psf = _psf_saved


def build_merge():
    nc = bass.Bass("TRN2", target_bir_lowering=False)
    P = Prog(nc)
    T = decl_merge(P, "")
    cx = Cx(P, wcols=128)
    modT = load_modT(cx, T['modT_d'])
    body_merge(cx, T['xin'], modT, T, T['out_d'])
    P.emit()
    return nc


import ml_dtypes

OFF = dict(s_z=0, s_xbc=1024, s_dt=2560, m_cq=2592, m_ckv=3104, m_kpe=3360, m_z=3424, g_q=4448, g_k=5472,
           g_v=5600, g_z=5728, hy_xv=6752, hy_z=9824, merge=10848)
_CACHE = {}


def _bf(a):
    return np.ascontiguousarray(a.astype(np.float32)).astype(ml_dtypes.bfloat16)


def _fm(v, nchunk):
    return np.ascontiguousarray(np.asarray(v, np.float32).reshape(nchunk, 128).T)


def hy_consts(h):
    key = ("hy", h)
    if key in _CACHE:
        return _CACHE[key]
    out = {}
    deltas = np.abs(np.linspace(np.log(0.01) / 0.3, np.log(0.01) / 1.5, 1024, dtype=np.float32)).astype(np.float64)[512 * h:512 * h + 512]
    for tag, n, TW in (("L", 2048, 256), ("C", 256, 256)):
        N = 2 * n
        nch = n // 128
        t = np.arange(n, dtype=np.float64)[:, None]
        f = np.arange(n, dtype=np.float64)[None, :]
        ang = np.pi * (2 * f + 1) * t / N
        Cm, Sm = np.cos(ang), -np.sin(ang)

        def fw(M):
            return _bf(M.reshape(nch, 128, nch, 128).transpose(2, 1, 0, 3).reshape(nch, 128, nch * 128))

        def iv(M):
            return _bf(((2.0 / N) * M).reshape(n // TW, TW, nch, 128).transpose(0, 3, 2, 1).reshape(n // TW, 128, nch * TW))

        out["fwc" + tag], out["fws" + tag] = fw(Cm), fw(Sm)
        out["ivc" + tag], out["ivs" + tag] = iv(Cm), iv(Sm)
        tl = np.linspace(0.0, 1.0, n, dtype=np.float32).astype(np.float64)[:, None]
        w_ang = (2.0 * np.pi / n) * np.arange(n, dtype=np.float64)[:, None]
        bands = np.linspace(1e-4, 15, 16, dtype=np.float32).astype(np.float64)[None, :]
        feats = np.concatenate([tl, np.cos(bands * w_ang), -np.sin(bands * w_ang)], axis=-1)
        out["feats" + tag] = np.ascontiguousarray(feats.T.astype(np.float32))
        dec = np.exp(-tl * deltas[None, :])
        out["dec" + tag] = np.ascontiguousarray(dec.reshape(nch, 128, 512).transpose(1, 0, 2).astype(np.float32))
    _CACHE[key] = out
    return out


def run_mod(inp, l):
    cvec = np.concatenate([inp["c"], inp["c_ctx"][None, :]], axis=0)
    cT = np.ascontiguousarray(cvec.reshape(5, 16, 128).transpose(2, 1, 0))
    if "mod" not in _CACHE:
        _CACHE["mod"] = build_mod()
    maps = []
    for i in range(8):
        maps.append(dict(cT=cT, wa=np.ascontiguousarray(inp["w_ada"][l][:, i * 768:(i + 1) * 768]),
                         ba=_fm(inp["b_ada"][l][i * 768:(i + 1) * 768], 6)))
    res = run_bass_kernel_spmd(_CACHE["mod"], maps, core_ids=list(range(8)))
    return np.concatenate([r["modT"] for r in res.results], axis=1)


def modT_for(modall, b):
    return np.ascontiguousarray(np.concatenate([modall[:, 0:16, b], modall[:, 16:32, b], modall[:, 0:16, 4], modall[:, 16:32, 4]], axis=1))


def hy_maps(inp, l, xins, modall):
    maps = []
    W = inp["w_in"][l]
    for core in range(8):
        b, h = core // 2, core % 2
        c0 = 512 * h
        o = OFF["hy_xv"]
        w = np.concatenate([W[:, o + c0:o + c0 + 512], W[:, o + 1024 + c0:o + 1024 + c0 + 512],
                            W[:, o + 2048 + c0:o + 2048 + c0 + 512], W[:, OFF["hy_z"] + c0:OFF["hy_z"] + c0 + 512]], axis=1)
        cw = np.zeros((128, 12, 4), np.float32)
        for j in range(12):
            ch = (j // 4) * 1024 + c0 + (j % 4) * 128
            cw[:, j, 0:3] = inp["hy_conv_w"][l][:, ch:ch + 128].T
            cw[:, j, 3] = inp["hy_conv_b"][l][ch:ch + 128]
        w3 = np.concatenate([inp["hy_w3"][l][:, c0:c0 + 512], inp["hy_w3"][l][:, 1024 + c0:1024 + c0 + 512]], axis=1)
        pb = np.stack([inp["hy_b1"][l], inp["hy_freq"][l][0], inp["hy_b2"][l], inp["hy_freq"][l][1]], axis=1)
        m = dict(xin=xins[b], modT=(modT_for(modall, b) if modall is not None else None), w=np.ascontiguousarray(w), cw=cw,
                 dsk=_fm(inp["hy_d"][l][c0:c0 + 512], 4), w1=np.ascontiguousarray(inp["hy_w1"][l]),
                 w2=np.ascontiguousarray(inp["hy_w2"][l]), w3=np.ascontiguousarray(w3), pb=np.ascontiguousarray(pb.astype(np.float32)))
        m.update(hy_consts(h))
        maps.append(m)
    return maps


def rope_tables():
    if "rope" in _CACHE:
        return _CACHE["rope"]
    t = np.arange(NSEQ)
    inv = 10000.0 ** (-np.arange(16, dtype=np.float32) / 16)
    ang = np.concatenate([(t // 64)[:, None].astype(np.float32) * inv[None, :], (t % 64)[:, None].astype(np.float32) * inv[None, :]], axis=-1)
    cos, sin = np.cos(ang.astype(np.float32)).T, np.sin(ang.astype(np.float32)).T
    C = np.ascontiguousarray(np.tile(cos, (4, 1)).astype(np.float32))
    S = np.ascontiguousarray(np.tile(sin, (4, 1)).astype(np.float32))
    _CACHE["rope"] = (C, S)
    return C, S


def mla_maps(inp, l, xins, modall):
    maps = []
    W = inp["w_in"][l]
    C, S = rope_tables()
    for core in range(8):
        b, h = core // 2, core % 2
        kpe = W[:, OFF["m_kpe"]:OFF["m_kpe"] + 64]
        w = np.concatenate([W[:, OFF["m_cq"]:OFF["m_cq"] + 512], W[:, OFF["m_ckv"]:OFF["m_ckv"] + 256], kpe, kpe,
                            W[:, OFF["m_z"] + 512 * h:OFF["m_z"] + 512 * h + 512]], axis=1)
        uq = inp["mla_w_uq"][l].reshape(512, 8, 192)[:, 4 * h:4 * h + 4]
        wuq = np.concatenate([uq[:, :, :128].reshape(512, 512), uq[:, :, 128:].reshape(512, 256)], axis=1)
        ukv = inp["mla_w_ukv"][l].reshape(256, 8, 256)[:, 4 * h:4 * h + 4]
        wukv = np.concatenate([ukv[:, :, :128].reshape(256, 512), ukv[:, :, 128:].reshape(256, 512)], axis=1)
        nrm = np.concatenate([_fm(inp["mla_q_norm"][l], 4), _fm(inp["mla_kv_norm"][l], 2)], axis=1)
        maps.append(dict(xin=xins[b], modT=(modT_for(modall, b) if modall is not None else None), w=np.ascontiguousarray(w),
                         wuq=np.ascontiguousarray(wuq.reshape(4, 128, 768).transpose(1, 0, 2)),
                         wukv=np.ascontiguousarray(wukv.reshape(2, 128, 1024).transpose(1, 0, 2)),
                         nrm=np.ascontiguousarray(nrm), ropeC=C, ropeS=S))
    return maps


def gqa_maps(inp, l, xins, modall):
    maps = []
    W = inp["w_in"][l]
    C, S = rope_tables()
    kk = np.arange(128)[:, None]
    qq = np.arange(128)[None, :]
    msk = _bf(np.stack([(qq <= kk), (qq >= kk)], axis=1).astype(np.float32))
    for core in range(8):
        b, h = core // 2, core % 2
        kw = W[:, OFF["g_k"] + 64 * h:OFF["g_k"] + 64 * h + 64]
        vw = W[:, OFF["g_v"] + 64 * h:OFF["g_v"] + 64 * h + 64]
        w = np.concatenate([W[:, OFF["g_q"] + 512 * h:OFF["g_q"] + 512 * h + 512], kw, kw,
                            W[:, OFF["g_z"] + 512 * h:OFF["g_z"] + 512 * h + 512], vw, vw], axis=1)
        sk = inp["gqa_sink"][l][8 * h:8 * h + 8]
        sink = np.zeros((128, 4), np.float32)
        for j in range(4):
            sink[:64, j] = sk[2 * j]
            sink[64:, j] = sk[2 * j + 1]
        maps.append(dict(xin=xins[b], modT=(modT_for(modall, b) if modall is not None else None), w=np.ascontiguousarray(w), ropeC=C, ropeS=S, sink=sink, msk=msk))
    return maps


def ssd_maps(inp, l, xins, modall):
    maps = []
    W = inp["w_in"][l]
    kk = np.arange(128)[:, None]
    ll = np.arange(128)[None, :]
    msk = np.stack([(kk <= ll).astype(np.float32), (kk >= ll).astype(np.float32),
                    np.where(ll < kk, -30000.0, 0.0).astype(np.float32), np.where(ll > kk, -30000.0, 0.0).astype(np.float32)], axis=1)
    for core in range(8):
        b, h = core // 2, core % 2
        ox = OFF["s_xbc"]
        dtw = np.zeros((D, 128), np.float32)
        dtw[:, 0:8] = W[:, OFF["s_dt"] + 8 * h:OFF["s_dt"] + 8 * h + 8]
        dtw[:, 8:16] = W[:, OFF["s_dt"] + 16 + 8 * h:OFF["s_dt"] + 16 + 8 * h + 8]
        w = np.concatenate([W[:, ox + 512 * h:ox + 512 * h + 512], W[:, ox + 1024 + 128 * h:ox + 1024 + 128 * h + 128],
                            W[:, ox + 1280 + 128 * h:ox + 1280 + 128 * h + 128],
                            W[:, OFF["s_z"] + 512 * h:OFF["s_z"] + 512 * h + 512], dtw], axis=1)
        chs = [512 * h + 128 * j for j in range(4)] + [1024 + 128 * h, 1280 + 128 * h]
        cw = np.zeros((128, 6, 4), np.float32)
        for j, ch in enumerate(chs):
            cw[:, j, 0:3] = inp["ssd_conv_w"][l][:, ch:ch + 128].T
            cw[:, j, 3] = inp["ssd_conv_b"][l][ch:ch + 128]
        row = np.concatenate([inp["ssd_dt_bias"][l][0, 8 * h:8 * h + 8], inp["ssd_dt_bias"][l][1, 8 * h:8 * h + 8],
                              inp["ssd_a_log"][l][0, 8 * h:8 * h + 8], inp["ssd_a_log"][l][1, 8 * h:8 * h + 8],
                              inp["ssd_d"][l][8 * h:8 * h + 8], inp["ssd_norm_w"][l][512 * h:512 * h + 512]]).astype(np.float32)
        rows = np.ascontiguousarray(np.broadcast_to(row[None, :], (128, row.shape[0])))
        maps.append(dict(xin=xins[b], modT=(modT_for(modall, b) if modall is not None else None), w=np.ascontiguousarray(w), cw=cw, rows=rows, msk=msk))
    return maps


def merge_maps(inp, l, xcur, ctxcur, modall, gouts):
    maps = []
    wm = np.ascontiguousarray(inp["w_in"][l][:, OFF["merge"]:OFF["merge"] + 8192])
    wbr = np.ascontiguousarray(inp["w_branch"][l])
    wo = np.ascontiguousarray(inp["w_out"][l])
    for core in range(8):
        b, hh = core // 2, core % 2
        toks = np.concatenate([np.arange(128 * hh, 128 * hh + 128), NCTX + np.arange(1024 * hh, 1024 * hh + 1024)])
        xin = np.ascontiguousarray(np.concatenate([ctxcur[b][128 * hh:128 * hh + 128], xcur[b][1024 * hh:1024 * hh + 1024]], axis=0))
        g = np.empty((4, 8, 128, MT_TOK), dtype=ml_dtypes.bfloat16)
        for kb in range(4):
            for h in range(2):
                g[kb, 4 * h:4 * h + 4] = np.asarray(gouts[kb][2 * b + h])[:, :, toks]
        def row(v):
            return np.broadcast_to(np.asarray(v, np.float32).reshape(1, D), (128, D))
        gate_lat = modall[:, 32:48, b].T.reshape(-1)
        gate_ctx = modall[:, 32:48, 4].T.reshape(-1)
        rows = np.ascontiguousarray(np.stack([row(gate_lat), row(gate_ctx), row(inp["ln_g"][l]), row(inp["ln_b"][l])], axis=0))
        maps.append(dict(xin=xin, modT=(modT_for(modall, b) if modall is not None else None), gTin=np.ascontiguousarray(g.reshape(32, 128, MT_TOK)), wm=wm, wbr=wbr, wo=wo, rows=rows))
    return maps


def _prog(name):
    if name not in _CACHE:
        _CACHE[name] = globals()["build_" + name]()
    return _CACHE[name]


def kernel_unfused(**inp):
    inp = {k_: np.asarray(v) for k_, v in inp.items()}
    xcur = [np.ascontiguousarray(inp["x"][b]) for b in range(4)]
    ctxcur = [np.ascontiguousarray(inp["ctx"][b]) for b in range(4)]
    cores = list(range(8))
    for l in range(2):
        modall = run_mod(inp, l)
        xins = [np.ascontiguousarray(np.concatenate([ctxcur[b], xcur[b]], axis=0)) for b in range(4)]
        gouts = []
        for name in ("ssd", "mla", "gqa", "hy"):
            maps = globals()[name + "_maps"](inp, l, xins, modall)
            res = run_bass_kernel_spmd(_prog(name), maps, core_ids=cores)
            gouts.append([r["gT"] for r in res.results])
        maps = merge_maps(inp, l, xcur, ctxcur, modall, gouts)
        res = run_bass_kernel_spmd(_prog("merge"), maps, core_ids=cores)
        for b in range(4):
            o0, o1 = res.results[2 * b]["xout"], res.results[2 * b + 1]["xout"]
            ctxcur[b] = np.ascontiguousarray(np.concatenate([o0[:128], o1[:128]], axis=0))
            xcur[b] = np.ascontiguousarray(np.concatenate([o0[128:], o1[128:]], axis=0))
    return np.stack(xcur, axis=0).astype(np.float32)


class Slot:
    def __init__(self, buf, base):
        self.buf = buf
        self.base = base

    def __getitem__(self, idx):
        return self.buf[(self.base + idx[0],) + tuple(idx[1:])]


def body_mod(cx, T, gate_d):
    P = cx.P
    modT = P.sb([128, 64], F32, "modTl")
    with P.scope():
        cs = P.sb([128, 16, 2], F32, "mod_cs")
        P.dma(cs[:, :, :], T["cT_d"].v)
        P.op("act", "activation", cs[:, :, :], cs[:, :, :], AF.Silu)
        baT = P.sb([128, 48], F32, "mod_ba")
        P.dma(baT[:, :], T["baT_d"].v)
        ws = [P.sb([128, 16, 512], F32, f"mod_ws{i}") for i in range(2)]
        raw = P.sb([128, 32, 2], F32, "mod_raw")
        psm = cx.bank()
        wi = 0
        for g in range(8):
            w_ = ws[wi % 2]
            P.dma(w_[:, :, :], T["wa_d"][:, g * 512:(g + 1) * 512].rearrange("(k p) n -> p k n", p=128), q=("sp" if wi % 2 == 0 else "pool"))
            wi += 1
            for j in range(4):
                ch = g * 4 + j
                for kk in range(16):
                    P.op("pe", "matmul", psm[:, ch * 2:ch * 2 + 2], w_[:, kk, j * 128:(j + 1) * 128], cs[:, kk, :], start=(kk == 0), stop=(kk == 15))
        P.op("dve", "tensor_tensor", raw[:, :, :], psm[:, 0:64].rearrange("p (c j) -> p c j", j=2),
             baT[:, 0:32].unsqueeze(2).to_broadcast([128, 32, 2]), ALU.add)
        P.op("dve", "tensor_copy", modT[:, 0:16], raw[:, 0:16, 0])
        P.op("dve", "tensor_copy", modT[:, 32:48], raw[:, 0:16, 1])
        P.op("dve", "tensor_scalar", modT[:, 16:32], raw[:, 16:32, 0], 1.0, None, ALU.add)
        P.op("dve", "tensor_scalar", modT[:, 48:64], raw[:, 16:32, 1], 1.0, None, ALU.add)
        grow = P.sb([2, D], F32, "mod_grow")
        bg = P.sb([2, D], F32, "mod_bg")
        sel = P.sb([2, 2, 128], F32, "mod_sel")
        P.dma(bg[:, :], T["bgrow_d"].v)
        P.dma(sel[:, :, :], T["sel_d"].v)
        for g in range(4):
            w_ = ws[wi % 2]
            P.dma(w_[:, :, :], T["wa_d"][:, 4096 + g * 512:4096 + (g + 1) * 512].rearrange("(k p) n -> p k n", p=128), q=("sp" if wi % 2 == 0 else "pool"))
            wi += 1
            ps = cx.bank()
            for kk in range(16):
                P.op("pe", "matmul", ps[:2, :], cs[:, kk, :], w_[:, kk, :], start=(kk == 0), stop=(kk == 15))
            P.op("dve", "tensor_tensor", grow[:, g * 512:(g + 1) * 512], ps[:2, :], bg[:, g * 512:(g + 1) * 512], ALU.add)
        gsb = [P.sb([128, D], F32, f"mod_gsb{i}") for i in range(2)]
        for j in range(2):
            for g in range(4):
                ps = cx.bank()
                P.op("pe", "matmul", ps[:, :], sel[:, j, :], grow[:, g * 512:(g + 1) * 512], start=True, stop=True)
                P.op("act", "activation", gsb[j][:, g * 512:(g + 1) * 512], ps[:, :], AF.Identity)
            P.dma(gate_d[j], gsb[j][:, :])
    return modT


def body_merge2(cx, xin, modT, T, gT_all, gate_d, xout, t_lo, t_hi, is_out):
    P = cx.P
    wm_d, wb_d, wo_d, ln_d = T["wm_d"], T["wb_d"], T["wo_d"], T["lnrows_d"]
    ntk = (t_hi - t_lo) * 128
    tb0 = t_lo * 128
    groups = tgroups(t_lo, t_hi)
    mT = P.sb([128, 16, ntk], BF16, "mergedT")
    with P.scope():
        hT = emit_h(cx, xin, modT, t_lo, t_hi)
        gT = P.sb([128, 32, ntk], BF16, "gT")
        for i in range(32):
            P.dma(gT[:, i, :], gT_all[i, :, tb0:tb0 + ntk], q=("sp" if i % 2 == 0 else "pool"))
        cx.alloc_w()
        bf = [P.sb([128, 8, 128], F32, f"bf{i}") for i in range(3)]
        bb = [P.sb([128, 8, 128], BF16, f"bb{i}") for i in range(2)]
        sig = P.sb([128, 512], F32, "sig")
        prod = P.sb([128, 512], F32, "prod")
        acc = [P.sb([128, 512], F32, f"acc{i}") for i in range(len(groups))]
        slabs = [(dc, kb) for dc in range(16) for kb in range(4)]

        def issue(si):
            dc_, kb_ = slabs[si]
            c0_ = kb_ * 2048 + dc_ * 128
            P.dma(cx.wf[si % 3][:, :, :], wm_d[:, c0_:c0_ + 128].rearrange("(k p) n -> p k n", p=128), q="sp")
            P.dma(bf[si % 3][:, :, :], wb_d[kb_, :, dc_ * 128:(dc_ + 1) * 128].rearrange("(k p) n -> p k n", p=128), q="sp")

        issue(0)
        issue(1)
        for it, (dc, kb) in enumerate(slabs):
            if True:
                if it + 2 < len(slabs):
                    issue(it + 2)
                wf, wb = cx.wf[it % 3], cx.wb[it % 2]
                b_f, b_b = bf[it % 3], bb[it % 2]
                P.op("dve", "tensor_copy", wb[:, :, :], wf[:, :, :])
                P.op("act", "activation", b_b[:, :, :], b_f[:, :, :], AF.Identity)
                for gi, (t0, tl) in enumerate(groups):
                    r0 = t0 - tb0
                    pl, pb_ = cx.bank(), cx.bank()
                    for kk in range(16):
                        P.op("pe", "matmul", pl[:, :tl], wb[:, kk, :], hT[:, kk, r0:r0 + tl], start=(kk == 0), stop=(kk == 15))
                    for kk in range(8):
                        P.op("pe", "matmul", pb_[:, :tl], b_b[:, kk, :], gT[:, kb * 8 + kk, r0:r0 + tl], start=(kk == 0), stop=(kk == 7))
                    P.op("act", "activation", sig[:, :tl], pl[:, :tl], AF.Sigmoid)
                    if kb == 0:
                        P.op("dve", "tensor_tensor", acc[gi][:, :tl], pb_[:, :tl], sig[:, :tl], ALU.mult)
                    else:
                        P.op("dve", "tensor_tensor", prod[:, :tl], pb_[:, :tl], sig[:, :tl], ALU.mult)
                        if kb < 3:
                            P.op("pool", "tensor_tensor", acc[gi][:, :tl], acc[gi][:, :tl], prod[:, :tl], ALU.add)
                        else:
                            P.op("pool", "tensor_tensor", mT[:, dc, r0:r0 + tl], acc[gi][:, :tl], prod[:, :tl], ALU.add)
    with P.scope():
        wos = [P.sb([128, 16, 512], BF16, f"wo{i}") for i in range(4)]
        cx.alloc_w()
        rows = [P.sb([128, D], F32, f"row{i}") for i in range(4)]
        for cg in range(16):
            wf = cx.wf[cg % 3]
            P.dma(wf[:, :, :], wo_d[:, cg * 128:(cg + 1) * 128].rearrange("(k p) n -> p k n", p=128), q="sp")
            dst = wos[cg // 4][:, :, (cg % 4) * 128:(cg % 4 + 1) * 128]
            if cg % 2 == 0:
                P.op("dve", "tensor_copy", dst, wf[:, :, :])
            else:
                P.op("act", "activation", dst, wf[:, :, :], AF.Identity)
        for i, src in enumerate((gate_d[0], gate_d[1], ln_d[0], ln_d[1])):
            P.dma(rows[i][:, :], src, q=("sp" if i % 2 == 0 else "pool"))
        xts = [P.sb([128, D], F32, f"mx{i}") for i in range(2)]
        vs = [P.sb([128, D], F32, f"mv{i}") for i in range(2)]
        st = P.sb([128, 4, 6], F32, "mst")
        mv = P.sb([128, 2], F32, "mmv")
        rs = P.sb([128, 1], F32, "mrs")
        for t in range(t_lo, t_hi):
            xt, v = xts[t % 2], vs[t % 2]
            r0 = (t - t_lo) * 128
            P.dma(xt[:, :], xin[t * 128:(t + 1) * 128, :], q="sp")
            grow = rows[1] if t < 2 else rows[0]
            for cg in range(4):
                ps = cx.bank()
                for dc in range(16):
                    P.op("pe", "matmul", ps[:, :], mT[:, dc, r0:r0 + 128], wos[cg][:, dc, :], start=(dc == 0), stop=(dc == 15))
                P.op("dve", "tensor_tensor", v[:, cg * 512:(cg + 1) * 512], ps[:, :], grow[:, cg * 512:(cg + 1) * 512], ALU.mult)
            P.op("dve", "scalar_tensor_tensor", v[:, :], xt[:, :], ALPHA, v[:, :], ALU.mult, ALU.add)
            for c in range(4):
                P.op("dve", "bn_stats", st[:, c, :], v[:, c * 512:(c + 1) * 512])
            P.op("dve", "bn_aggr", mv[:, :], st[:, :, :])
            P.op("act", "activation", rs[:, :], mv[:, 1:2], AF.Sqrt, bias=cx.eps(), scale=1.0)
            P.op("dve", "reciprocal", rs[:, :], rs[:, :])
            P.op("dve", "tensor_scalar", v[:, :], v[:, :], mv[:, 0:1], rs[:, 0:1], ALU.subtract, ALU.mult)
            P.op("pool", "tensor_tensor", v[:, :], v[:, :], rows[2][:, :], ALU.mult)
            P.op("dve", "tensor_tensor", v[:, :], v[:, :], rows[3][:, :], ALU.add)
            P.dma(xout[t * 128:(t + 1) * 128, :], v[:, :], q="sp", is_out=is_out)


MIXERS = ("ssd", "mla", "gqa", "hy")


def build_mega(nlayers=2):
    nc = bass.Bass("TRN2", target_bir_lowering=False)
    P = Prog(nc)
    xin0 = P.dram("xin", [NTOK, D], F32, kind="ExternalInput")
    xmid = P.dram("xmid", [NTOK, D], F32)
    xfin = P.dram("xout", [NTOK, D], F32, kind="ExternalOutput")
    cT_d = P.dram("cT", [128, 16, 2], F32, kind="ExternalInput")
    sel_d = P.dram("sel", [2, 2, 128], F32, kind="ExternalInput")
    cx = Cx(P, wcols=128)
    for l in range(nlayers):
        Tm = dict(cT_d=cT_d, sel_d=sel_d,
                  wa_d=P.dram(f"L{l}_wa", [D, 6144], F32, kind="ExternalInput"),
                  baT_d=P.dram(f"L{l}_baT", [128, 48], F32, kind="ExternalInput"),
                  bgrow_d=P.dram(f"L{l}_bgrow", [2, D], F32, kind="ExternalInput"))
        Tg = dict(wm_d=P.dram(f"L{l}_wm", [D, 8192], F32, kind="ExternalInput"),
                  wb_d=P.dram(f"L{l}_wbr", [4, 1024, D], F32, kind="ExternalInput"),
                  wo_d=P.dram(f"L{l}_wo", [D, D], F32, kind="ExternalInput"),
                  lnrows_d=P.dram(f"L{l}_lnrows", [2, 128, D], F32, kind="ExternalInput"))
        gate_d = P.dram(f"L{l}_gate", [2, 128, D], F32)
        gT_all = P.dram(f"L{l}_gT", [32, 128, NTOK], BF16)
        xin = xin0 if l == 0 else xmid
        xout = xmid if l < nlayers - 1 else xfin
        hT_d = P.dram(f"L{l}_hT", [128, 16, NTOK], BF16)
        with P.scope():
            modT = body_mod(cx, Tm, gate_d)
            cx.hT_d = None
            for (ta, tb) in ((0, 9), (9, 18)):
                with P.scope():
                    hT = emit_h(cx, xin, modT, ta, tb)
                    for k0 in range(0, 16, 4):
                        P.dma(hT_d[:, k0:k0 + 4, ta * 128:tb * 128], hT[:, k0:k0 + 4, :], q=("sp" if (k0 // 4) % 2 == 0 else "pool"))
            cx.hT_d = hT_d
            for ki, name in enumerate(MIXERS):
                for h in range(2):
                    T = globals()["decl_" + name](P, f"L{l}_{name}{h}_", io=False)
                    with P.scope():
                        globals()["body_" + name](cx, xin, modT, T, Slot(gT_all, ki * 8 + h * 4))
            for (ta, tb) in ((0, 9), (9, 18)):
                with P.scope():
                    body_merge2(cx, xin, modT, Tg, gT_all, gate_d, xout, ta, tb, is_out=(l == nlayers - 1))
        cx.hT_d = None
    P.emit()
    return nc


def mega_maps(inp):
    maps = [dict() for _ in range(8)]
    cvec = [np.stack([inp["c"][b], inp["c_ctx"]], axis=0) for b in range(4)]
    sel = np.zeros((2, 2, 128), np.float32)
    sel[0, 0, :] = 1.0
    sel[1, 1, :] = 1.0
    for core in range(8):
        b = core // 2
        m = maps[core]
        m["xin"] = np.ascontiguousarray(np.concatenate([inp["ctx"][b], inp["x"][b]], axis=0))
        m["cT"] = np.ascontiguousarray(cvec[b].reshape(2, 16, 128).transpose(2, 1, 0))
        m["sel"] = sel
    for l in range(2):
        per = {name: globals()[name + "_maps"](inp, l, [None] * 4, None) for name in MIXERS}
        wa = np.ascontiguousarray(inp["w_ada"][l])
        baT = _fm(inp["b_ada"][l], 48)
        bgrow = np.ascontiguousarray(np.broadcast_to(inp["b_ada"][l][4096:6144][None, :], (2, D)).astype(np.float32))
        wm = np.ascontiguousarray(inp["w_in"][l][:, OFF["merge"]:OFF["merge"] + 8192])
        wbr = np.ascontiguousarray(inp["w_branch"][l])
        wo = np.ascontiguousarray(inp["w_out"][l])
        lnrows = np.ascontiguousarray(np.stack([np.broadcast_to(inp["ln_g"][l][None, :], (128, D)),
                                                np.broadcast_to(inp["ln_b"][l][None, :], (128, D))], axis=0).astype(np.float32))
        for core in range(8):
            b = core // 2
            m = maps[core]
            m[f"L{l}_wa"], m[f"L{l}_baT"], m[f"L{l}_bgrow"] = wa, baT, bgrow
            m[f"L{l}_wm"], m[f"L{l}_wbr"], m[f"L{l}_wo"], m[f"L{l}_lnrows"] = wm, wbr, wo, lnrows
            for name in MIXERS:
                for h in range(2):
                    src = per[name][2 * b + h]
                    for key, val in src.items():
                        if key in ("xin", "modT"):
                            continue
                        m[f"L{l}_{name}{h}_{key}"] = val
    return maps


def kernel(**inp):
    inp = {k_: np.asarray(v) for k_, v in inp.items()}
    if "mega" not in _CACHE:
        _CACHE["mega"] = build_mega()
    maps = mega_maps(inp)
    res = run_bass_kernel_spmd(_CACHE["mega"], maps, core_ids=list(range(8)))
    out = np.stack([res.results[2 * b]["xout"][NCTX:] for b in range(4)], axis=0)
    return np.ascontiguousarray(out.astype(np.float32))
```

```python
import contextlib
import numpy as np
import concourse.bass as bass
import concourse.mybir as mybir
from concourse.bass_utils import run_bass_kernel_spmd

F32 = mybir.dt.float32
BF16 = mybir.dt.bfloat16
AF = mybir.ActivationFunctionType
ALU = mybir.AluOpType
AX = mybir.AxisListType

SAME_ENGINE_SYNC = True
NDMASEM = 8


class V:
    def __init__(self, buf, ap):
        self.buf = buf
        self.ap = ap

    def __getitem__(self, idx):
        return V(self.buf, self.ap[idx])

    def __getattr__(self, name):
        attr = getattr(self.ap, name)
        if callable(attr):
            buf = self.buf
            apt = type(self.ap)

            def f(*a, **k):
                r = attr(*a, **k)
                return V(buf, r) if isinstance(r, apt) else r
            return f
        return attr


class Buf:
    def __init__(self, t, name):
        self.t = t
        self.name = name
        self.w = None
        self.r = []
        self.full = t.ap() if hasattr(t, "ap") else t[:]
        self.is_psum = False

    def __getitem__(self, idx):
        return V(self, self.full[idx])

    @property
    def v(self):
        return V(self, self.full)


class Prog:
    ENG = ["pe", "act", "dve", "pool", "sp"]

    def __init__(self, nc):
        self.nc = nc
        self.ops = {e: [] for e in self.ENG}
        self.sems = {}
        for e in self.ENG:
            self.sems[("c", e)] = nc.alloc_semaphore("c_" + e)
        self.cnt = {e: 0 for e in self.ENG}
        self.waited = {e: {} for e in self.ENG}
        self.dma_i = {e: 0 for e in self.ENG}
        self.dma_use = {}
        self.nbuf = 0
        self.out_tokens = []
        self.scopes = []

    def sb(self, shape, dt, name=None):
        self.nbuf += 1
        name = f"s{self.nbuf}_" + (name or "t")
        if self.scopes:
            t = self.scopes[-1].enter_context(self.nc.sbuf_tensor(name, list(shape), dt))
        else:
            t = self.nc.alloc_sbuf_tensor(name, list(shape), dt)
        return Buf(t, name)

    @contextlib.contextmanager
    def scope(self):
        es = contextlib.ExitStack()
        self.scopes.append(es)
        try:
            yield
        finally:
            self.barrier()
            self.scopes.pop()
            es.close()

    def barrier(self):
        snap = [(("c", e), self.cnt[e]) for e in self.ENG if self.cnt[e] > 0]
        snap += [(key, 16 * n) for key, n in self.dma_use.items() if n > 0]
        for e in self.ENG:
            waits = []
            for key, val in snap:
                if key == ("c", e):
                    continue
                if self.waited[e].get(key, -1) >= val:
                    continue
                self.waited[e][key] = val
                waits.append((key, val))
            if waits:
                self.ops[e].append((waits, None, None, 0))

    def ps(self, shape, dt, name=None):
        self.nbuf += 1
        name = name or f"ps{self.nbuf}"
        b = Buf(self.nc.alloc_psum_tensor(name, list(shape), dt), name)
        b.is_psum = True
        return b

    def dram(self, name, shape, dt, kind="Internal"):
        return Buf(self.nc.dram_tensor(name, list(shape), dt, kind=kind), name)

    def _deps(self, eng, reads, writes, is_dma):
        toks = []
        for b in reads:
            if b.w is not None:
                toks.append(b.w)
            if b.is_psum:
                toks.extend(t for t in b.r if t[0] != eng)
        for b in writes:
            if b.w is not None:
                toks.append(b.w)
            toks.extend(b.r)
        best = {}
        for tok in toks:
            teng, key, val, tdma = tok
            if (not tdma) and teng == eng and not is_dma:
                if eng == "pe" or not SAME_ENGINE_SYNC:
                    continue
            if key not in best or best[key] < val:
                best[key] = val
        waits = []
        for key, val in best.items():
            if self.waited[eng].get(key, -1) >= val:
                continue
            self.waited[eng][key] = val
            waits.append((key, val))
        return waits

    def _commit(self, tok, reads, writes):
        for b in reads:
            b.r.append(tok)
            if len(b.r) > 64:
                d = {}
                for t in b.r:
                    k = (t[0], t[1], t[3])
                    if k not in d or d[k][2] < t[2]:
                        d[k] = t
                b.r = list(d.values())
        for b in writes:
            b.w = tok
            b.r = []

    @staticmethod
    def _scan(args, kwargs, reads, writes):
        reads = list(reads)
        writes = list(writes)
        a2 = []
        for i, a in enumerate(args):
            if isinstance(a, V):
                (writes if i == 0 else reads).append(a.buf)
                a = a.ap
            a2.append(a)
        k2 = {}
        for k, a in kwargs.items():
            if isinstance(a, V):
                (writes if k in ("out", "accum_out") else reads).append(a.buf)
                a = a.ap
            k2[k] = a
        return a2, k2, reads, writes

    def op(self, eng, meth, *args, reads=(), writes=(), **kwargs):
        args, kwargs, reads, writes = self._scan(args, kwargs, reads, writes)
        fn = (meth, args, kwargs)
        waits = self._deps(eng, reads, writes, False)
        self.cnt[eng] += 1
        key = ("c", eng)
        tok = (eng, key, self.cnt[eng], False)
        self.ops[eng].append((waits, fn, key, 1))
        self._commit(tok, reads, writes)
        return tok

    def dma(self, out, in_, q="sp", is_out=False, reads=(), writes=(), **kw):
        args, kwargs, reads, writes = self._scan((), dict(out=out, in_=in_, **kw), reads, writes)
        fn = ("dma_start", args, kwargs)
        waits = self._deps(q, reads, writes, True)
        i = self.dma_i[q]
        self.dma_i[q] += 1
        slot = i % NDMASEM
        key = ("d", q, slot)
        if key not in self.sems:
            self.sems[key] = self.nc.alloc_semaphore(f"d_{q}_{slot}")
            self.dma_use[key] = 0
        prev = self.dma_use[key]
        if prev > 0 and self.waited[q].get(key, -1) < 16 * prev:
            self.waited[q][key] = 16 * prev
            waits.append((key, 16 * prev))
        self.dma_use[key] = prev + 1
        tok = (q, key, 16 * (prev + 1), True)
        self.ops[q].append((waits, fn, key, 16))
        self._commit(tok, reads, writes)
        if is_out:
            self.out_tokens.append(tok)
        return tok

    def cc(self, kind, alu, groups, src, dst):
        q = "pool"
        reads, writes = [src], [dst]
        waits = self._deps(q, reads, writes, True)
        key = ("d", q, "cc%d" % self.dma_i[q])
        self.dma_i[q] += 1
        self.sems[key] = self.nc.alloc_semaphore("cc_%d" % len(self.sems))
        self.dma_use[key] = 1
        tok = (q, key, 16, True)
        fn = ("collective_compute", [kind, alu], dict(replica_groups=groups, ins=[src.full], outs=[dst.full]))
        self.ops[q].append((waits, fn, key, 16))
        self._commit(tok, reads, writes)
        return tok

    def emit(self):
        nc = self.nc
        finals = [(t[1], t[2]) for t in self.out_tokens]
        for e in self.ENG:
            if self.cnt[e] > 0:
                finals.append((("c", e), self.cnt[e]))
        for key, n in self.dma_use.items():
            finals.append((key, 16 * n))
        sems = self.sems
        ops = self.ops

        def run(engname):
            def body(eng):
                for waits, fn, key, inc in ops[engname]:
                    for k, v in waits:
                        eng.wait_ge(sems[k], v)
                    if fn is None:
                        continue
                    ins = getattr(eng, fn[0])(*fn[1], **fn[2])
                    ins.then_inc(sems[key], inc)
                if engname == "sp":
                    for k, v in finals:
                        eng.wait_ge(sems[k], v)
            return body

        with nc.Block() as blk:
            blk.tensor(run("pe"))
            blk.scalar(run("act"))
            blk.vector(run("dve"))
            blk.gpsimd(run("pool"))
            blk.sync(run("sp"))


D = 2048
NTOK = 2304
NTILE = 18
NCTX = 256
NSEQ = 2048
TG = [(0, 256), (256, 512), (768, 512), (1280, 512), (1792, 512)]
PI = float(np.pi)


class Cx:
    def __init__(self, P, wcols=256):
        self.P = P
        self.psf = [P.ps([128, 512], F32, f"psf{i}") for i in range(6)]
        self.pst = [P.ps([128, 1024], BF16, f"pst{i}") for i in range(2)]
        self.ipf = 0
        self.ipt = 0
        self.wcols = wcols
        self.wi = 0
        identf = P.sb([128, 128], F32, "identf")
        self.ident = P.sb([128, 128], BF16, "ident")
        P.op("pool", "memset", identf[:, :], 0.0)
        P.op("pool", "affine_select", identf[:, :], identf[:, :], [[1, 128]], ALU.not_equal, 1.0,
             base=0, channel_multiplier=-1)
        P.op("dve", "tensor_copy", self.ident[:, :], identf[:, :])
        self.identf = identf
        self.cst = P.sb([128, 8], F32, "cst")
        for i, v in enumerate([1e-6, -PI, 1.0, 0.0]):
            P.op("dve", "memset", self.cst[:, i:i + 1], float(v))
        self.onesb = P.sb([128, 128], BF16, "onesb")
        P.op("dve", "memset", self.onesb[:, :], 1.0)

    def alloc_w(self):
        P = self.P
        nf = 2 if self.wcols > 128 else 3
        self.wf = [P.sb([128, 16, self.wcols], F32, f"wf{i}") for i in range(nf)]
        self.wb = [P.sb([128, 16, self.wcols], BF16, f"wb{i}") for i in range(2)]

    def eps(self, n=128):
        return self.cst[:n, 0:1]

    def negpi(self, n=128):
        return self.cst[:n, 1:2]

    def bank(self):
        b = self.psf[self.ipf % len(self.psf)]
        self.ipf += 1
        return b

    def tbank(self):
        b = self.pst[self.ipt % len(self.pst)]
        self.ipt += 1
        return b


def tgroups(t_lo, t_hi):
    a, e = t_lo * 128, t_hi * 128
    out = []
    if a < NCTX:
        out.append((a, min(e, NCTX) - a))
        a = min(e, NCTX)
    while a < e:
        l = min(512, e - a)
        out.append((a, l))
        a += l
    return out


class HT:
    def __init__(self, cx, hT_d, t_lo, t_hi):
        P = cx.P
        self.base = t_lo * 128
        self.parts = []
        for gi, (t0, tl) in enumerate(tgroups(t_lo, t_hi)):
            b = P.sb([128, 16, tl], BF16, f"hTg{gi}")
            P.dma(b[:, :, :], hT_d[:, :, t0:t0 + tl], q=("sp" if gi % 2 == 0 else "pool"))
            self.parts.append((t0 - self.base, tl, b))

    def __getitem__(self, idx):
        p, kk, sl = idx
        a, e = sl.start, sl.stop
        for (r0, tl, b) in self.parts:
            if r0 <= a and e <= r0 + tl:
                return b[p, kk, a - r0:e - r0]
        raise IndexError((a, e))


def emit_h(cx, xin, modT, t_lo=0, t_hi=NTILE, ctx_tiles=2):
    P = cx.P
    hT_d = getattr(cx, "hT_d", None)
    if hT_d is not None:
        return HT(cx, hT_d, t_lo, t_hi)
    hT = P.sb([128, 16, (t_hi - t_lo) * 128], BF16, "hT")
    with P.scope():
        _emit_h_body(cx, xin, modT, hT, t_lo, t_hi, ctx_tiles)
    return hT


def _emit_h_body(cx, xin, modT, hT, t_lo, t_hi, ctx_tiles):
    P = cx.P
    xts = [P.sb([128, D], F32, f"xt{i}") for i in range(2)]
    xns = [P.sb([128, D], BF16, f"xn{i}") for i in range(2)]
    sts = [P.sb([128, 4, 6], F32, f"lnst{i}") for i in range(2)]
    mvs = [P.sb([128, 2], F32, f"lnmv{i}") for i in range(2)]
    rss = [P.sb([128, 1], F32, f"lnrs{i}") for i in range(2)]
    for t in range(t_lo, t_hi):
        tt = t - t_lo
        xt, xn, st, mv, rs = xts[t % 2], xns[t % 2], sts[t % 2], mvs[t % 2], rss[t % 2]
        P.dma(xt[:, :], xin[t * 128:(t + 1) * 128, :], q=("sp" if t % 2 == 0 else "pool"))
        for c in range(4):
            P.op("dve", "bn_stats", st[:, c, :], xt[:, c * 512:(c + 1) * 512])
        P.op("dve", "bn_aggr", mv[:, :], st[:, :, :])
        P.op("act", "activation", rs[:, :], mv[:, 1:2], AF.Sqrt, bias=cx.eps(), scale=1.0)
        P.op("dve", "reciprocal", rs[:, :], rs[:, :])
        P.op("dve", "tensor_scalar", xn[:, :], xt[:, :], mv[:, 0:1], rs[:, 0:1], ALU.subtract, ALU.mult)
        mo = 32 if t < ctx_tiles else 0
        for g in range(2):
            pt = cx.tbank()
            for j in range(8):
                k = g * 8 + j
                P.op("pe", "transpose", pt[:, j * 128:(j + 1) * 128], xn[:, k * 128:(k + 1) * 128], cx.ident[:, :])
            for j in range(8):
                k = g * 8 + j
                P.op("act", "activation", hT[:, k, tt * 128:(tt + 1) * 128], pt[:, j * 128:(j + 1) * 128],
                     AF.Identity, bias=modT[:, mo + k:mo + k + 1], scale=modT[:, mo + 16 + k:mo + 17 + k])


def load_modT(cx, modT_d):
    P = cx.P
    m = P.sb([128, 64], F32, "modT_sb")
    P.dma(m[:, :], modT_d.v)
    P.op("dve", "tensor_scalar", m[:, 16:32], m[:, 16:32], 1.0, None, ALU.add)
    P.op("dve", "tensor_scalar", m[:, 48:64], m[:, 48:64], 1.0, None, ALU.add)
    return m


def project(cx, hT, wd, col0, n, mode, evac, cast_eng="pool", t_lo=0, t_hi=NTILE):
    P = cx.P
    W = cx.wcols
    groups = [(c0, min(W, n - c0)) for c0 in range(0, n, W)]
    base = cx.wi
    cx.wi += len(groups)

    nf = len(cx.wf)

    def issue(gi):
        c0, cw = groups[gi]
        wf = cx.wf[(base + gi) % nf]
        P.dma(wf[:, :, :cw], wd[:, col0 + c0:col0 + c0 + cw].rearrange("(k p) n -> p k n", p=128), q="sp")

    for gi in range(min(nf - 1, len(groups))):
        issue(gi)
    for gi, (c0, cw) in enumerate(groups):
        if gi + nf - 1 < len(groups):
            issue(gi + nf - 1)
        wf, wb = cx.wf[(base + gi) % nf], cx.wb[(base + gi) % 2]
        if (base + gi) % 2 == 0:
            P.op("dve", "tensor_copy", wb[:, :, :cw], wf[:, :, :cw])
        else:
            P.op("act", "activation", wb[:, :, :cw], wf[:, :, :cw], AF.Identity)
        if mode == "fm":
            for j in range(0, cw, 128):
                jw = min(128, cw - j)
                for (t0, tl) in tgroups(t_lo, t_hi):
                    ps = cx.bank()
                    h0 = t0 - t_lo * 128
                    for k in range(16):
                        P.op("pe", "matmul", ps[:jw, :tl], wb[:, k, j:j + jw], hT[:, k, h0:h0 + tl],
                             start=(k == 0), stop=(k == 15))
                    evac(ps, c0 + j, jw, t0, tl)
        else:
            for t in range(t_lo, t_hi):
                ps = cx.bank()
                tt = t - t_lo
                for k in range(16):
                    P.op("pe", "matmul", ps[:, :cw], hT[:, k, tt * 128:(tt + 1) * 128], wb[:, k, :cw],
                         start=(k == 0), stop=(k == 15))
                evac(ps, c0, cw, t * 128, 128)


def conv3(P, eng, out, pre, w, b, a, e):
    P.op(eng, "tensor_scalar", out[:, a:e], pre[:, a:e], w[:, 1:2], b, ALU.mult, ALU.add)
    P.op("dve", "scalar_tensor_tensor", out[:, a + 1:e], pre[:, a:e - 1], w[:, 0:1], out[:, a + 1:e], ALU.mult, ALU.add)
    P.op("dve", "scalar_tensor_tensor", out[:, a:e - 1], pre[:, a + 1:e], w[:, 2:3], out[:, a:e - 1], ALU.mult, ALU.add)


def build_mod():
    nc = bass.Bass("TRN2", target_bir_lowering=False)
    P = Prog(nc)
    cT = P.dram("cT", [128, 16, 5], F32, kind="ExternalInput")
    wa = P.dram("wa", [2048, 768], F32, kind="ExternalInput")
    ba = P.dram("ba", [128, 6], F32, kind="ExternalInput")
    out = P.dram("modT", [128, 6, 5], F32, kind="ExternalOutput")
    cs = P.sb([128, 16, 5], F32, "cs")
    bs = P.sb([128, 6], F32, "bs")
    ws = P.sb([128, 16, 768], F32, "ws")
    P.dma(cs[:, :, :], cT.v)
    P.dma(bs[:, :], ba.v)
    P.dma(ws[:, :, :], wa.v.rearrange("(k p) n -> p k n", p=128))
    P.op("act", "activation", cs[:, :, :], cs[:, :, :], AF.Silu)
    ps = P.ps([128, 512], F32, "ps")
    o = P.sb([128, 6, 5], F32, "o")
    for j in range(6):
        for k in range(16):
            P.op("pe", "matmul", ps[:, j * 8:j * 8 + 5], ws[:, k, j * 128:(j + 1) * 128], cs[:, k, :],
                 start=(k == 0), stop=(k == 15))
    for j in range(6):
        P.op("dve", "tensor_scalar", o[:, j, :], ps[:, j * 8:j * 8 + 5], bs[:, j:j + 1], None, ALU.add)
    P.dma(out.v, o[:, :, :], is_out=True)
    P.emit()
    return nc


def sin_mlp_layer(cx, outT, ps, n, bcol, fcol, tmp, tmpi):
    P = cx.P
    I2PI = 1.0 / (2 * PI)
    P.op("dve", "tensor_scalar", tmp[:64, :n], ps[:64, :n], bcol, fcol, ALU.add, ALU.mult)
    P.op("dve", "tensor_scalar", tmp[:64, :n], tmp[:64, :n], I2PI, 16.0, ALU.mult, ALU.add)
    P.op("dve", "tensor_copy", tmpi[:64, :n], tmp[:64, :n])
    P.op("dve", "tensor_copy", outT[:64, :n], tmpi[:64, :n])
    P.op("dve", "tensor_tensor", tmp[:64, :n], tmp[:64, :n], outT[:64, :n], ALU.subtract)
    P.op("act", "activation", outT[:64, :n], tmp[:64, :n], AF.Sin, scale=6.28318)


def hy_filter(cx, n, featsT_d, dec_d, mlp, gp, gm):
    P = cx.P
    w1, w2, w3, pb = mlp
    nch = n // 128
    fT = P.sb([33, n], F32, f"featsT{n}")
    P.dma(fT[:, :], featsT_d.v)
    decs = [P.sb([128, 512], F32, f"dec{n}_{i}") for i in range(2)]
    h1 = P.sb([64, n], F32, f"h1T{n}")
    h2 = P.sb([64, n], F32, f"h2T{n}")
    tmp = P.sb([64, 512], F32, f"mlptmp{n}")
    tmpi = P.sb([64, 512], mybir.dt.int32, f"mlptmpi{n}")
    for m0 in range(0, n, 512):
        ml = min(512, n - m0)
        ps = cx.bank()
        P.op("pe", "matmul", ps[:64, :ml], w1[:, :], fT[:, m0:m0 + ml], start=True, stop=True)
        sin_mlp_layer(cx, h1[:, m0:m0 + ml], ps, ml, pb[:, 0:1], pb[:, 1:2], tmp, tmpi)
    for m0 in range(0, n, 512):
        ml = min(512, n - m0)
        ps = cx.bank()
        P.op("pe", "matmul", ps[:64, :ml], w2[:, :], h1[:, m0:m0 + ml], start=True, stop=True)
        sin_mlp_layer(cx, h2[:, m0:m0 + ml], ps, ml, pb[:, 2:3], pb[:, 3:4], tmp, tmpi)
    hb = P.sb([128, 512], F32, f"hb{n}")
    sm = P.sb([128, 512], F32, f"hsum{n}")
    df = P.sb([128, 512], F32, f"hdif{n}")
    for mc in range(nch):
        dec = decs[mc % 2]
        P.dma(dec[:, :], dec_d[:, mc, :])
        pf = cx.bank()
        pbk = cx.bank()
        P.op("pe", "matmul", pf[:, :], h2[:, mc * 128:(mc + 1) * 128], w3[:, 0:512], start=True, stop=True)
        P.op("pe", "matmul", pbk[:, :], h2[:, mc * 128:(mc + 1) * 128], w3[:, 512:1024], start=True, stop=True)
        P.op("act", "activation", hb[:, :], pbk[:, :], AF.Identity)
        if mc == 0:
            P.op("dve", "memset", hb[0:1, :], 0.0)
        P.op("dve", "tensor_tensor", sm[:, :], pf[:, :], hb[:, :], ALU.add)
        P.op("dve", "tensor_tensor", df[:, :], pf[:, :], hb[:, :], ALU.subtract)
        P.op("pool", "tensor_tensor", gp[:, mc, :], sm[:, :], dec[:, :], ALU.mult)
        P.op("pool", "tensor_tensor", gm[:, mc, :], df[:, :], dec[:, :], ALU.mult)


def hy_seq(cx, n, t0, tabs, gp, gm, ut, ufm, x0z, dcol, outT):
    P = cx.P
    fwc, fws, ivc, ivs = tabs
    nch = n // 128
    TW = 256
    K = [P.sb([128, nch, 512], BF16, f"Kr{n}"), P.sb([128, nch, 512], BF16, f"Ki{n}")]
    sc = [P.sb([128, nch * TW], BF16, f"slc{n}_{i}") for i in range(2)]
    ss = [P.sb([128, nch * TW], BF16, f"sls{n}_{i}") for i in range(2)]
    ur = P.sb([128, 512], F32, f"ur{n}")
    ui = P.sb([128, 512], F32, f"ui{n}")
    t1 = P.sb([128, 512], F32, f"t1{n}")
    t2 = P.sb([128, 512], F32, f"t2{n}")
    t3 = P.sb([128, 512], F32, f"t3{n}")
    t4 = P.sb([128, 512], F32, f"t4{n}")
    tile0 = t0 // 128
    Kr, Ki = K
    for fc in range(nch):
        c_, s_ = sc[fc % 2], ss[fc % 2]
        P.dma(c_[:, :nch * 128], fwc[fc], q="sp")
        P.dma(s_[:, :nch * 128], fws[fc], q="pool")
        cv = c_[:, :nch * 128].rearrange("p (k f) -> p k f", k=nch)
        sv = s_[:, :nch * 128].rearrange("p (k f) -> p k f", k=nch)
        pr, pi = cx.bank(), cx.bank()
        for k in range(nch):
            P.op("pe", "matmul", pr[:, :], cv[:, k, :], gp[:, k, :], start=(k == 0), stop=(k == nch - 1))
        for k in range(nch):
            P.op("pe", "matmul", pi[:, :], sv[:, k, :], gm[:, k, :], start=(k == 0), stop=(k == nch - 1))
        P.op("act", "activation", Kr[:, fc, :], pr[:, :], AF.Identity)
        P.op("act", "activation", Ki[:, fc, :], pi[:, :], AF.Identity)
        pr, pi = cx.bank(), cx.bank()
        for k in range(nch):
            P.op("pe", "matmul", pr[:, :], cv[:, k, :], ut[:, tile0 + k, :], start=(k == 0), stop=(k == nch - 1))
        for k in range(nch):
            P.op("pe", "matmul", pi[:, :], sv[:, k, :], ut[:, tile0 + k, :], start=(k == 0), stop=(k == nch - 1))
        P.op("act", "activation", ur[:, :], pr[:, :], AF.Identity)
        P.op("act", "activation", ui[:, :], pi[:, :], AF.Identity)
        P.op("dve", "tensor_tensor", t1[:, :], ur[:, :], Kr[:, fc, :], ALU.mult)
        P.op("pool", "tensor_tensor", t2[:, :], ui[:, :], Ki[:, fc, :], ALU.mult)
        P.op("dve", "tensor_tensor", t3[:, :], ur[:, :], Ki[:, fc, :], ALU.mult)
        P.op("pool", "tensor_tensor", t4[:, :], ui[:, :], Kr[:, fc, :], ALU.mult)
        P.op("dve", "tensor_tensor", Kr[:, fc, :], t1[:, :], t2[:, :], ALU.subtract)
        P.op("dve", "tensor_tensor", Ki[:, fc, :], t3[:, :], t4[:, :], ALU.add)
    Yr, Yi = Kr, Ki
    ores = [P.sb([128, TW], F32, f"hyo{n}_{i}") for i in range(2)]
    ob = [P.sb([128, TW], BF16, f"hyob{n}_{i}") for i in range(2)]
    it = 0
    for tg in range(n // TW):
        c_, s_ = sc[tg % 2], ss[tg % 2]
        P.dma(c_[:, :], ivc[tg], q="sp")
        P.dma(s_[:, :], ivs[tg], q="pool")
        cv = c_[:, :].rearrange("p (k t) -> p k t", k=nch)
        sv = s_[:, :].rearrange("p (k t) -> p k t", k=nch)
        a = t0 + tg * TW
        for cc in range(4):
            ps = cx.bank()
            for k in range(nch):
                P.op("pe", "matmul", ps[:, :TW], Yr[:, k, cc * 128:(cc + 1) * 128], cv[:, k, :], start=(k == 0), stop=False)
            for k in range(nch):
                P.op("pe", "matmul", ps[:, :TW], Yi[:, k, cc * 128:(cc + 1) * 128], sv[:, k, :], start=False, stop=(k == nch - 1))
            o, o2 = ores[it % 2], ob[it % 2]
            it += 1
            P.op("dve", "scalar_tensor_tensor", o[:, :], ufm[:, cc, a:a + TW], dcol[:, cc:cc + 1], ps[:, :TW], ALU.mult, ALU.add)
            P.op("pool", "tensor_tensor", o2[:, :], o[:, :], x0z[:, cc, a:a + TW], ALU.mult)
            P.dma(outT[cc, :, a:a + TW], o2[:, :], q="sp", is_out=True)


def decl_hy(P, pfx, io=True):
    xin = (P.dram(pfx + "xin", [NTOK, D], F32, kind="ExternalInput") if io else None)
    modT_d = (P.dram(pfx + "modT", [128, 64], F32, kind="ExternalInput") if io else None)
    w = P.dram(pfx + "w", [D, 2048], F32, kind="ExternalInput")
    cw_d = P.dram(pfx + "cw", [128, 12, 4], F32, kind="ExternalInput")
    dsk_d = P.dram(pfx + "dsk", [128, 4], F32, kind="ExternalInput")
    w1_d = P.dram(pfx + "w1", [33, 64], F32, kind="ExternalInput")
    w2_d = P.dram(pfx + "w2", [64, 64], F32, kind="ExternalInput")
    w3_d = P.dram(pfx + "w3", [64, 1024], F32, kind="ExternalInput")
    pb_d = P.dram(pfx + "pb", [64, 4], F32, kind="ExternalInput")
    fL = P.dram(pfx + "featsL", [33, 2048], F32, kind="ExternalInput")
    fC = P.dram(pfx + "featsC", [33, 256], F32, kind="ExternalInput")
    dL = P.dram(pfx + "decL", [128, 16, 512], F32, kind="ExternalInput")
    dC = P.dram(pfx + "decC", [128, 2, 512], F32, kind="ExternalInput")
    tabL = [P.dram(pfx + nm, [16, 128, 2048], BF16, kind="ExternalInput") for nm in ("fwcL", "fwsL")] + \
           [P.dram(pfx + nm, [8, 128, 16 * 256], BF16, kind="ExternalInput") for nm in ("ivcL", "ivsL")]
    tabC = [P.dram(pfx + nm, [2, 128, 256], BF16, kind="ExternalInput") for nm in ("fwcC", "fwsC")] + \
           [P.dram(pfx + nm, [1, 128, 2 * 256], BF16, kind="ExternalInput") for nm in ("ivcC", "ivsC")]
    outT = (P.dram(pfx + "gT", [4, 128, NTOK], BF16, kind="ExternalOutput") if io else None)
    return dict(xin=xin, modT_d=modT_d, w=w, cw_d=cw_d, dsk_d=dsk_d, w1_d=w1_d, w2_d=w2_d, w3_d=w3_d, pb_d=pb_d, fL=fL, fC=fC, dL=dL, dC=dC, tabL=tabL, tabC=tabC, outT=outT)


def body_hy(cx, xin, modT, T, outT):
    P = cx.P
    w = T["w"]
    cw_d = T["cw_d"]
    dsk_d = T["dsk_d"]
    w1_d = T["w1_d"]
    w2_d = T["w2_d"]
    w3_d = T["w3_d"]
    pb_d = T["pb_d"]
    fL = T["fL"]
    fC = T["fC"]
    dL = T["dL"]
    dC = T["dC"]
    tabL = T["tabL"]
    tabC = T["tabC"]
    _psf_saved = list(cx.psf)
    cw = P.sb([128, 12, 4], F32, "cw")
    dsk = P.sb([128, 4], F32, "dsk")
    P.dma(cw[:, :, :], cw_d.v)
    P.dma(dsk[:, :], dsk_d.v)
    w1 = P.sb([33, 64], F32, "w1")
    w2 = P.sb([64, 64], F32, "w2")
    w3 = P.sb([64, 1024], F32, "w3")
    pb = P.sb([64, 4], F32, "pb")
    for s_, d_ in ((w1, w1_d), (w2, w2_d), (w3, w3_d), (pb, pb_d)):
        P.dma(s_.v, d_.v)
    x0z = P.sb([128, 4, NTOK], BF16, "x0z")
    ufm = P.sb([128, 4, NTOK], BF16, "ufm")
    ut = P.sb([128, NTILE, 512], BF16, "ut")
    with P.scope():
        hT = emit_h(cx, xin, modT)
        cx.alloc_w()
        pre = P.sb([128, NTOK], F32, "pre")
        cv = P.sb([128, NTOK], F32, "cvtmp")
        x1c = P.sb([128, 4, NTOK], BF16, "x1c")

        def evac(ps, c, cwid, t0, tl):
            j = c // 128
            if j >= 12:
                P.op("act", "activation", pre[:, t0:t0 + tl], ps[:, :tl], AF.Silu)
                P.op("dve", "tensor_tensor", x0z[:, j - 12, t0:t0 + tl], x0z[:, j - 12, t0:t0 + tl], pre[:, t0:t0 + tl], ALU.mult)
                return
            P.op("act", "activation", pre[:, t0:t0 + tl], ps[:, :tl], AF.Identity)
            if t0 + tl == NTOK:
                dst = x0z[:, j, :] if j < 4 else (x1c[:, j - 4, :] if j < 8 else cv[:, :])
                for (a, e) in ((0, NCTX), (NCTX, NTOK)):
                    conv3(P, "pool", dst, pre, cw[:, j, 0:3], cw[:, j, 3:4], a, e)
                if j >= 8:
                    cc = j - 8
                    P.op("dve", "tensor_tensor", ufm[:, cc, :], x1c[:, cc, :], cv[:, :], ALU.mult)
                    for t in range(NTILE):
                        if t % 8 == 0:
                            pt = cx.tbank()
                        P.op("pe", "transpose", pt[:, (t % 8) * 128:(t % 8 + 1) * 128], ufm[:, cc, t * 128:(t + 1) * 128], cx.ident[:, :])
                        if t % 8 == 7 or t == NTILE - 1:
                            tb = (t // 8) * 8
                            nt = t - tb + 1
                            P.op("act", "activation", ut[:, tb:tb + nt, cc * 128:(cc + 1) * 128],
                                 pt[:, :nt * 128].rearrange("p (t c) -> p t c", t=nt), AF.Identity)

        project(cx, hT, w, 0, 2048, "fm", evac)
    for (n, t0, fd, dd, tabs) in ((256, 0, fC, dC, tabC), (2048, NCTX, fL, dL, tabL)):
        with P.scope():
            gp = P.sb([128, n // 128, 512], BF16, f"gp{n}")
            gm = P.sb([128, n // 128, 512], BF16, f"gm{n}")
            with P.scope():
                hy_filter(cx, n, fd, dd, (w1, w2, w3, pb), gp, gm)
            hy_seq(cx, n, t0, tabs, gp, gm, ut, ufm, x0z, dsk, outT)
    cx.psf = _psf_saved


def build_hy():
    nc = bass.Bass("TRN2", target_bir_lowering=False)
    P = Prog(nc)
    T = decl_hy(P, "")
    cx = Cx(P, wcols=128)
    modT = load_modT(cx, T['modT_d'])
    body_hy(cx, T['xin'], modT, T, T['outT'])
    P.emit()
    return nc


def rope_blocks(P, out_bf, src, CS, SN, tA, tB, nblk, tl, c0):
    n = 64 * nblk
    P.op("dve", "tensor_tensor", tA[:n, :tl], src[:n, :tl], CS[:n, c0:c0 + tl], ALU.mult)
    for b in range(nblk):
        lo, hi = slice(64 * b, 64 * b + 32), slice(64 * b + 32, 64 * b + 64)
        P.op("pool", "tensor_tensor", tB[lo, :tl], src[hi, :tl], SN[hi, c0:c0 + tl], ALU.mult)
        P.op("pool", "tensor_tensor", tB[hi, :tl], src[lo, :tl], SN[lo, c0:c0 + tl], ALU.mult)
        P.op("dve", "tensor_tensor", out_bf[lo, :tl], tA[lo, :tl], tB[lo, :tl], ALU.subtract)
        P.op("dve", "tensor_tensor", out_bf[hi, :tl], tA[hi, :tl], tB[hi, :tl], ALU.add)


def decl_mla(P, pfx, io=True):
    xin = (P.dram(pfx + "xin", [NTOK, D], F32, kind="ExternalInput") if io else None)
    modT_d = (P.dram(pfx + "modT", [128, 64], F32, kind="ExternalInput") if io else None)
    w = P.dram(pfx + "w", [D, 1408], F32, kind="ExternalInput")
    wuq_d = P.dram(pfx + "wuq", [128, 4, 768], F32, kind="ExternalInput")
    wukv_d = P.dram(pfx + "wukv", [128, 2, 1024], F32, kind="ExternalInput")
    nrm_d = P.dram(pfx + "nrm", [128, 6], F32, kind="ExternalInput")
    cs_d = P.dram(pfx + "ropeC", [128, NSEQ], F32, kind="ExternalInput")
    sn_d = P.dram(pfx + "ropeS", [128, NSEQ], F32, kind="ExternalInput")
    outT = (P.dram(pfx + "gT", [4, 128, NTOK], BF16, kind="ExternalOutput") if io else None)
    return dict(xin=xin, modT_d=modT_d, w=w, wuq_d=wuq_d, wukv_d=wukv_d, nrm_d=nrm_d, cs_d=cs_d, sn_d=sn_d, outT=outT)


def body_mla(cx, xin, modT, T, outT):
    P = cx.P
    w = T["w"]
    wuq_d = T["wuq_d"]
    wukv_d = T["wukv_d"]
    nrm_d = T["nrm_d"]
    cs_d = T["cs_d"]
    sn_d = T["sn_d"]
    _psf_saved = list(cx.psf)
    zs = P.sb([128, 4, NTOK], BF16, "zs")
    kpT = P.sb([128, NTOK], BF16, "kpT")
    qnT = P.sb([128, 4, NTOK], BF16, "qnT")
    qpT = P.sb([128, 2, NTOK], BF16, "qpT")
    knT = P.sb([128, 4, NTOK], BF16, "knT")
    vtm = P.sb([128, NTILE, 512], BF16, "vtm")
    with P.scope():
        cqT = P.sb([128, 4, NTOK], BF16, "cqT")
        ckvT = P.sb([128, 2, NTOK], BF16, "ckvT")
        kpf = P.sb([128, NTOK], F32, "kpf")
        for (ta, tb) in ((0, 9), (9, 18)):
            with P.scope():
                hT = emit_h(cx, xin, modT, ta, tb)
                cx.alloc_w()

                def evac(ps, c, cwid, t0, tl):
                    j = c // 128
                    if j < 4:
                        P.op("act", "activation", cqT[:, j, t0:t0 + tl], ps[:, :tl], AF.Identity)
                    elif j < 6:
                        P.op("act", "activation", ckvT[:, j - 4, t0:t0 + tl], ps[:, :tl], AF.Identity)
                    elif j == 6:
                        P.op("act", "activation", kpf[:, t0:t0 + tl], ps[:, :tl], AF.Identity)
                    else:
                        P.op("act", "activation", zs[:, j - 7, t0:t0 + tl], ps[:, :tl], AF.Silu)

                project(cx, hT, w, 0, 1408, "fm", evac, t_lo=ta, t_hi=tb)
        CS = P.sb([128, NSEQ], F32, "CS")
        SN = P.sb([128, NSEQ], F32, "SN")
        P.dma(CS[:, :], cs_d.v)
        P.dma(SN[:, :], sn_d.v, q="pool")
        nrm = P.sb([128, 6], F32, "nrm")
        P.dma(nrm[:, :], nrm_d.v)
        wq = P.sb([128, 4, 768], BF16, "wq")
        wkv = P.sb([128, 2, 1024], BF16, "wkv")
        with P.scope():
            wq_f = P.sb([128, 4, 768], F32, "wq_f")
            wkv_f = P.sb([128, 2, 1024], F32, "wkv_f")
            P.dma(wq_f[:, :, :], wuq_d.v)
            P.dma(wkv_f[:, :, :], wukv_d.v, q="pool")
            for r in range(4):
                P.op("dve", "tensor_scalar", wq[:, r, :], wq_f[:, r, :], nrm[:, r:r + 1], None, ALU.mult)
            for r in range(2):
                P.op("dve", "tensor_scalar", wkv[:, r, :], wkv_f[:, r, :], nrm[:, 4 + r:5 + r], None, ALU.mult)
        rq = P.sb([128, NTOK], F32, "rq")
        rk_ = P.sb([128, NTOK], F32, "rk")
        rktm = P.sb([128, NTILE], F32, "rktm")
        tgs = tgroups(0, NTILE)
        sq = P.sb([128, 4, 512], BF16, "sq")
        skv = P.sb([128, 2, 512], BF16, "skv")
        allps = list(cx.psf)
        pstm = allps[5]
        cx.psf = allps[:5]
        for (t0, tl) in tgs:
            P.op("act", "activation", sq[:, :, :tl], cqT[:, :, t0:t0 + tl], AF.Square)
            P.op("act", "activation", skv[:, :, :tl], ckvT[:, :, t0:t0 + tl], AF.Square)
            ps = cx.bank()
            for r in range(4):
                P.op("pe", "matmul", ps[:, :tl], cx.onesb[:, :], sq[:, r, :tl], start=(r == 0), stop=(r == 3))
            P.op("act", "activation", rq[:, t0:t0 + tl], ps[:, :tl], AF.Sqrt, bias=cx.eps(), scale=1.0 / 512)
            ps = cx.bank()
            for r in range(2):
                P.op("pe", "matmul", ps[:, :tl], cx.onesb[:, :], skv[:, r, :tl], start=(r == 0), stop=(r == 1))
            P.op("act", "activation", rk_[:, t0:t0 + tl], ps[:, :tl], AF.Sqrt, bias=cx.eps(), scale=1.0 / 256)
            for ti in range(tl // 128):
                t = t0 // 128 + ti
                for r in range(2):
                    P.op("pe", "matmul", pstm[:, t:t + 1], skv[:, r, ti * 128:(ti + 1) * 128], cx.onesb[:, 0:1], start=(r == 0), stop=(r == 1))
        P.op("act", "activation", rktm[:, :], pstm[:, :NTILE], AF.Sqrt, bias=cx.eps(), scale=1.0 / 256)
        P.op("dve", "reciprocal", rq[:, :], rq[:, :])
        P.op("dve", "reciprocal", rk_[:, :], rk_[:, :])
        P.op("dve", "reciprocal", rktm[:, :], rktm[:, :])
        P.op("pool", "tensor_scalar", rq[:, :], rq[:, :], float(192 ** -0.5), None, ALU.mult)
        tA = P.sb([128, 512], F32, "ropeA")
        tB = P.sb([128, 512], F32, "ropeB")
        tS = P.sb([128, 512], F32, "ropeSrc")
        for (t0, tl) in tgs:
            lat = t0 >= NCTX
            for hh in range(4):
                ps = cx.bank()
                for r in range(4):
                    P.op("pe", "matmul", ps[:, :tl], wq[:, r, hh * 128:(hh + 1) * 128], cqT[:, r, t0:t0 + tl], start=(r == 0), stop=(r == 3))
                P.op("dve", "tensor_tensor", qnT[:, hh, t0:t0 + tl], ps[:, :tl], rq[:, t0:t0 + tl], ALU.mult)
                ps = cx.bank()
                for r in range(2):
                    P.op("pe", "matmul", ps[:, :tl], wkv[:, r, hh * 128:(hh + 1) * 128], ckvT[:, r, t0:t0 + tl], start=(r == 0), stop=(r == 1))
                P.op("dve", "tensor_tensor", knT[:, hh, t0:t0 + tl], ps[:, :tl], rk_[:, t0:t0 + tl], ALU.mult)
            for pr in range(2):
                ps = cx.bank()
                for r in range(4):
                    P.op("pe", "matmul", ps[:, :tl], wq[:, r, 512 + pr * 128:512 + (pr + 1) * 128], cqT[:, r, t0:t0 + tl], start=(r == 0), stop=(r == 3))
                if lat:
                    P.op("dve", "tensor_tensor", tS[:, :tl], ps[:, :tl], rq[:, t0:t0 + tl], ALU.mult)
                    rope_blocks(P, qpT[:, pr, t0:t0 + tl], tS, CS, SN, tA, tB, 2, tl, t0 - NCTX)
                else:
                    P.op("dve", "tensor_tensor", qpT[:, pr, t0:t0 + tl], ps[:, :tl], rq[:, t0:t0 + tl], ALU.mult)
            if lat:
                rope_blocks(P, kpT[:, t0:t0 + tl], kpf[:, t0:t0 + tl], CS, SN, tA, tB, 2, tl, t0 - NCTX)
            else:
                P.op("dve", "tensor_copy", kpT[:, t0:t0 + tl], kpf[:, t0:t0 + tl])
        for t in range(NTILE):
            ps = cx.bank()
            for r in range(2):
                P.op("pe", "matmul", ps[:, :], ckvT[:, r, t * 128:(t + 1) * 128], wkv[:, r, 512:1024], start=(r == 0), stop=(r == 1))
            P.op("act", "activation", vtm[:, t, :], ps[:, :], AF.Identity, scale=rktm[:, t:t + 1])
    pts = [P.sb([128, 512], BF16, f"pT{i}") for i in range(3)]
    rec = P.sb([128, 512], F32, "rec")
    yo = [P.sb([128, 512], F32, f"yo{i}") for i in range(2)]
    yb = [P.sb([128, 512], BF16, f"yb{i}") for i in range(2)]
    acc_y = allps[4]
    acc_s = allps[5]
    cx.psf = allps[:4]
    ip = 0
    io = 0
    for hh in range(4):
        pb0 = 64 * (hh % 2)
        for (q0, ql) in tgs:
            nkt = 2 if q0 < NCTX else NTILE
            for kt in range(nkt):
                ps = cx.bank()
                ks = slice(kt * 128, (kt + 1) * 128)
                P.op("pe", "matmul", ps[:, :ql], knT[:, hh, ks], qnT[:, hh, q0:q0 + ql], start=True, stop=False)
                P.op("pe", "matmul", ps[:, :ql], kpT[pb0:pb0 + 64, ks], qpT[pb0:pb0 + 64, hh // 2, q0:q0 + ql], start=False, stop=True)
                pT = pts[ip % 3]
                ip += 1
                P.op("act", "activation", pT[:, :ql], ps[:, :ql], AF.Exp)
                P.op("pe", "matmul", acc_y[:, :ql], vtm[:, kt, hh * 128:(hh + 1) * 128], pT[:, :ql], start=(kt == 0), stop=(kt == nkt - 1))
                P.op("pe", "matmul", acc_s[:, :ql], cx.onesb[:, :], pT[:, :ql], start=(kt == 0), stop=(kt == nkt - 1))
            o, o2 = yo[io % 2], yb[io % 2]
            io += 1
            P.op("dve", "reciprocal", rec[:, :ql], acc_s[:, :ql])
            P.op("dve", "tensor_tensor", o[:, :ql], acc_y[:, :ql], rec[:, :ql], ALU.mult)
            P.op("pool", "tensor_tensor", o2[:, :ql], o[:, :ql], zs[:, hh, q0:q0 + ql], ALU.mult)
            P.dma(outT[hh, :, q0:q0 + ql], o2[:, :ql], is_out=True)
    cx.psf = _psf_saved


def build_mla():
    nc = bass.Bass("TRN2", target_bir_lowering=False)
    P = Prog(nc)
    T = decl_mla(P, "")
    cx = Cx(P, wcols=128)
    modT = load_modT(cx, T['modT_d'])
    body_mla(cx, T['xin'], modT, T, T['outT'])
    P.emit()
    return nc


def decl_gqa(P, pfx, io=True):
    xin = (P.dram(pfx + "xin", [NTOK, D], F32, kind="ExternalInput") if io else None)
    modT_d = (P.dram(pfx + "modT", [128, 64], F32, kind="ExternalInput") if io else None)
    w = P.dram(pfx + "w", [D, 1280], F32, kind="ExternalInput")
    cs_d = P.dram(pfx + "ropeC", [128, NSEQ], F32, kind="ExternalInput")
    sn_d = P.dram(pfx + "ropeS", [128, NSEQ], F32, kind="ExternalInput")
    sink_d = P.dram(pfx + "sink", [128, 4], F32, kind="ExternalInput")
    msk_d = P.dram(pfx + "msk", [128, 2, 128], BF16, kind="ExternalInput")
    outT = (P.dram(pfx + "gT", [4, 128, NTOK], BF16, kind="ExternalOutput") if io else None)
    return dict(xin=xin, modT_d=modT_d, w=w, cs_d=cs_d, sn_d=sn_d, sink_d=sink_d, msk_d=msk_d, outT=outT)


def body_gqa(cx, xin, modT, T, outT):
    P = cx.P
    w = T["w"]
    cs_d = T["cs_d"]
    sn_d = T["sn_d"]
    sink_d = T["sink_d"]
    msk_d = T["msk_d"]
    _psf_saved = list(cx.psf)
    zs = P.sb([128, 4, NTOK], BF16, "zs")
    qT = P.sb([128, 4, NTOK], BF16, "qT")
    kT = P.sb([128, NTOK], BF16, "kT")
    vT = P.sb([128, NTOK], BF16, "vT")
    vA = P.sb([128, NTILE, 128], BF16, "vA")
    vB = P.sb([128, NTILE, 128], BF16, "vB")
    P.op("pool", "memset", vA[:, :, :], 0.0)
    P.op("pool", "memset", vB[:, :, :], 0.0)
    onesA = P.sb([128, 128], BF16, "onesA")
    onesB = P.sb([128, 128], BF16, "onesB")
    P.op("pool", "memset", onesA[:, :], 0.0)
    P.op("pool", "memset", onesB[:, :], 0.0)
    P.op("pool", "memset", onesA[:, 0:64], 1.0)
    P.op("pool", "memset", onesB[:, 64:128], 1.0)
    msk = P.sb([128, 2, 128], BF16, "msk")
    P.dma(msk[:, :, :], msk_d.v)
    esink = P.sb([128, 4], F32, "esink")
    P.dma(esink[:, :], sink_d.v)
    P.op("act", "activation", esink[:, :], esink[:, :], AF.Exp)
    with P.scope():
        CS = P.sb([128, NSEQ], F32, "CS")
        SN = P.sb([128, NSEQ], F32, "SN")
        P.dma(CS[:, :], cs_d.v)
        P.dma(SN[:, :], sn_d.v, q="pool")
        tA = P.sb([128, 512], F32, "ropeA")
        tB = P.sb([128, 512], F32, "ropeB")
        tS = P.sb([128, 512], F32, "ropeSrc")
        for (ta, tb) in ((0, 9), (9, 18)):
            with P.scope():
                hT = emit_h(cx, xin, modT, ta, tb)
                cx.alloc_w()

                def evac(ps, c, cwid, t0, tl):
                    j = c // 128
                    if j < 5:
                        dst = qT[:, j, t0:t0 + tl] if j < 4 else kT[:, t0:t0 + tl]
                        sc = 0.125 if j < 4 else 1.0
                        if t0 >= NCTX:
                            P.op("act", "activation", tS[:, :tl], ps[:, :tl], AF.Identity, scale=sc)
                            rope_blocks(P, dst, tS, CS, SN, tA, tB, 2, tl, t0 - NCTX)
                        else:
                            P.op("act", "activation", dst, ps[:, :tl], AF.Identity, scale=sc)
                    elif j < 9:
                        P.op("act", "activation", zs[:, j - 5, t0:t0 + tl], ps[:, :tl], AF.Silu)
                    else:
                        P.op("act", "activation", vT[:, t0:t0 + tl], ps[:, :tl], AF.Identity)

                project(cx, hT, w, 0, 1280, "fm", evac, t_lo=ta, t_hi=tb)
        for t in range(NTILE):
            if t % 8 == 0:
                pt = cx.tbank()
            P.op("pe", "transpose", pt[:, (t % 8) * 128:(t % 8 + 1) * 128], vT[:, t * 128:(t + 1) * 128], cx.ident[:, :])
            if t % 8 == 7 or t == NTILE - 1:
                tb_ = (t // 8) * 8
                nt = t - tb_ + 1
                ptv = pt[:, :nt * 128].rearrange("p (t c) -> p t c", t=nt)
                P.op("act", "activation", vA[:, tb_:tb_ + nt, 0:64], ptv[:, :, 0:64], AF.Identity)
                P.op("act", "activation", vB[:, tb_:tb_ + nt, 64:128], ptv[:, :, 64:128], AF.Identity)
    qTo = P.sb([128, 4, NTOK], BF16, "qTo")
    P.op("pool", "tensor_copy", qTo[:, :, :], qT[:, :, :])
    P.op("pool", "memset", qTo[0:64, :, :], 0.0)
    P.op("dve", "memset", qT[64:128, :, :], 0.0)
    pts = [[P.sb([128, 512], BF16, f"pT{e}_{i}") for i in range(2)] for e in range(2)]
    den = P.sb([128, 512], F32, "den")
    yo = P.sb([128, 512], F32, "yo")
    stg = [P.sb([128, 4, 512], BF16, f"stg{i}") for i in range(2)]
    ogroups = [(0, 2)] + [(2 + 4 * i, 4) for i in range(4)]
    acc_y, acc_s = cx.psf[4], cx.psf[5]
    cx.psf = cx.psf[:4]
    ip = 0
    for qb in range(NTILE):
        q0 = qb * 128
        if qb < 2:
            chunks = [(0, None), (1, None)]
        else:
            chunks = []
            if qb > 2:
                chunks.append((qb - 1, 0))
            chunks.append((qb, None))
            if qb < NTILE - 1:
                chunks.append((qb + 1, 1))
            chunks += [(0, None), (1, None)]
        for ci, (kt, mk) in enumerate(chunks):
            ks = slice(kt * 128, (kt + 1) * 128)
            pTs = []
            for e in range(2):
                ps = cx.bank()
                src = qT if e == 0 else qTo
                P.op("pe", "matmul", ps[:, :].rearrange("p (j q) -> p j q", j=4), kT[:, ks], src[:, :, q0:q0 + 128], start=True, stop=True)
                pT = pts[e][ip % 2]
                P.op("act", "activation", pT[:, :], ps[:, :], AF.Exp)
                if mk is not None:
                    P.op(("dve" if e == 0 else "pool"), "tensor_tensor", pT[:, :].rearrange("p (j q) -> p j q", j=4),
                         pT[:, :].rearrange("p (j q) -> p j q", j=4), msk[:, mk:mk + 1, :].to_broadcast([128, 4, 128]), ALU.mult)
                pTs.append(pT)
            ip += 1
            first, last = (ci == 0), (ci == len(chunks) - 1)
            P.op("pe", "matmul", acc_y[:, :], vA[:, kt, :], pTs[0][:, :], start=first, stop=False)
            P.op("pe", "matmul", acc_y[:, :], vB[:, kt, :], pTs[1][:, :], start=False, stop=last)
            P.op("pe", "matmul", acc_s[:, :], onesA[:, :], pTs[0][:, :], start=first, stop=False)
            P.op("pe", "matmul", acc_s[:, :], onesB[:, :], pTs[1][:, :], start=False, stop=last)
        gi = [i for i, (a, n_) in enumerate(ogroups) if a <= qb < a + n_][0]
        ga, gn = ogroups[gi]
        sg = stg[gi % 2]
        v3 = lambda t: t[:, :].rearrange("p (j q) -> p j q", j=4)
        P.op("dve", "tensor_tensor", v3(den), v3(acc_s), esink[:, :].unsqueeze(2).to_broadcast([128, 4, 128]), ALU.add)
        P.op("dve", "reciprocal", den[:, :], den[:, :])
        P.op("dve", "tensor_tensor", yo[:, :], acc_y[:, :], den[:, :], ALU.mult)
        P.op("pool", "tensor_tensor", sg[:, :, (qb - ga) * 128:(qb - ga + 1) * 128], v3(yo), zs[:, :, q0:q0 + 128], ALU.mult)
        if qb == ga + gn - 1:
            for jj in range(4):
                P.dma(outT[jj, :, ga * 128:(ga + gn) * 128], sg[:, jj, :gn * 128], is_out=True)
    cx.psf = _psf_saved


def build_gqa():
    nc = bass.Bass("TRN2", target_bir_lowering=False)
    P = Prog(nc)
    T = decl_gqa(P, "")
    cx = Cx(P, wcols=128)
    modT = load_modT(cx, T['modT_d'])
    body_gqa(cx, T['xin'], modT, T, T['outT'])
    P.emit()
    return nc


def decl_ssd(P, pfx, io=True):
    xin = (P.dram(pfx + "xin", [NTOK, D], F32, kind="ExternalInput") if io else None)
    modT_d = (P.dram(pfx + "modT", [128, 64], F32, kind="ExternalInput") if io else None)
    w = P.dram(pfx + "w", [D, 1408], F32, kind="ExternalInput")
    cw_d = P.dram(pfx + "cw", [128, 6, 4], F32, kind="ExternalInput")
    rows_d = P.dram(pfx + "rows", [128, 16 + 16 + 8 + 512], F32, kind="ExternalInput")
    msk_d = P.dram(pfx + "msk", [128, 4, 128], F32, kind="ExternalInput")
    outT = (P.dram(pfx + "gT", [4, 128, NTOK], BF16, kind="ExternalOutput") if io else None)
    return dict(xin=xin, modT_d=modT_d, w=w, cw_d=cw_d, rows_d=rows_d, msk_d=msk_d, outT=outT)


def body_ssd(cx, xin, modT, T, outT):
    P = cx.P
    w = T["w"]
    cw_d = T["cw_d"]
    rows_d = T["rows_d"]
    msk_d = T["msk_d"]
    _psf_saved = list(cx.psf)
    cx.wcols = 256
    cw = P.sb([128, 6, 4], F32, "cw")
    rows = P.sb([128, 552], F32, "rows")
    msk = P.sb([128, 4, 128], F32, "msk")
    P.dma(cw[:, :, :], cw_d.v)
    P.dma(rows[:, :], rows_d.v)
    P.dma(msk[:, :, :], msk_d.v)
    onesf = P.sb([128, 128], F32, "onesf")
    P.op("dve", "memset", onesf[:, :], 1.0)
    abc = P.sb([128, 16], F32, "abc")
    P.op("act", "activation", abc[:, :], rows[:, 16:32], AF.Exp)
    P.op("dve", "tensor_scalar", abc[:, :], abc[:, :], -1.0, None, ALU.mult)
    xs = P.sb([128, NTILE, 512], BF16, "xs_tm")
    Btm = P.sb([128, NTILE, 128], BF16, "B_tm")
    BT = P.sb([128, NTOK], BF16, "BT")
    CT = P.sb([128, NTOK], BF16, "CT")
    zs = P.sb([128, NTILE, 512], BF16, "zs_tm")
    dt = P.sb([128, NTILE, 16], F32, "dt")
    la = P.sb([128, NTILE, 16], F32, "la")
    with P.scope():
        pre = P.sb([128, 6, NTOK], BF16, "pre")
        xT = P.sb([128, 4, NTOK], BF16, "xT")
        cvt = P.sb([128, NTOK], F32, "cvt")
        for (ta, tb) in ((0, 9), (9, 18)):
            with P.scope():
                hT = emit_h(cx, xin, modT, ta, tb)
                cx.alloc_w()

                def evac(ps, c, cwid, t0, tl):
                    P.op("act", "activation", pre[:, c // 128, t0:t0 + tl], ps[:, :tl], AF.Identity)

                def evac_tm(ps, c, cwid, t0, tl):
                    t = t0 // 128
                    if c < 512:
                        P.op("act", "activation", zs[:, t, c:c + cwid], ps[:, :cwid], AF.Silu)
                    else:
                        P.op("dve", "tensor_tensor", dt[:, t, :], ps[:, 0:16], rows[:, 0:16], ALU.add)

                project(cx, hT, w, 0, 768, "fm", evac, t_lo=ta, t_hi=tb)
                project(cx, hT, w, 768, 640, "tm", evac_tm, t_lo=ta, t_hi=tb)
        for j in range(6):
            for (a, e) in ((0, NCTX), (NCTX, NTOK)):
                conv3(P, "pool", cvt, pre[:, j, :], cw[:, j, 0:3], cw[:, j, 3:4], a, e)
            dst = xT[:, j, :] if j < 4 else (BT[:, :] if j == 4 else CT[:, :])
            P.op("act", "activation", dst, cvt[:, :], AF.Silu)
        for t in range(NTILE):
            pt = cx.tbank()
            for cc in range(4):
                P.op("pe", "transpose", pt[:, cc * 128:(cc + 1) * 128], xT[:, cc, t * 128:(t + 1) * 128], cx.ident[:, :])
            P.op("pe", "transpose", pt[:, 512:640], BT[:, t * 128:(t + 1) * 128], cx.ident[:, :])
            P.op("act", "activation", xs[:, t, :], pt[:, 0:512], AF.Identity)
            P.op("dve", "tensor_copy", Btm[:, t, :], pt[:, 512:640])
    P.op("act", "activation", dt[:, :, :], dt[:, :, :], AF.Exp)
    P.op("act", "activation", dt[:, :, :], dt[:, :, :], AF.Ln, bias=1.0, scale=1.0)
    P.op("dve", "tensor_tensor", la[:, :, :], dt[:, :, :], abc[:, :].unsqueeze(1).to_broadcast([128, NTILE, 16]), ALU.mult)
    with P.scope():
        yacc = P.sb([128, NTILE, 512], F32, "yacc")
        R = P.sb([128, 8, 128], F32, "R")
        arg = P.sb([128, 8, 128], F32, "arg")
        E = P.sb([128, 8, 128], F32, "E")
        MT = P.sb([128, 8, 128], BF16, "MT")
        Ec = P.sb([128, 8, 128], F32, "Ecum")
        CdT = P.sb([128, 8, 128], BF16, "CdT")
        sc = P.sb([128, 128], F32, "scoresT")
        cum = P.sb([128, 8], F32, "cum_tm")
        dte = P.sb([128, 8], F32, "dte")
        xd = P.sb([128, 512], BF16, "xd")
        xdw = P.sb([128, 512], BF16, "xdw")
        xdf = P.sb([128, 512], F32, "xdf")
        ST = P.sb([128, 8, 64], F32, "ST")
        STt = P.sb([128, 8, 64], F32, "STt")
        STb = P.sb([128, 8, 64], BF16, "STb")
        p_cum, p_row0, p_row1, p_sc, p_y, p_loc = cx.psf
        for t in range(NTILE):
            P.op("pool", "tensor_tensor", yacc[:, t, :].rearrange("p (j q) -> p j q", j=8),
                 xs[:, t, :].rearrange("p (j q) -> p j q", j=8),
                 rows[:, 32:40].unsqueeze(2).to_broadcast([128, 8, 64]), ALU.mult)
        for d in range(2):
            order = [0, 1] + list(range(2, NTILE)) if d == 0 else [1, 0] + list(range(NTILE - 1, 1, -1))
            tri = msk[:, d, :]
            neg = msk[:, 2 + d, :]
            edge = 127 if d == 0 else 0
            P.op("dve", "memset", ST[:, :, :], 0.0)
            P.op("dve", "memset", STb[:, :, :], 0.0)
            for c in order:
                ts_ = slice(c * 128, (c + 1) * 128)
                lac = la[:, c, 8 * d:8 * d + 8]
                P.op("pe", "matmul", p_cum[:, 0:8], tri, lac, start=True, stop=True)
                P.op("dve", "tensor_tensor", R[:, :, :], lac.unsqueeze(2).to_broadcast([128, 8, 128]),
                     tri.unsqueeze(1).to_broadcast([128, 8, 128]), ALU.mult)
                P.op("pe", "matmul", p_row0[:, :], onesf[:, :], R[:, 0:4, :].rearrange("p j l -> p (j l)"), start=True, stop=True)
                P.op("pe", "matmul", p_row1[:, :], onesf[:, :], R[:, 4:8, :].rearrange("p j l -> p (j l)"), start=True, stop=True)
                P.op("pe", "matmul", p_sc[:, 0:128], BT[:, ts_], CT[:, ts_], start=True, stop=True)
                P.op("act", "activation", cum[:, :], p_cum[:, 0:8], AF.Identity)
                P.op("act", "activation", sc[:, :], p_sc[:, 0:128], AF.Identity)
                for hf, pr in ((0, p_row0), (1, p_row1)):
                    js = slice(4 * hf, 4 * hf + 4)
                    prv = pr[:, :].rearrange("p (j l) -> p j l", j=4)
                    P.op("dve", "tensor_tensor", arg[:, js, :], prv, cum[:, js].unsqueeze(2).to_broadcast([128, 4, 128]), ALU.subtract)
                    P.op("act", "activation", Ec[:, js, :], prv, AF.Exp)
                P.op("dve", "tensor_tensor", arg[:, :, :], arg[:, :, :], neg.unsqueeze(1).to_broadcast([128, 8, 128]), ALU.add)
                P.op("act", "activation", E[:, :, :], arg[:, :, :], AF.Exp)
                P.op("dve", "tensor_tensor", MT[:, :, :], E[:, :, :], sc[:, :].unsqueeze(1).to_broadcast([128, 8, 128]), ALU.mult)
                P.op("pool", "tensor_tensor", CdT[:, :, :], Ec[:, :, :], CT[:, ts_].unsqueeze(1).to_broadcast([128, 8, 128]), ALU.mult)
                P.op("dve", "tensor_tensor", xdf[:, :].rearrange("p (j q) -> p j q", j=8), xs[:, c, :].rearrange("p (j q) -> p j q", j=8),
                     dt[:, c, 8 * d:8 * d + 8].unsqueeze(2).to_broadcast([128, 8, 64]), ALU.mult)
                P.op("pool", "tensor_copy", xd[:, :], xdf[:, :])
                for hf, pr in ((0, p_row0), (1, p_row1)):
                    js = slice(4 * hf, 4 * hf + 4)
                    prv = pr[:, :].rearrange("p (j l) -> p j l", j=4)
                    P.op("dve", "tensor_tensor", dte[:, js], prv[:, :, edge], cum[:, js], ALU.subtract)
                P.op("act", "activation", dte[:, :], dte[:, :], AF.Exp)
                P.op("dve", "tensor_tensor", xdw[:, :].rearrange("p (j q) -> p j q", j=8), xdf[:, :].rearrange("p (j q) -> p j q", j=8),
                     dte[:, :].unsqueeze(2).to_broadcast([128, 8, 64]), ALU.mult)
                for j in range(8):
                    ys = p_y[:, j * 64:(j + 1) * 64]
                    P.op("pe", "matmul", ys, MT[:, j, :], xd[:, j * 64:(j + 1) * 64], start=True, stop=False)
                    P.op("pe", "matmul", ys, CdT[:, j, :], STb[:, j, :], start=False, stop=True)
                P.op("dve", "tensor_tensor", yacc[:, c, :], yacc[:, c, :], p_y[:, :], ALU.add)
                P.op("pe", "matmul", p_loc[:, :], Btm[:, c, :], xdw[:, :], start=True, stop=True)
                P.op("pool", "tensor_tensor", STt[:, :, :], ST[:, :, :], Ec[:, :, edge:edge + 1].to_broadcast([128, 8, 64]), ALU.mult)
                P.op("dve", "tensor_tensor", ST[:, :, :], STt[:, :, :], p_loc[:, :].rearrange("p (j q) -> p j q", j=8), ALU.add)
                P.op("act", "activation", STb[:, :, :], ST[:, :, :], AF.Identity)
        g = P.sb([128, 512], F32, "g")
        g2 = P.sb([128, 512], F32, "g2")
        gb = P.sb([128, 512], BF16, "gb")
        ssq = P.sb([128, 1], F32, "ssq")
        stg = [P.sb([128, 4, 512], BF16, f"stg{i}") for i in range(2)]
        ogroups = [(0, 2)] + [(2 + 4 * i, 4) for i in range(4)]
        for gi, (ga, gn) in enumerate(ogroups):
            sg = stg[gi % 2]
            for t in range(ga, ga + gn):
                P.op("dve", "tensor_tensor", g[:, :], yacc[:, t, :], zs[:, t, :], ALU.mult)
                P.op("pool", "tensor_tensor", g2[:, :], g[:, :], g[:, :], ALU.mult)
                P.op("dve", "reduce_sum", ssq[:, :], g2[:, :], AX.X)
                P.op("act", "activation", ssq[:, :], ssq[:, :], AF.Sqrt, bias=cx.eps(), scale=1.0 / 512)
                P.op("dve", "reciprocal", ssq[:, :], ssq[:, :])
                P.op("dve", "scalar_tensor_tensor", gb[:, :], g[:, :], ssq[:, 0:1], rows[:, 40:552], ALU.mult, ALU.mult)
                pt = cx.tbank()
                for cc in range(4):
                    P.op("pe", "transpose", pt[:, cc * 128:(cc + 1) * 128], gb[:, cc * 128:(cc + 1) * 128], cx.ident[:, :])
                P.op("act", "activation", sg[:, :, (t - ga) * 128:(t - ga + 1) * 128], pt[:, 0:512].rearrange("p (c q) -> p c q", c=4), AF.Identity)
            for cc in range(4):
                P.dma(outT[cc, :, ga * 128:(ga + gn) * 128], sg[:, cc, :gn * 128], is_out=True)
    cx.psf = _psf_saved
    cx.wcols = 128


def build_ssd():
    nc = bass.Bass("TRN2", target_bir_lowering=False)
    P = Prog(nc)
    T = decl_ssd(P, "")
    cx = Cx(P, wcols=128)
    modT = load_modT(cx, T['modT_d'])
    body_ssd(cx, T['xin'], modT, T, T['outT'])
    P.emit()
    return nc


MT_TOK = 1152
MTG = [(0, 128), (128, 512), (640, 512)]
ALPHA = float((2 * 2) ** 0.25)


def decl_merge(P, pfx, io=True):
    xin = (P.dram(pfx + "xin", [MT_TOK, D], F32, kind="ExternalInput") if io else None)
    modT_d = (P.dram(pfx + "modT", [128, 64], F32, kind="ExternalInput") if io else None)
    gT_d = P.dram(pfx + "gTin", [32, 128, MT_TOK], BF16, kind="ExternalInput")
    wm_d = P.dram(pfx + "wm", [D, 8192], F32, kind="ExternalInput")
    wb_d = P.dram(pfx + "wbr", [4, 1024, D], F32, kind="ExternalInput")
    wo_d = P.dram(pfx + "wo", [D, D], F32, kind="ExternalInput")
    rows_d = P.dram(pfx + "rows", [4, 128, D], F32, kind="ExternalInput")
    out_d = (P.dram(pfx + "xout", [MT_TOK, D], F32, kind="ExternalOutput") if io else None)
    return dict(xin=xin, modT_d=modT_d, gT_d=gT_d, wm_d=wm_d, wb_d=wb_d, wo_d=wo_d, rows_d=rows_d, out_d=out_d)


def body_merge(cx, xin, modT, T, out_d):
    P = cx.P
    gT_d = T["gT_d"]
    wm_d = T["wm_d"]
    wb_d = T["wb_d"]
    wo_d = T["wo_d"]
    rows_d = T["rows_d"]
    _psf_saved = list(cx.psf)
    mT = P.sb([128, 16, MT_TOK], BF16, "mergedT")
    with P.scope():
        hT = emit_h(cx, xin, modT, 0, 9, ctx_tiles=1)
        gT = P.sb([128, 32, MT_TOK], BF16, "gT")
        for i in range(32):
            P.dma(gT[:, i, :], gT_d[i], q=("sp" if i % 2 == 0 else "pool"))
        cx.alloc_w()
        bf = [P.sb([128, 8, 128], F32, f"bf{i}") for i in range(2)]
        bb = [P.sb([128, 8, 128], BF16, f"bb{i}") for i in range(2)]
        sig = P.sb([128, 512], F32, "sig")
        prod = P.sb([128, 512], F32, "prod")
        acc = [P.sb([128, 512], F32, f"acc{i}") for i in range(3)]
        it = 0
        for dc in range(16):
            for kb in range(4):
                wf, wb = cx.wf[it % 2], cx.wb[it % 2]
                b_f, b_b = bf[it % 2], bb[it % 2]
                it += 1
                c0 = kb * 2048 + dc * 128
                P.dma(wf[:, :, :], wm_d[:, c0:c0 + 128].rearrange("(k p) n -> p k n", p=128), q="sp")
                P.dma(b_f[:, :, :], wb_d[kb, :, dc * 128:(dc + 1) * 128].rearrange("(k p) n -> p k n", p=128), q="pool")
                P.op("pool", "tensor_copy", wb[:, :, :], wf[:, :, :])
                P.op("pool", "tensor_copy", b_b[:, :, :], b_f[:, :, :])
                for gi, (t0, tl) in enumerate(MTG):
                    pl, pb_ = cx.bank(), cx.bank()
                    for kk in range(16):
                        P.op("pe", "matmul", pl[:, :tl], wb[:, kk, :], hT[:, kk, t0:t0 + tl], start=(kk == 0), stop=(kk == 15))
                    for kk in range(8):
                        P.op("pe", "matmul", pb_[:, :tl], b_b[:, kk, :], gT[:, kb * 8 + kk, t0:t0 + tl], start=(kk == 0), stop=(kk == 7))
                    P.op("act", "activation", sig[:, :tl], pl[:, :tl], AF.Sigmoid)
                    if kb == 0:
                        P.op("dve", "tensor_tensor", acc[gi][:, :tl], pb_[:, :tl], sig[:, :tl], ALU.mult)
                    else:
                        P.op("dve", "tensor_tensor", prod[:, :tl], pb_[:, :tl], sig[:, :tl], ALU.mult)
                        if kb < 3:
                            P.op("pool", "tensor_tensor", acc[gi][:, :tl], acc[gi][:, :tl], prod[:, :tl], ALU.add)
                        else:
                            P.op("pool", "tensor_tensor", mT[:, dc, t0:t0 + tl], acc[gi][:, :tl], prod[:, :tl], ALU.add)
    with P.scope():
        wo = P.sb([128, 16, D], BF16, "wo")
        cx.alloc_w()
        for cg in range(16):
            wf = cx.wf[cg % 2]
            P.dma(wf[:, :, :], wo_d[:, cg * 128:(cg + 1) * 128].rearrange("(k p) n -> p k n", p=128), q=("sp" if cg % 2 == 0 else "pool"))
            P.op("pool", "tensor_copy", wo[:, :, cg * 128:(cg + 1) * 128], wf[:, :, :])
        rows = [P.sb([128, D], F32, f"row{i}") for i in range(4)]
        for i in range(4):
            P.dma(rows[i][:, :], rows_d[i], q=("sp" if i % 2 == 0 else "pool"))
        xts = [P.sb([128, D], F32, f"mx{i}") for i in range(2)]
        vs = [P.sb([128, D], F32, f"mv{i}") for i in range(2)]
        st = P.sb([128, 4, 6], F32, "mst")
        mv = P.sb([128, 2], F32, "mmv")
        rs = P.sb([128, 1], F32, "mrs")
        for t in range(9):
            xt, v = xts[t % 2], vs[t % 2]
            P.dma(xt[:, :], xin[t * 128:(t + 1) * 128, :], q="sp")
            grow = rows[1] if t == 0 else rows[0]
            for cg in range(4):
                ps = cx.bank()
                for dc in range(16):
                    P.op("pe", "matmul", ps[:, :], mT[:, dc, t * 128:(t + 1) * 128], wo[:, dc, cg * 512:(cg + 1) * 512], start=(dc == 0), stop=(dc == 15))
                P.op("dve", "tensor_tensor", v[:, cg * 512:(cg + 1) * 512], ps[:, :], grow[:, cg * 512:(cg + 1) * 512], ALU.mult)
            P.op("dve", "scalar_tensor_tensor", v[:, :], xt[:, :], ALPHA, v[:, :], ALU.mult, ALU.add)
            for c in range(4):
                P.op("dve", "bn_stats", st[:, c, :], v[:, c * 512:(c + 1) * 512])
            P.op("dve", "bn_aggr", mv[:, :], st[:, :, :])
            P.op("act", "activation", rs[:, :], mv[:, 1:2], AF.Sqrt, bias=cx.eps(), scale=1.0)
            P.op("dve", "reciprocal", rs[:, :], rs[:, :])
            P.op("dve", "tensor_scalar", v[:, :], v[:, :], mv[:, 0:1], rs[:, 0:1], ALU.subtract, ALU.mult)
            P.op("pool", "tensor_tensor", v[:, :], v[:, :], rows[2][:, :], ALU.mult)
            P.op("dve", "tensor_tensor", v[:, :], v[:, :], rows[3][:, :], ALU.add)
            P.dma(out_d[t * 128:(t + 1) * 128, :], v[:, :], q="sp", is_out=True)
    cx.psf = _psf_saved


def build_merge():
    nc = bass.Bass("TRN2", target_bir_lowering=False)
    P = Prog(nc)
    T = decl_merge(P, "")
    cx = Cx(P, wcols=128)
    modT = load_modT(cx, T['modT_d'])
    body_merge(cx, T['xin'], modT, T, T['out_d'])
    P.emit()
    return nc


import ml_dtypes

OFF = dict(s_z=0, s_xbc=1024, s_dt=2560, m_cq=2592, m_ckv=3104, m_kpe=3360, m_z=3424, g_q=4448, g_k=5472,
           g_v=5600, g_z=5728, hy_xv=6752, hy_z=9824, merge=10848)
_CACHE = {}


def _bf(a):
    return np.ascontiguousarray(a.astype(np.float32)).astype(ml_dtypes.bfloat16)


def _fm(v, nchunk):
    return np.ascontiguousarray(np.asarray(v, np.float32).reshape(nchunk, 128).T)


def hy_consts(h):
    key = ("hy", h)
    if key in _CACHE:
        return _CACHE[key]
    out = {}
    deltas = np.abs(np.linspace(np.log(0.01) / 0.3, np.log(0.01) / 1.5, 1024, dtype=np.float32)).astype(np.float64)[512 * h:512 * h + 512]
    for tag, n, TW in (("L", 2048, 256), ("C", 256, 256)):
        N = 2 * n
        nch = n // 128
        t = np.arange(n, dtype=np.float64)[:, None]
        f = np.arange(n, dtype=np.float64)[None, :]
        ang = np.pi * (2 * f + 1) * t / N
        Cm, Sm = np.cos(ang), -np.sin(ang)

        def fw(M):
            return _bf(M.reshape(nch, 128, nch, 128).transpose(2, 1, 0, 3).reshape(nch, 128, nch * 128))

        def iv(M):
            return _bf(((2.0 / N) * M).reshape(n // TW, TW, nch, 128).transpose(0, 3, 2, 1).reshape(n // TW, 128, nch * TW))

        out["fwc" + tag], out["fws" + tag] = fw(Cm), fw(Sm)
        out["ivc" + tag], out["ivs" + tag] = iv(Cm), iv(Sm)
        tl = np.linspace(0.0, 1.0, n, dtype=np.float32).astype(np.float64)[:, None]
        w_ang = (2.0 * np.pi / n) * np.arange(n, dtype=np.float64)[:, None]
        bands = np.linspace(1e-4, 15, 16, dtype=np.float32).astype(np.float64)[None, :]
        feats = np.concatenate([tl, np.cos(bands * w_ang), -np.sin(bands * w_ang)], axis=-1)
        out["feats" + tag] = np.ascontiguousarray(feats.T.astype(np.float32))
        dec = np.exp(-tl * deltas[None, :])
        out["dec" + tag] = np.ascontiguousarray(dec.reshape(nch, 128, 512).transpose(1, 0, 2).astype(np.float32))
    _CACHE[key] = out
    return out


def run_mod(inp, l):
    cvec = np.concatenate([inp["c"], inp["c_ctx"][None, :]], axis=0)
    cT = np.ascontiguousarray(cvec.reshape(5, 16, 128).transpose(2, 1, 0))
    if "mod" not in _CACHE:
        _CACHE["mod"] = build_mod()
    maps = []
    for i in range(8):
        maps.append(dict(cT=cT, wa=np.ascontiguousarray(inp["w_ada"][l][:, i * 768:(i + 1) * 768]),
                         ba=_fm(inp["b_ada"][l][i * 768:(i + 1) * 768], 6)))
    res = run_bass_kernel_spmd(_CACHE["mod"], maps, core_ids=list(range(8)))
    return np.concatenate([r["modT"] for r in res.results], axis=1)


def modT_for(modall, b):
    return np.ascontiguousarray(np.concatenate([modall[:, 0:16, b], modall[:, 16:32, b], modall[:, 0:16, 4], modall[:, 16:32, 4]], axis=1))


def hy_maps(inp, l, xins, modall):
    maps = []
    W = inp["w_in"][l]
    for core in range(8):
        b, h = core // 2, core % 2
        c0 = 512 * h
        o = OFF["hy_xv"]
        w = np.concatenate([W[:, o + c0:o + c0 + 512], W[:, o + 1024 + c0:o + 1024 + c0 + 512],
                            W[:, o + 2048 + c0:o + 2048 + c0 + 512], W[:, OFF["hy_z"] + c0:OFF["hy_z"] + c0 + 512]], axis=1)
        cw = np.zeros((128, 12, 4), np.float32)
        for j in range(12):
            ch = (j // 4) * 1024 + c0 + (j % 4) * 128
            cw[:, j, 0:3] = inp["hy_conv_w"][l][:, ch:ch + 128].T
            cw[:, j, 3] = inp["hy_conv_b"][l][ch:ch + 128]
        w3 = np.concatenate([inp["hy_w3"][l][:, c0:c0 + 512], inp["hy_w3"][l][:, 1024 + c0:1024 + c0 + 512]], axis=1)
        pb = np.stack([inp["hy_b1"][l], inp["hy_freq"][l][0], inp["hy_b2"][l], inp["hy_freq"][l][1]], axis=1)
        m = dict(xin=xins[b], modT=(modT_for(modall, b) if modall is not None else None), w=np.ascontiguousarray(w), cw=cw,
                 dsk=_fm(inp["hy_d"][l][c0:c0 + 512], 4), w1=np.ascontiguousarray(inp["hy_w1"][l]),
                 w2=np.ascontiguousarray(inp["hy_w2"][l]), w3=np.ascontiguousarray(w3), pb=np.ascontiguousarray(pb.astype(np.float32)))
        m.update(hy_consts(h))
        maps.append(m)
    return maps


def rope_tables():
    if "rope" in _CACHE:
        return _CACHE["rope"]
    t = np.arange(NSEQ)
    inv = 10000.0 ** (-np.arange(16, dtype=np.float32) / 16)
    ang = np.concatenate([(t // 64)[:, None].astype(np.float32) * inv[None, :], (t % 64)[:, None].astype(np.float32) * inv[None, :]], axis=-1)
    cos, sin = np.cos(ang.astype(np.float32)).T, np.sin(ang.astype(np.float32)).T
    C = np.ascontiguousarray(np.tile(cos, (4, 1)).astype(np.float32))
    S = np.ascontiguousarray(np.tile(sin, (4, 1)).astype(np.float32))
    _CACHE["rope"] = (C, S)
    return C, S


def mla_maps(inp, l, xins, modall):
    maps = []
    W = inp["w_in"][l]
    C, S = rope_tables()
    for core in range(8):
        b, h = core // 2, core % 2
        kpe = W[:, OFF["m_kpe"]:OFF["m_kpe"] + 64]
        w = np.concatenate([W[:, OFF["m_cq"]:OFF["m_cq"] + 512], W[:, OFF["m_ckv"]:OFF["m_ckv"] + 256], kpe, kpe,
                            W[:, OFF["m_z"] + 512 * h:OFF["m_z"] + 512 * h + 512]], axis=1)
        uq = inp["mla_w_uq"][l].reshape(512, 8, 192)[:, 4 * h:4 * h + 4]
        wuq = np.concatenate([uq[:, :, :128].reshape(512, 512), uq[:, :, 128:].reshape(512, 256)], axis=1)
        ukv = inp["mla_w_ukv"][l].reshape(256, 8, 256)[:, 4 * h:4 * h + 4]
        wukv = np.concatenate([ukv[:, :, :128].reshape(256, 512), ukv[:, :, 128:].reshape(256, 512)], axis=1)
        nrm = np.concatenate([_fm(inp["mla_q_norm"][l], 4), _fm(inp["mla_kv_norm"][l], 2)], axis=1)
        maps.append(dict(xin=xins[b], modT=(modT_for(modall, b) if modall is not None else None), w=np.ascontiguousarray(w),
                         wuq=np.ascontiguousarray(wuq.reshape(4, 128, 768).transpose(1, 0, 2)),
                         wukv=np.ascontiguousarray(wukv.reshape(2, 128, 1024).transpose(1, 0, 2)),
                         nrm=np.ascontiguousarray(nrm), ropeC=C, ropeS=S))
    return maps


def gqa_maps(inp, l, xins, modall):
    maps = []
    W = inp["w_in"][l]
    C, S = rope_tables()
    kk = np.arange(128)[:, None]
    qq = np.arange(128)[None, :]
    msk = _bf(np.stack([(qq <= kk), (qq >= kk)], axis=1).astype(np.float32))
    for core in range(8):
        b, h = core // 2, core % 2
        kw = W[:, OFF["g_k"] + 64 * h:OFF["g_k"] + 64 * h + 64]
        vw = W[:, OFF["g_v"] + 64 * h:OFF["g_v"] + 64 * h + 64]
        w = np.concatenate([W[:, OFF["g_q"] + 512 * h:OFF["g_q"] + 512 * h + 512], kw, kw,
                            W[:, OFF["g_z"] + 512 * h:OFF["g_z"] + 512 * h + 512], vw, vw], axis=1)
        sk = inp["gqa_sink"][l][8 * h:8 * h + 8]
        sink = np.zeros((128, 4), np.float32)
        for j in range(4):
            sink[:64, j] = sk[2 * j]
            sink[64:, j] = sk[2 * j + 1]
        maps.append(dict(xin=xins[b], modT=(modT_for(modall, b) if modall is not None else None), w=np.ascontiguousarray(w), ropeC=C, ropeS=S, sink=sink, msk=msk))
    return maps


def ssd_maps(inp, l, xins, modall):
    maps = []
    W = inp["w_in"][l]
    kk = np.arange(128)[:, None]
    ll = np.arange(128)[None, :]
    msk = np.stack([(kk <= ll).astype(np.float32), (kk >= ll).astype(np.float32),
                    np.where(ll < kk, -30000.0, 0.0).astype(np.float32), np.where(ll > kk, -30000.0, 0.0).astype(np.float32)], axis=1)
    for core in range(8):
        b, h = core // 2, core % 2
        ox = OFF["s_xbc"]
        dtw = np.zeros((D, 128), np.float32)
        dtw[:, 0:8] = W[:, OFF["s_dt"] + 8 * h:OFF["s_dt"] + 8 * h + 8]
        dtw[:, 8:16] = W[:, OFF["s_dt"] + 16 + 8 * h:OFF["s_dt"] + 16 + 8 * h + 8]
        w = np.concatenate([W[:, ox + 512 * h:ox + 512 * h + 512], W[:, ox + 1024 + 128 * h:ox + 1024 + 128 * h + 128],
                            W[:, ox + 1280 + 128 * h:ox + 1280 + 128 * h + 128],
                            W[:, OFF["s_z"] + 512 * h:OFF["s_z"] + 512 * h + 512], dtw], axis=1)
        chs = [512 * h + 128 * j for j in range(4)] + [1024 + 128 * h, 1280 + 128 * h]
        cw = np.zeros((128, 6, 4), np.float32)
        for j, ch in enumerate(chs):
            cw[:, j, 0:3] = inp["ssd_conv_w"][l][:, ch:ch + 128].T
            cw[:, j, 3] = inp["ssd_conv_b"][l][ch:ch + 128]
        row = np.concatenate([inp["ssd_dt_bias"][l][0, 8 * h:8 * h + 8], inp["ssd_dt_bias"][l][1, 8 * h:8 * h + 8],
                              inp["ssd_a_log"][l][0, 8 * h:8 * h + 8], inp["ssd_a_log"][l][1, 8 * h:8 * h + 8],
                              inp["ssd_d"][l][8 * h:8 * h + 8], inp["ssd_norm_w"][l][512 * h:512 * h + 512]]).astype(np.float32)
        rows = np.ascontiguousarray(np.broadcast_to(row[None, :], (128, row.shape[0])))
        maps.append(dict(xin=xins[b], modT=(modT_for(modall, b) if modall is not None else None), w=np.ascontiguousarray(w), cw=cw, rows=rows, msk=msk))
    return maps


def merge_maps(inp, l, xcur, ctxcur, modall, gouts):
    maps = []
    wm = np.ascontiguousarray(inp["w_in"][l][:, OFF["merge"]:OFF["merge"] + 8192])
    wbr = np.ascontiguousarray(inp["w_branch"][l])
    wo = np.ascontiguousarray(inp["w_out"][l])
    for core in range(8):
        b, hh = core // 2, core % 2
        toks = np.concatenate([np.arange(128 * hh, 128 * hh + 128), NCTX + np.arange(1024 * hh, 1024 * hh + 1024)])
        xin = np.ascontiguousarray(np.concatenate([ctxcur[b][128 * hh:128 * hh + 128], xcur[b][1024 * hh:1024 * hh + 1024]], axis=0))
        g = np.empty((4, 8, 128, MT_TOK), dtype=ml_dtypes.bfloat16)
        for kb in range(4):
            for h in range(2):
                g[kb, 4 * h:4 * h + 4] = np.asarray(gouts[kb][2 * b + h])[:, :, toks]
        def row(v):
            return np.broadcast_to(np.asarray(v, np.float32).reshape(1, D), (128, D))
        gate_lat = modall[:, 32:48, b].T.reshape(-1)
        gate_ctx = modall[:, 32:48, 4].T.reshape(-1)
        rows = np.ascontiguousarray(np.stack([row(gate_lat), row(gate_ctx), row(inp["ln_g"][l]), row(inp["ln_b"][l])], axis=0))
        maps.append(dict(xin=xin, modT=(modT_for(modall, b) if modall is not None else None), gTin=np.ascontiguousarray(g.reshape(32, 128, MT_TOK)), wm=wm, wbr=wbr, wo=wo, rows=rows))
    return maps


def _prog(name):
    if name not in _CACHE:
        _CACHE[name] = globals()["build_" + name]()
    return _CACHE[name]


def kernel_unfused(**inp):
    inp = {k_: np.asarray(v) for k_, v in inp.items()}
    xcur = [np.ascontiguousarray(inp["x"][b]) for b in range(4)]
    ctxcur = [np.ascontiguousarray(inp["ctx"][b]) for b in range(4)]
    cores = list(range(8))
    for l in range(2):
        modall = run_mod(inp, l)
        xins = [np.ascontiguousarray(np.concatenate([ctxcur[b], xcur[b]], axis=0)) for b in range(4)]
        gouts = []
        for name in ("ssd", "mla", "gqa", "hy"):
            maps = globals()[name + "_maps"](inp, l, xins, modall)
            res = run_bass_kernel_spmd(_prog(name), maps, core_ids=cores)
            gouts.append([r["gT"] for r in res.results])
        maps = merge_maps(inp, l, xcur, ctxcur, modall, gouts)
        res = run_bass_kernel_spmd(_prog("merge"), maps, core_ids=cores)
        for b in range(4):
            o0, o1 = res.results[2 * b]["xout"], res.results[2 * b + 1]["xout"]
            ctxcur[b] = np.ascontiguousarray(np.concatenate([o0[:128], o1[:128]], axis=0))
            xcur[b] = np.ascontiguousarray(np.concatenate([o0[128:], o1[128:]], axis=0))
    return np.stack(xcur, axis=0).astype(np.float32)


class Slot:
    def __init__(self, buf, base):
        self.buf = buf
        self.base = base

    def __getitem__(self, idx):
        return self.buf[(self.base + idx[0],) + tuple(idx[1:])]


def body_mod(cx, T, gate_d):
    P = cx.P
    modT = P.sb([128, 64], F32, "modTl")
    with P.scope():
        cs = P.sb([128, 16, 2], F32, "mod_cs")
        P.dma(cs[:, :, :], T["cT_d"].v)
        P.op("act", "activation", cs[:, :, :], cs[:, :, :], AF.Silu)
        baT = P.sb([128, 48], F32, "mod_ba")
        P.dma(baT[:, :], T["baT_d"].v)
        ws = [P.sb([128, 16, 512], F32, f"mod_ws{i}") for i in range(3)]
        raw = P.sb([128, 32, 2], F32, "mod_raw")
        psm = cx.bank()
        wi = 0
        for g in range(8):
            w_ = ws[wi % 3]
            P.dma(w_[:, :, :], T["wa_d"][:, g * 512:(g + 1) * 512].rearrange("(k p) n -> p k n", p=128), q=("sp" if wi % 2 == 0 else "pool"))
            wi += 1
            for j in range(4):
                ch = g * 4 + j
                for kk in range(16):
                    P.op("pe", "matmul", psm[:, ch * 2:ch * 2 + 2], w_[:, kk, j * 128:(j + 1) * 128], cs[:, kk, :], start=(kk == 0), stop=(kk == 15))
        P.op("dve", "tensor_tensor", raw[:, :, :], psm[:, 0:64].rearrange("p (c j) -> p c j", j=2),
             baT[:, 0:32].unsqueeze(2).to_broadcast([128, 32, 2]), ALU.add)
        P.op("dve", "tensor_copy", modT[:, 0:16], raw[:, 0:16, 0])
        P.op("dve", "tensor_copy", modT[:, 32:48], raw[:, 0:16, 1])
        P.op("dve", "tensor_scalar", modT[:, 16:32], raw[:, 16:32, 0], 1.0, None, ALU.add)
        P.op("dve", "tensor_scalar", modT[:, 48:64], raw[:, 16:32, 1], 1.0, None, ALU.add)
        grow = P.sb([2, D], F32, "mod_grow")
        bg = P.sb([2, D], F32, "mod_bg")
        sel = P.sb([2, 2, 128], F32, "mod_sel")
        P.dma(bg[:, :], T["bgrow_d"].v)
        P.dma(sel[:, :, :], T["sel_d"].v)
        for g in range(4):
            w_ = ws[wi % 3]
            P.dma(w_[:, :, :], T["wa_d"][:, 4096 + g * 512:4096 + (g + 1) * 512].rearrange("(k p) n -> p k n", p=128), q=("sp" if wi % 2 == 0 else "pool"))
            wi += 1
            ps = cx.bank()
            for kk in range(16):
                P.op("pe", "matmul", ps[:2, :], cs[:, kk, :], w_[:, kk, :], start=(kk == 0), stop=(kk == 15))
            P.op("dve", "tensor_tensor", grow[:, g * 512:(g + 1) * 512], ps[:2, :], bg[:, g * 512:(g + 1) * 512], ALU.add)
        gsb = [P.sb([128, D], F32, f"mod_gsb{i}") for i in range(2)]
        for j in range(2):
            for g in range(4):
                ps = cx.bank()
                P.op("pe", "matmul", ps[:, :], sel[:, j, :], grow[:, g * 512:(g + 1) * 512], start=True, stop=True)
                P.op("act", "activation", gsb[j][:, g * 512:(g + 1) * 512], ps[:, :], AF.Identity)
            P.dma(gate_d[j], gsb[j][:, :])
    return modT


def body_merge2(cx, xin, modT, T, gT_all, gate_d, xout, t_lo, t_hi, is_out):
    P = cx.P
    wm_d, wb_d, wo_d, ln_d = T["wm_d"], T["wb_d"], T["wo_d"], T["lnrows_d"]
    ntk = (t_hi - t_lo) * 128
    tb0 = t_lo * 128
    groups = tgroups(t_lo, t_hi)
    mT = P.sb([128, 16, ntk], BF16, "mergedT")
    with P.scope():
        hT = emit_h(cx, xin, modT, t_lo, t_hi)
        gTs = [P.sb([128, 8, ntk], BF16, f"gT{i}") for i in range(4)]
        for i in range(32):
            P.dma(gTs[i // 8][:, i % 8, :], gT_all[i, :, tb0:tb0 + ntk], q=("sp" if i % 2 == 0 else "pool"))
        cx.alloc_w()
        bf = [P.sb([128, 8, 128], F32, f"bf{i}") for i in range(3)]
        bb = [P.sb([128, 8, 128], BF16, f"bb{i}") for i in range(2)]
        sig = P.sb([128, 512], F32, "sig")
        prod = P.sb([128, 512], F32, "prod")
        acc = [P.sb([128, 512], F32, f"acc{i}") for i in range(len(groups))]
        slabs = [(dc, kb) for dc in range(16) for kb in range(4)]

        def issue(si):
            dc_, kb_ = slabs[si]
            c0_ = kb_ * 2048 + dc_ * 128
            P.dma(cx.wf[si % 3][:, :, :], wm_d[:, c0_:c0_ + 128].rearrange("(k p) n -> p k n", p=128), q="sp")
            P.dma(bf[si % 3][:, :, :], wb_d[kb_, :, dc_ * 128:(dc_ + 1) * 128].rearrange("(k p) n -> p k n", p=128), q="sp")

        issue(0)
        issue(1)
        for it, (dc, kb) in enumerate(slabs):
            if True:
                if it + 2 < len(slabs):
                    issue(it + 2)
                wf, wb = cx.wf[it % 3], cx.wb[it % 2]
                b_f, b_b = bf[it % 3], bb[it % 2]
                P.op("dve", "tensor_copy", wb[:, :, :], wf[:, :, :])
                P.op("act", "activation", b_b[:, :, :], b_f[:, :, :], AF.Identity)
                for gi, (t0, tl) in enumerate(groups):
                    r0 = t0 - tb0
                    pl, pb_ = cx.bank(), cx.bank()
                    for kk in range(16):
                        P.op("pe", "matmul", pl[:, :tl], wb[:, kk, :], hT[:, kk, r0:r0 + tl], start=(kk == 0), stop=(kk == 15))
                    for kk in range(8):
                        P.op("pe", "matmul", pb_[:, :tl], b_b[:, kk, :], gTs[kb][:, kk, r0:r0 + tl], start=(kk == 0), stop=(kk == 7))
                    P.op("act", "activation", sig[:, :tl], pl[:, :tl], AF.Sigmoid)
                    if kb == 0:
                        P.op("dve", "tensor_tensor", acc[gi][:, :tl], pb_[:, :tl], sig[:, :tl], ALU.mult)
                    else:
                        P.op("dve", "tensor_tensor", prod[:, :tl], pb_[:, :tl], sig[:, :tl], ALU.mult)
                        if kb < 3:
                            P.op("pool", "tensor_tensor", acc[gi][:, :tl], acc[gi][:, :tl], prod[:, :tl], ALU.add)
                        else:
                            P.op("pool", "tensor_tensor", mT[:, dc, r0:r0 + tl], acc[gi][:, :tl], prod[:, :tl], ALU.add)
    with P.scope():
        wos = [P.sb([128, 16, 512], BF16, f"wo{i}") for i in range(4)]
        cx.alloc_w()
        rows = [P.sb([128, D], F32, f"row{i}") for i in range(4)]
        for cg in range(16):
            wf = cx.wf[cg % 3]
            P.dma(wf[:, :, :], wo_d[:, cg * 128:(cg + 1) * 128].rearrange("(k p) n -> p k n", p=128), q="sp")
            dst = wos[cg // 4][:, :, (cg % 4) * 128:(cg % 4 + 1) * 128]
            if cg % 2 == 0:
                P.op("dve", "tensor_copy", dst, wf[:, :, :])
            else:
                P.op("act", "activation", dst, wf[:, :, :], AF.Identity)
        for i, src in enumerate((gate_d[0], gate_d[1], ln_d[0], ln_d[1])):
            P.dma(rows[i][:, :], src, q=("sp" if i % 2 == 0 else "pool"))
        xts = [P.sb([128, D], F32, f"mx{i}") for i in range(2)]
        vs = [P.sb([128, D], F32, f"mv{i}") for i in range(2)]
        st = P.sb([128, 4, 6], F32, "mst")
        mv = P.sb([128, 2], F32, "mmv")
        rs = P.sb([128, 1], F32, "mrs")
        for t in range(t_lo, t_hi):
            xt, v = xts[t % 2], vs[t % 2]
            r0 = (t - t_lo) * 128
            P.dma(xt[:, :], xin[t * 128:(t + 1) * 128, :], q="sp")
            grow = rows[1] if t < 2 else rows[0]
            for cg in range(4):
                ps = cx.bank()
                for dc in range(16):
                    P.op("pe", "matmul", ps[:, :], mT[:, dc, r0:r0 + 128], wos[cg][:, dc, :], start=(dc == 0), stop=(dc == 15))
                P.op("dve", "tensor_tensor", v[:, cg * 512:(cg + 1) * 512], ps[:, :], grow[:, cg * 512:(cg + 1) * 512], ALU.mult)
            P.op("dve", "scalar_tensor_tensor", v[:, :], xt[:, :], ALPHA, v[:, :], ALU.mult, ALU.add)
            for c in range(4):
                P.op("dve", "bn_stats", st[:, c, :], v[:, c * 512:(c + 1) * 512])
            P.op("dve", "bn_aggr", mv[:, :], st[:, :, :])
            P.op("act", "activation", rs[:, :], mv[:, 1:2], AF.Sqrt, bias=cx.eps(), scale=1.0)
            P.op("dve", "reciprocal", rs[:, :], rs[:, :])
            P.op("dve", "tensor_scalar", v[:, :], v[:, :], mv[:, 0:1], rs[:, 0:1], ALU.subtract, ALU.mult)
            P.op("pool", "tensor_tensor", v[:, :], v[:, :], rows[2][:, :], ALU.mult)
            P.op("dve", "tensor_tensor", v[:, :], v[:, :], rows[3][:, :], ALU.add)
            P.dma(xout[t * 128:(t + 1) * 128, :], v[:, :], q="sp", is_out=is_out)


MIXERS = ("ssd", "mla", "gqa", "hy")


def build_mega(nlayers=2):
    nc = bass.Bass("TRN2", target_bir_lowering=False)
    P = Prog(nc)
    xin0 = P.dram("xin", [NTOK, D], F32, kind="ExternalInput")
    xmid = P.dram("xmid", [NTOK, D], F32)
    xfin = P.dram("xout", [NTOK, D], F32, kind="ExternalOutput")
    cT_d = P.dram("cT", [128, 16, 2], F32, kind="ExternalInput")
    sel_d = P.dram("sel", [2, 2, 128], F32, kind="ExternalInput")
    cx = Cx(P, wcols=128)
    for l in range(nlayers):
        Tm = dict(cT_d=cT_d, sel_d=sel_d,
                  wa_d=P.dram(f"L{l}_wa", [D, 6144], F32, kind="ExternalInput"),
                  baT_d=P.dram(f"L{l}_baT", [128, 48], F32, kind="ExternalInput"),
                  bgrow_d=P.dram(f"L{l}_bgrow", [2, D], F32, kind="ExternalInput"))
        Tg = dict(wm_d=P.dram(f"L{l}_wm", [D, 8192], F32, kind="ExternalInput"),
                  wb_d=P.dram(f"L{l}_wbr", [4, 1024, D], F32, kind="ExternalInput"),
                  wo_d=P.dram(f"L{l}_wo", [D, D], F32, kind="ExternalInput"),
                  lnrows_d=P.dram(f"L{l}_lnrows", [2, 128, D], F32, kind="ExternalInput"))
        gate_d = P.dram(f"L{l}_gate", [2, 128, D], F32)
        gT_all = P.dram(f"L{l}_gT", [32, 128, NTOK], BF16)
        xin = xin0 if l == 0 else xmid
        xout = xmid if l < nlayers - 1 else xfin
        hT_d = P.dram(f"L{l}_hT", [128, 16, NTOK], BF16)
        with P.scope():
            modT = body_mod(cx, Tm, gate_d)
            cx.hT_d = None
            for (ta, tb) in ((0, 9), (9, 18)):
                with P.scope():
                    hT = emit_h(cx, xin, modT, ta, tb)
                    for k0 in range(0, 16, 4):
                        P.dma(hT_d[:, k0:k0 + 4, ta * 128:tb * 128], hT[:, k0:k0 + 4, :], q=("sp" if (k0 // 4) % 2 == 0 else "pool"))
            cx.hT_d = hT_d
            for ki, name in enumerate(MIXERS):
                for h in range(2):
                    T = globals()["decl_" + name](P, f"L{l}_{name}{h}_", io=False)
                    with P.scope():
                        globals()["body_" + name](cx, xin, modT, T, Slot(gT_all, ki * 8 + h * 4))
            for (ta, tb) in ((0, 9), (9, 18)):
                with P.scope():
                    body_merge2(cx, xin, modT, Tg, gT_all, gate_d, xout, ta, tb, is_out=(l == nlayers - 1))
        cx.hT_d = None
    P.emit()
    return nc


def mega_maps(inp):
    maps = [dict() for _ in range(8)]
    cvec = [np.stack([inp["c"][b], inp["c_ctx"]], axis=0) for b in range(4)]
    sel = np.zeros((2, 2, 128), np.float32)
    sel[0, 0, :] = 1.0
    sel[1, 1, :] = 1.0
    for core in range(8):
        b = core // 2
        m = maps[core]
        m["xin"] = np.ascontiguousarray(np.concatenate([inp["ctx"][b], inp["x"][b]], axis=0))
        m["cT"] = np.ascontiguousarray(cvec[b].reshape(2, 16, 128).transpose(2, 1, 0))
        m["sel"] = sel
    for l in range(2):
        per = {name: globals()[name + "_maps"](inp, l, [None] * 4, None) for name in MIXERS}
        wa = np.ascontiguousarray(inp["w_ada"][l])
        baT = _fm(inp["b_ada"][l], 48)
        bgrow = np.ascontiguousarray(np.broadcast_to(inp["b_ada"][l][4096:6144][None, :], (2, D)).astype(np.float32))
        wm = np.ascontiguousarray(inp["w_in"][l][:, OFF["merge"]:OFF["merge"] + 8192])
        wbr = np.ascontiguousarray(inp["w_branch"][l])
        wo = np.ascontiguousarray(inp["w_out"][l])
        lnrows = np.ascontiguousarray(np.stack([np.broadcast_to(inp["ln_g"][l][None, :], (128, D)),
                                                np.broadcast_to(inp["ln_b"][l][None, :], (128, D))], axis=0).astype(np.float32))
        for core in range(8):
            b = core // 2
            m = maps[core]
            m[f"L{l}_wa"], m[f"L{l}_baT"], m[f"L{l}_bgrow"] = wa, baT, bgrow
            m[f"L{l}_wm"], m[f"L{l}_wbr"], m[f"L{l}_wo"], m[f"L{l}_lnrows"] = wm, wbr, wo, lnrows
            for name in MIXERS:
                for h in range(2):
                    src = per[name][2 * b + h]
                    for key, val in src.items():
                        if key in ("xin", "modT"):
                            continue
                        m[f"L{l}_{name}{h}_{key}"] = val
    return maps


def kernel(**inp):
    inp = {k_: np.asarray(v) for k_, v in inp.items()}
    if "mega" not in _CACHE:
        _CACHE["mega"] = build_mega()
    maps = mega_maps(inp)
    res = run_bass_kernel_spmd(_CACHE["mega"], maps, core_ids=list(range(8)))
    out = np.stack([res.results[2 * b]["xout"][NCTX:] for b in range(4)], axis=0)
    return np.ascontiguousarray(out.astype(np.float32))
```
